# Optimizing a Trainium2 kernel written in Bass

```python
import math
import jax, jax.numpy as jnp
from jax import lax
import numpy as np

D_MODEL = 2048
BATCH = 8
SEQ = 4096
DEPTH = 2

MIX_WIDTH = D_MODEL
ATTN_WIDTH = MIX_WIDTH // 2
SSM_WIDTH = MIX_WIDTH - ATTN_WIDTH
HEAD_DIM = 64
N_Q_HEADS = ATTN_WIDTH // HEAD_DIM
KV_RATIO = 8
N_KV_HEADS = N_Q_HEADS // KV_RATIO
KV_DIM = N_KV_HEADS * HEAD_DIM
WINDOW = 128
SSM_GROUP = 16
N_SSM_GROUPS = SSM_WIDTH // SSM_GROUP
STATE = 64
IN_COLS = ATTN_WIDTH + 2 * KV_DIM + SSM_WIDTH
D_FF = ((8 * D_MODEL // 3 + 255) // 256) * 256
CONV_WIDTH = 3
EPS = 1e-6
NEG = -1e30

kernel_name = "hybrid_swa_s5_convffn_sandwich_adaln"


def rmsnorm(x, g):
    x32 = x.astype(jnp.float32)
    y = x32 * lax.rsqrt(jnp.mean(x32 * x32, axis=-1, keepdims=True) + EPS)
    return y.astype(x.dtype) * g


def sliding_window_attention(q, k, v, sinks):
    bsz, seq = q.shape[0], q.shape[1]
    nb = seq // WINDOW
    grp = N_Q_HEADS // N_KV_HEADS
    qb = q.reshape(bsz, nb, WINDOW, N_KV_HEADS, grp, HEAD_DIM).astype(jnp.float32)

    def band(t):
        tb = t.reshape(bsz, nb, WINDOW, N_KV_HEADS, HEAD_DIM)
        prev = jnp.pad(tb, ((0, 0), (1, 0), (0, 0), (0, 0), (0, 0)))[:, :-1]
        return jnp.concatenate([prev, tb], axis=2)

    kb = band(k).astype(jnp.float32)
    vb = band(v)
    s = jnp.einsum('bnqhgd,bnkhd->bnhgqk', qb, kb) * (HEAD_DIM ** -0.5)
    qi = jnp.arange(WINDOW)[:, None]
    kj = jnp.arange(2 * WINDOW)[None, :]
    in_band = (kj > qi) & (kj <= qi + WINDOW)
    blk = jnp.arange(nb)[:, None, None]
    valid = in_band[None] & ((blk * WINDOW + kj[None] - WINDOW) >= 0)
    s = jnp.where(valid[None, :, None, None], s, NEG)
    sink = sinks.astype(jnp.float32).reshape(1, 1, N_KV_HEADS, grp, 1, 1)
    m = jnp.maximum(jnp.max(s, axis=-1, keepdims=True), sink)
    e = jnp.exp(s - m)
    p = e / (jnp.sum(e, axis=-1, keepdims=True) + jnp.exp(sink - m))
    out = jnp.einsum('bnhgqk,bnkhd->bnqhgd', p.astype(v.dtype), vb)
    return out.reshape(bsz, seq, N_Q_HEADS * HEAD_DIM)


def s5_ssm(u, lam_re, lam_im, log_step, b_re, b_im, c_re, c_im, d_skip):
    bsz, seq = u.shape[0], u.shape[1]
    dtype = u.dtype
    u4 = u.reshape(bsz, seq, N_SSM_GROUPS, SSM_GROUP).astype(jnp.float32)
    lr = lam_re.astype(jnp.float32)
    li = lam_im.astype(jnp.float32)
    dt = jnp.exp(log_step.astype(jnp.float32))[:, None]
    mag = jnp.exp(lr * dt)
    ang = li * dt
    ab_re = mag * jnp.cos(ang)
    ab_im = mag * jnp.sin(ang)
    den = lr * lr + li * li
    f_re = ((ab_re - 1.0) * lr + ab_im * li) / den
    f_im = (ab_im * lr - (ab_re - 1.0) * li) / den
    br = b_re.astype(jnp.float32)
    bi = b_im.astype(jnp.float32)
    bb_re = f_re[..., None] * br - f_im[..., None] * bi
    bb_im = f_re[..., None] * bi + f_im[..., None] * br
    bu_re = jnp.einsum('bsgh,gph->bsgp', u4, bb_re)
    bu_im = jnp.einsum('bsgh,gph->bsgp', u4, bb_im)
    a_re = jnp.broadcast_to(ab_re[None, None], (1, seq, N_SSM_GROUPS, STATE))
    a_im = jnp.broadcast_to(ab_im[None, None], (1, seq, N_SSM_GROUPS, STATE))

    def combine(e1, e2):
        a1r, a1i, b1r, b1i = e1
        a2r, a2i, b2r, b2i = e2
        return (a2r * a1r - a2i * a1i,
                a2r * a1i + a2i * a1r,
                a2r * b1r - a2i * b1i + b2r,
                a2r * b1i + a2i * b1r + b2i)

    _, _, xr, xi = lax.associative_scan(combine, (a_re, a_im, bu_re, bu_im), axis=1)
    y = (jnp.einsum('bsgp,ghp->bsgh', xr, c_re.astype(jnp.float32))
         - jnp.einsum('bsgp,ghp->bsgh', xi, c_im.astype(jnp.float32))
         + d_skip.astype(jnp.float32)[None, None] * u4)
    return y.reshape(bsz, seq, SSM_WIDTH).astype(dtype)


def causal_depthwise_conv(h, w, b):
    seq = h.shape[1]
    hp = jnp.pad(h, ((0, 0), (CONV_WIDTH - 1, 0), (0, 0)))
    out = b
    for k in range(CONV_WIDTH):
        out = out + hp[:, k:k + seq] * w[k]
    return out


def setup_inputs(seed: int = 0) -> dict:
    key = jax.random.key(seed)
    ks = jax.random.split(key, 32)
    nrm = jax.random.normal
    G, P, H = N_SSM_GROUPS, STATE, SSM_GROUP
    return {
        "x": nrm(ks[0], (BATCH, SEQ, D_MODEL), jnp.float32),
        "c": nrm(ks[1], (BATCH, D_MODEL), jnp.float32),
        "w_ada": nrm(ks[2], (DEPTH, D_MODEL, 6 * D_MODEL), jnp.float32) * (0.5 * D_MODEL ** -0.5),
        "b_ada": nrm(ks[3], (DEPTH, 6 * D_MODEL), jnp.float32) * 0.02,
        "g_pre_mix": 1.0 + 0.1 * nrm(ks[4], (DEPTH, D_MODEL), jnp.float32),
        "g_post_mix": 1.0 + 0.1 * nrm(ks[5], (DEPTH, D_MODEL), jnp.float32),
        "w_in": nrm(ks[6], (DEPTH, D_MODEL, IN_COLS), jnp.float32) * D_MODEL ** -0.5,
        "attn_sinks": nrm(ks[7], (DEPTH, N_Q_HEADS), jnp.float32),
        "lam_re": -0.5 + 0.01 * nrm(ks[8], (DEPTH, G, P), jnp.float32),
        "lam_im": jnp.pi * jnp.arange(P, dtype=jnp.float32)[None, None, :]
                  + 0.01 * nrm(ks[9], (DEPTH, G, P), jnp.float32),
        "log_step": jax.random.uniform(ks[10], (DEPTH, G), jnp.float32,
                                       minval=math.log(1e-3), maxval=math.log(1e-1)),
        "ssm_b_re": nrm(ks[11], (DEPTH, G, P, H), jnp.float32) * (2 * H) ** -0.5,
        "ssm_b_im": nrm(ks[12], (DEPTH, G, P, H), jnp.float32) * (2 * H) ** -0.5,
        "ssm_c_re": nrm(ks[13], (DEPTH, G, H, P), jnp.float32) * 0.5,
        "ssm_c_im": nrm(ks[14], (DEPTH, G, H, P), jnp.float32) * 0.5,
        "ssm_d": nrm(ks[15], (DEPTH, G, H), jnp.float32),
        "w_glu": nrm(ks[16], (DEPTH, SSM_WIDTH, SSM_WIDTH), jnp.float32) * SSM_WIDTH ** -0.5,
        "g_attn_out": 1.0 + 0.1 * nrm(ks[17], (DEPTH, ATTN_WIDTH), jnp.float32),
        "g_ssm_out": 1.0 + 0.1 * nrm(ks[18], (DEPTH, SSM_WIDTH), jnp.float32),
        "w_out": nrm(ks[19], (DEPTH, MIX_WIDTH, D_MODEL), jnp.float32) * MIX_WIDTH ** -0.5,
        "g_pre_ffn": 1.0 + 0.1 * nrm(ks[20], (DEPTH, D_MODEL), jnp.float32),
        "g_post_ffn": 1.0 + 0.1 * nrm(ks[21], (DEPTH, D_MODEL), jnp.float32),
        "w_up": nrm(ks[22], (DEPTH, D_MODEL, 2 * D_FF), jnp.float32) * D_MODEL ** -0.5,
        "conv_w": nrm(ks[23], (DEPTH, CONV_WIDTH, 2 * D_FF), jnp.float32) * CONV_WIDTH ** -0.5,
        "conv_b": nrm(ks[24], (DEPTH, 2 * D_FF), jnp.float32) * 0.01,
        "w_down": nrm(ks[25], (DEPTH, D_FF, D_MODEL), jnp.float32) * D_FF ** -0.5,
    }


def reference(x, c, w_ada, b_ada, g_pre_mix, g_post_mix, w_in, attn_sinks, lam_re, lam_im,
              log_step, ssm_b_re, ssm_b_im, ssm_c_re, ssm_c_im, ssm_d, w_glu, g_attn_out,
              g_ssm_out, w_out, g_pre_ffn, g_post_ffn, w_up, conv_w, conv_b, w_down):
    bsz, seq = x.shape[0], x.shape[1]
    c_act = jax.nn.silu(c)
    for l in range(DEPTH):
        ada = c_act @ w_ada[l] + b_ada[l]
        sh_m, sc_m, gt_m, sh_f, sc_f, gt_f = [t[:, None, :] for t in jnp.split(ada, 6, axis=-1)]

        h = rmsnorm(x, g_pre_mix[l]) * (1.0 + sc_m) + sh_m
        proj = h @ w_in[l]
        q = proj[..., :ATTN_WIDTH].reshape(bsz, seq, N_Q_HEADS, HEAD_DIM)
        k = proj[..., ATTN_WIDTH:ATTN_WIDTH + KV_DIM].reshape(bsz, seq, N_KV_HEADS, HEAD_DIM)
        v = proj[..., ATTN_WIDTH + KV_DIM:ATTN_WIDTH + 2 * KV_DIM].reshape(bsz, seq, N_KV_HEADS, HEAD_DIM)
        u = proj[..., ATTN_WIDTH + 2 * KV_DIM:]

        attn = sliding_window_attention(q, k, v, attn_sinks[l])
        y = s5_ssm(u, lam_re[l], lam_im[l], log_step[l], ssm_b_re[l], ssm_b_im[l],
                   ssm_c_re[l], ssm_c_im[l], ssm_d[l])
        z = jax.nn.gelu(y, approximate=True)
        ssm = z * jax.nn.sigmoid(z @ w_glu[l])

        merged = jnp.concatenate([rmsnorm(attn, g_attn_out[l]), rmsnorm(ssm, g_ssm_out[l])], axis=-1)
        mix = merged @ w_out[l]
        x = x + (1.0 + gt_m) * rmsnorm(mix, g_post_mix[l])

        h = rmsnorm(x, g_pre_ffn[l]) * (1.0 + sc_f) + sh_f
        up = causal_depthwise_conv(h @ w_up[l], conv_w[l], conv_b[l])
        val, gate = up[..., :D_FF], up[..., D_FF:]
        ff = (jax.nn.gelu(gate, approximate=True) * val) @ w_down[l]
        x = x + (1.0 + gt_f) * rmsnorm(ff, g_post_ffn[l])
    return x
```

```python
import os
import numpy as np
import ml_dtypes
from contextlib import ExitStack
import concourse.bass as bass
import concourse.mybir as mybir
from concourse.bass_utils import run_bass_kernel_spmd

F32 = mybir.dt.float32
BF16 = mybir.dt.bfloat16
AF = mybir.ActivationFunctionType
ALU = mybir.AluOpType

D = 2048
SEQ = 4096
NB = 8
DEPTH = 2
DFF = 5632
T = 512
NCH = D // 128
NFF = DFF // 128
EPS = 1e-6
NCHUNK = T // 8

ENGS = ["pe", "act", "dve", "pool", "sp"]


class Buf:
    __slots__ = ("name", "last_w", "readers", "excl")

    def __init__(self, name, excl=False):
        self.name = name
        self.last_w = None
        self.readers = {}
        self.excl = excl


class Op:
    __slots__ = ("eng", "fn", "deps", "idx", "signal", "count", "is_dma", "dsem", "dval", "prev_dma")

    def __init__(self, eng, fn, is_dma):
        self.eng = eng
        self.fn = fn
        self.deps = []
        self.signal = False
        self.count = 0
        self.is_dma = is_dma
        self.dsem = None
        self.dval = 0
        self.prev_dma = None


class Prog:
    NDMA = 12

    def __init__(self):
        self.ops = {e: [] for e in ENGS}
        self.ndma = {e: 0 for e in ENGS}
        self.dma_hist = {e: [] for e in ENGS}

    def op(self, eng, fn, reads=(), writes=(), dma=False):
        o = Op(eng, fn, dma)
        o.idx = len(self.ops[eng])
        deps = {}

        def add(d):
            if d is o:
                return
            if d.is_dma:
                deps[("dma", id(d))] = d
            else:
                k = ("eng", d.eng)
                if k not in deps or deps[k].idx < d.idx:
                    deps[k] = d

        for b in reads:
            if b.last_w is not None:
                add(b.last_w)
            if b.excl:
                for r in b.readers.values():
                    if r.eng != eng:
                        add(r)
        for b in writes:
            if b.last_w is not None:
                add(b.last_w)
            for r in b.readers.values():
                add(r)
        for d in deps.values():
            if (not d.is_dma) and d.eng == eng and eng == "pe":
                continue
            o.deps.append(d)
            if not d.is_dma:
                d.signal = True
        for b in reads:
            if dma:
                b.readers[("dma", id(o))] = o
            else:
                b.readers[("eng", eng)] = o
        for b in writes:
            b.last_w = o
            b.readers = {}
        if dma:
            n = self.ndma[eng]
            self.ndma[eng] += 1
            o.dsem = (eng, n % self.NDMA)
            o.dval = 16 * (n // self.NDMA + 1)
            if n >= self.NDMA:
                o.prev_dma = self.dma_hist[eng][n - self.NDMA]
            self.dma_hist[eng].append(o)
        self.ops[eng].append(o)
        return o

    def emit(self, nc, es):
        engsem = {e: es.enter_context(nc.semaphore("s_" + e)) for e in ENGS}
        dsem = {}
        for e in ENGS:
            for i in range(min(self.NDMA, self.ndma[e])):
                dsem[(e, i)] = es.enter_context(nc.semaphore("d_%s_%d" % (e, i)))
        for e in ENGS:
            c = 0
            for o in self.ops[e]:
                if o.signal and not o.is_dma:
                    c += 1
                    o.count = c
        block = es.enter_context(nc.Block())
        prog = self

        def run(e, eh):
            waited = {}
            for o in prog.ops[e]:
                wl = []
                for d in o.deps:
                    if d.is_dma:
                        wl.append((("d",) + d.dsem, dsem[d.dsem], d.dval))
                    else:
                        wl.append((("e", d.eng), engsem[d.eng], d.count))
                if o.prev_dma is not None:
                    d = o.prev_dma
                    wl.append((("d",) + d.dsem, dsem[d.dsem], d.dval))
                for k, s, v in wl:
                    if waited.get(k, 0) >= v:
                        continue
                    waited[k] = v
                    eh.wait_ge(s, v)
                inst = o.fn(eh)
                if o.is_dma:
                    inst.then_inc(dsem[o.dsem], 16)
                elif o.signal:
                    inst.then_inc(engsem[e], 1)
            for (q, i), s in dsem.items():
                if q == e and prog.ndma[e] > 0:
                    last = [d for d in prog.dma_hist[e] if d.dsem == (q, i)][-1]
                    if waited.get(("d", q, i), 0) < last.dval:
                        eh.wait_ge(s, last.dval)

        @block.tensor
        def _(eh):
            run("pe", eh)

        @block.scalar
        def _(eh):
            run("act", eh)

        @block.vector
        def _(eh):
            run("dve", eh)

        @block.gpsimd
        def _(eh):
            run("pool", eh)

        @block.sync
        def _(eh):
            run("sp", eh)


def _tile_w(w, cols_list):
    K = w.shape[0]
    KC = K // 128
    wz = np.concatenate([w, np.zeros((K, 1), w.dtype)], axis=1)
    out = np.empty((len(cols_list), 128, KC * len(cols_list[0])), np.float32)
    for m, cols in enumerate(cols_list):
        sub = wz[:, cols]
        C = sub.shape[1]
        out[m] = sub.reshape(KC, 128, C).transpose(1, 0, 2).reshape(128, KC * C)
    return out


def _fm(v, nch):
    return np.ascontiguousarray(v.reshape(nch, 128).T)


IN_TILES = 20


def _in_cols():
    cols = []
    for qt in range(8):
        cols.append(list(range(qt * 128, qt * 128 + 128)))
    k0 = list(range(1024, 1088))
    k1 = list(range(1088, 1152))
    z = [-1] * 64
    cols += [k0 + z, z + k0, k1 + z, z + k1]
    for ct in range(8):
        cols.append(list(range(1280 + ct * 128, 1280 + ct * 128 + 128)))
    return cols


def _v_cols():
    v0 = list(range(1152, 1216))
    v1 = list(range(1216, 1280))
    z = [-1] * 64
    return [v0 + z + z + v0 + v1 + z + z + v1]


SP_GPRE, SP_GPOST, SP_GPREF, SP_GPOSTF = 0, 16, 32, 48
SP_BADA = 64
SP_GATT, SP_GSSM = 160, 168
SP_CW = 176
SP_CB = 440
SP_SINK = 528
SP_L = 536
SP_C = 2 * SP_L
SP_ID = SP_C + 16
SP_HM = SP_ID + 128
SP_TOT = SP_HM + 2

SS_LR, SS_LI, SS_LS = 0, 32, 64
SS_BR, SS_BI, SS_CR, SS_CI = 96, 608, 1120, 1632
SS_D = 2144
SS_L = 2208

CB_ID = 0
CB_SEL = 128
CB_MP = CB_SEL + 8 * 352
CB_MC = CB_MP + 128
CB_ONE = CB_MC + 128
CB_OP0 = CB_ONE + 128
CB_OP1 = CB_OP0 + 128
CB_TOT = CB_OP1 + 128


def prep_shared(inp):
    sh = {}
    f32 = np.float32
    small = np.zeros((128, SP_TOT), f32)
    ssmp = np.zeros((128, 2 * SS_L), f32)
    for l in range(DEPTH):
        o = l * SP_L
        small[:, o + SP_GPRE:o + SP_GPRE + 16] = _fm(inp["g_pre_mix"][l], 16)
        small[:, o + SP_GPOST:o + SP_GPOST + 16] = _fm(inp["g_post_mix"][l], 16)
        small[:, o + SP_GPREF:o + SP_GPREF + 16] = _fm(inp["g_pre_ffn"][l], 16)
        small[:, o + SP_GPOSTF:o + SP_GPOSTF + 16] = _fm(inp["g_post_ffn"][l], 16)
        small[:, o + SP_BADA:o + SP_BADA + 96] = _fm(inp["b_ada"][l], 96)
        small[:, o + SP_GATT:o + SP_GATT + 8] = _fm(inp["g_attn_out"][l], 8)
        small[:, o + SP_GSSM:o + SP_GSSM + 8] = _fm(inp["g_ssm_out"][l], 8)
        cw = inp["conv_w"][l]
        small[:, o + SP_CW:o + SP_CW + 264] = cw.reshape(3, 88, 128).transpose(2, 1, 0).reshape(128, 264)
        small[:, o + SP_CB:o + SP_CB + 88] = _fm(inp["conv_b"][l], 88)
        sk = inp["attn_sinks"][l]
        small[:, o + SP_SINK:o + SP_SINK + 8] = sk.reshape(8, 2).T[np.arange(128) // 64]
        so = l * SS_L

        def pairlay(a):
            return a.reshape(32, 2, 64).transpose(1, 2, 0).reshape(128, 32)

        ssmp[:, so + SS_LR:so + SS_LR + 32] = pairlay(inp["lam_re"][l])
        ssmp[:, so + SS_LI:so + SS_LI + 32] = pairlay(inp["lam_im"][l])
        ssmp[:, so + SS_LS:so + SS_LS + 32] = pairlay(np.repeat(inp["log_step"][l][:, None], 64, axis=1))

        def pairlay3(a):
            return a.reshape(32, 2, 64, 16).transpose(1, 2, 0, 3).reshape(128, 512)

        ssmp[:, so + SS_BR:so + SS_BR + 512] = pairlay3(inp["ssm_b_re"][l])
        ssmp[:, so + SS_BI:so + SS_BI + 512] = pairlay3(inp["ssm_b_im"][l])
        ssmp[:, so + SS_CR:so + SS_CR + 512] = pairlay3(inp["ssm_c_re"][l].transpose(0, 2, 1))
        ssmp[:, so + SS_CI:so + SS_CI + 512] = pairlay3(inp["ssm_c_im"][l].transpose(0, 2, 1))
        dd = inp["ssm_d"][l]
        ssmp[:, so + SS_D:so + SS_D + 64] = np.tile(dd.T, (8, 1))
    small[:, SP_ID:SP_ID + 128] = np.eye(128, dtype=f32)
    small[:64, SP_HM] = 1.0
    small[64:, SP_HM + 1] = 1.0
    sh["small"] = small
    sh["ssmp"] = ssmp
    cb = np.zeros((128, CB_TOT), f32)
    cb[:, CB_ID:CB_ID + 128] = np.eye(128)
    k = np.arange(128)
    for g in range(8):
        rows = k[(k // 16) == g]
        cb[rows, CB_SEL + g * 352 + 112 + rows] = 1.0
    kk = k[:, None]
    qq = k[None, :]
    NEG = -30000.0
    cb[:, CB_MP:CB_MP + 128] = np.where(kk > qq, 0.0, NEG)
    cb[:, CB_MC:CB_MC + 128] = np.where(kk <= qq, 0.0, NEG)
    cb[:, CB_ONE:CB_ONE + 128] = 1.0
    cb[:, CB_OP0:CB_OP0 + 64] = 1.0
    cb[:, CB_OP1 + 64:CB_OP1 + 128] = 1.0
    sh["cbf"] = cb.astype(ml_dtypes.bfloat16)
    full = [list(range(m * 128, m * 128 + 128)) for m in range(96)]
    for l in range(DEPTH):
        sh["wada%d" % l] = _tile_w(inp["w_ada"][l], full)
        sh["win%d" % l] = _tile_w(inp["w_in"][l], _in_cols())
        sh["wv%d" % l] = _tile_w(inp["w_in"][l], _v_cols())
        sh["wglu%d" % l] = _tile_w(inp["w_glu"][l], full[:8])
        sh["wout%d" % l] = _tile_w(inp["w_out"][l], full[:16])
        sh["wup%d" % l] = _tile_w(inp["w_up"][l], full[:88])
        sh["wdn%d" % l] = _tile_w(inp["w_down"][l], full[:16])
    return sh


WSHAPES = {"win": (IN_TILES, 2048), "wv": (1, 8192), "wglu": (8, 1024), "wout": (16, 2048),
           "wup": (88, 2048), "wdn": (16, 5632)}


def build_nc(n_tiles=SEQ // T, n_layers=DEPTH, do_attn=True, do_ssm=True, do_ffn=True, dbg=False):
    nc = bass.Bass("TRN2", target_bir_lowering=False)
    P = Prog()
    es = ExitStack()

    def din(name, shape, dt=F32):
        return nc.dram_tensor(name, list(shape), dt, kind="ExternalInput").ap()

    x_in = din("x", [SEQ, D])
    small_d = din("small", [128, SP_TOT])
    ssmp_d = din("ssmp", [128, 2 * SS_L]) if do_ssm else None
    cbf_d = din("cbf", [128, CB_TOT], BF16)
    wada_d = [din("wada%d" % l, [96, 128, 2048]) for l in range(n_layers)]
    wsrc = {}
    wscr = {}
    for l in range(n_layers):
        for k, (M, C) in WSHAPES.items():
            wsrc[(k, l)] = din("%s%d" % (k, l), [M, 128, C])
            wscr[(k, l)] = nc.dram_tensor("s_%s%d" % (k, l), [M, 128, C], BF16, kind="Internal").ap()
        wscr[("ssm", l)] = nc.dram_tensor("s_ssm%d" % l, [8, 128, 5120], BF16, kind="Internal").ap()
    y_out = nc.dram_tensor("y", [SEQ, D], F32, kind="ExternalOutput").ap()

    def sb(name, shape, dt=F32):
        return es.enter_context(nc.sbuf_tensor("sb_" + name, list(shape), dt))

    small = sb("small", [128, SP_TOT])
    cbf = sb("cbf", [128, CB_TOT], BF16)
    ada = sb("ada", [128, DEPTH, 96])
    vec = sb("vec", [128, DEPTH, 4, 16])
    cact = sb("cact", [128, 16])
    epsb = sb("epsb", [128, 1])
    xT = sb("xT", [128, NCH, T])
    hT = sb("hT", [128, NCH, T], BF16)
    mixb = sb("mixb", [128, NCH, T], BF16)
    Rt = [sb("R%d" % i, [128, T]) for i in range(3)]
    NTMP = 6
    tmpf = [sb("tmpf%d" % i, [128, T]) for i in range(NTMP)]
    sqb = [sb("sqb%d" % i, [128, T], BF16) for i in range(3)]
    NSLOT = 6
    wslot = [sb("wslot%d" % i, [128, 2048], BF16) for i in range(NSLOT)]
    stage = sb("stage", [128, D])
    convc = [sb("convc%d" % l, [128, 88, 2]) for l in range(DEPTH)]
    hid = sb("hid", [128, NFF, T], BF16)
    bank = [es.enter_context(nc.psum_tensor("bank%d" % i, [128, 512], F32)) for i in range(8)]

    B_small = Buf("small"); B_cbf = Buf("cbf"); B_ada = Buf("ada"); B_vec = Buf("vec")
    B_cact = Buf("cact"); B_eps = Buf("eps"); B_xT = [Buf("xT%d" % c) for c in range(NCH)]
    B_hT = [Buf("hT%d" % c) for c in range(NCH)]; B_mix = [Buf("mix%d" % c) for c in range(NCH)]
    B_R = [Buf("R%d" % i) for i in range(3)]; B_tmp = [Buf("tmp%d" % i) for i in range(NTMP)]
    B_sq = [Buf("sq%d" % i) for i in range(3)]; B_ws = [Buf("ws%d" % i) for i in range(NSLOT)]
    B_stage = Buf("stage"); B_convc = [Buf("convc%d" % l) for l in range(DEPTH)]
    B_bank = [Buf("bank%d" % i, excl=True) for i in range(8)]
    B_scr = {k: Buf("scr_%s%d" % k) for k in wscr}

    ident_f = small[:, SP_ID:SP_ID + 128]
    ident_b = cbf[:, CB_ID:CB_ID + 128]
    ones_b = cbf[:, CB_ONE:CB_ONE + 128]

    ctr = {"tmp": 0, "sq": 0, "ws": 0, "bank": 0, "tbank": 0}

    def nxt(k, n):
        v = ctr[k]
        ctr[k] = (v + 1) % n
        return v

    P.op("sp", lambda e: e.dma_start(out=small[:], in_=small_d[:, :]), writes=[B_small], dma=True)
    P.op("sp", lambda e: e.dma_start(out=cbf[:], in_=cbf_d[:, :]), writes=[B_cbf], dma=True)
    P.op("dve", lambda e: e.memset(epsb[:], EPS), writes=[B_eps])
    P.op("act", lambda e: e.activation(out=cact[:], in_=small[:, SP_C:SP_C + 16], func=AF.Silu),
         reads=[B_small], writes=[B_cact])

    adast = [xT[:, 4 * i:4 * i + 4, :].rearrange("p a b -> p (a b)") for i in range(3)]
    B_adast = [B_xT[4 * i:4 * i + 4] for i in range(3)]
    cactb = sb("cactb", [128, 16], BF16)
    P.op("dve", lambda e: e.tensor_copy(out=cactb[:], in_=cact[:]), reads=[B_cact], writes=[B_cact])
    adab = [hT[:, 4 * i:4 * i + 4, :].rearrange("p a b -> p (a b)") for i in range(3)]
    B_adab = [B_hT[4 * i:4 * i + 4] for i in range(3)]
    for l in range(n_layers):
        for m in range(96):
            s = m % 3
            P.op("sp", lambda e, s=s, l=l, m=m: e.dma_start(out=adast[s], in_=wada_d[l][m]),
                 writes=B_adast[s], dma=True)
            if m % 2 == 0:
                P.op("act", lambda e, s=s: e.activation(out=adab[s], in_=adast[s], func=AF.Copy),
                     reads=B_adast[s], writes=B_adab[s])
            else:
                P.op("dve", lambda e, s=s: e.tensor_copy(out=adab[s], in_=adast[s]),
                     reads=B_adast[s], writes=B_adab[s])
            bk = 5 + (m % 2)
            for kc in range(16):
                P.op("pe", lambda e, s=s, kc=kc, bk=bk: e.matmul(
                    bank[bk][:, 0:1], lhsT=adab[s][:, kc * 128:(kc + 1) * 128], rhs=cactb[:, kc:kc + 1],
                    start=(kc == 0), stop=(kc == 15)),
                    reads=B_adab[s] + [B_cact], writes=[B_bank[bk]])
            P.op("dve", lambda e, l=l, m=m, bk=bk: e.tensor_tensor(
                out=ada[:, l, m:m + 1], in0=bank[bk][:, 0:1],
                in1=small[:, l * SP_L + SP_BADA + m:l * SP_L + SP_BADA + m + 1], op=ALU.add),
                reads=[B_bank[bk], B_small], writes=[B_ada])
        o = l * SP_L
        for j, (a0, g0) in enumerate([(16, SP_GPRE), (32, SP_GPOST), (64, SP_GPREF), (80, SP_GPOSTF)]):
            P.op("dve", lambda e, l=l, j=j, a0=a0, g0=g0, o=o: e.scalar_tensor_tensor(
                out=vec[:, l, j, :], in0=ada[:, l, a0:a0 + 16], scalar=1.0,
                in1=small[:, o + g0:o + g0 + 16], op0=ALU.add, op1=ALU.mult),
                reads=[B_ada, B_small], writes=[B_vec])

    def gs1(l, c): return vec[:, l, 0, c:c + 1]
    def gg1(l, c): return vec[:, l, 1, c:c + 1]
    def gs2(l, c): return vec[:, l, 2, c:c + 1]
    def gg2(l, c): return vec[:, l, 3, c:c + 1]
    def sh_m(l, c): return ada[:, l, 0 + c:c + 1]
    def sh_f(l, c): return ada[:, l, 48 + c:48 + c + 1]

    pre_f = adast
    pre_b = [hT[:, 4 * i:4 * i + 4, :].rearrange("p a b -> p (a b)") for i in range(3)]
    B_pref = B_adast
    B_preb = [B_hT[4 * i:4 * i + 4] for i in range(3)]
    pc = 0
    wlist = ["win", "wv", "wglu", "wout", "wup", "wdn"]
    if not do_ffn:
        wlist = ["win", "wv", "wglu", "wout"]
    for l in range(n_layers):
        for k in wlist:
            M, C = WSHAPES[k]
            for m in range(M):
                for c0 in range(0, C, 2048):
                    cw = min(2048, C - c0)
                    s = pc % 3
                    P.op("sp", lambda e, s=s, k=k, l=l, m=m, c0=c0, cw=cw: e.dma_start(
                        out=pre_f[s][:, :cw], in_=wsrc[(k, l)][m][:, c0:c0 + cw]),
                        writes=B_pref[s], dma=True)
                    ce = "act" if pc % 2 == 0 else "dve"
                    if ce == "act":
                        P.op("act", lambda e, s=s, cw=cw: e.activation(out=pre_b[s][:, :cw], in_=pre_f[s][:, :cw],
                                                                       func=AF.Copy),
                             reads=B_pref[s], writes=B_preb[s])
                    else:
                        P.op("dve", lambda e, s=s, cw=cw: e.tensor_copy(out=pre_b[s][:, :cw], in_=pre_f[s][:, :cw]),
                             reads=B_pref[s], writes=B_preb[s])
                    P.op("pool", lambda e, s=s, k=k, l=l, m=m, c0=c0, cw=cw: e.dma_start(
                        out=wscr[(k, l)][m][:, c0:c0 + cw], in_=pre_b[s][:, :cw]),
                        reads=B_preb[s], writes=[B_scr[(k, l)]], dma=True)
                    pc += 1


    import math
    hidb = hid[:, :, :].rearrange("p a b -> p (a b)")
    hf = hid[:, :, :].bitcast(F32).rearrange("p a b -> p (a b)")
    hTb = hT[:, :, :].rearrange("p a b -> p (a b)")
    A8 = [sb("A8_%d" % l, [128, 3, 32]) for l in range(DEPTH)]
    XSc = [sb("XSc%d" % l, [128, 2, 32]) for l in range(DEPTH)]
    B_A8 = [Buf("A8_%d" % l) for l in range(DEPTH)]
    B_XSc = [Buf("XSc%d" % l) for l in range(DEPTH)]
    B_prm = Buf("g_prm"); B_sc = Buf("g_sc"); B_BB = Buf("g_B"); B_bcf = Buf("g_bcf"); B_ccf = Buf("g_ccf")
    B_gt = Buf("g_t"); B_bctb = Buf("g_bctb"); B_cm0 = Buf("g_cm0")
    GEN_BUFS = [B_prm, B_sc, B_BB, B_bcf, B_ccf, B_gt, B_bctb, B_cm0]
    if do_ssm:
        prm = hf[:, 0:SS_L]
        def SC(i): return hf[:, 2208 + i * 32:2208 + (i + 1) * 32]
        I_DT, I_E1, I_MAG, I_ANG, I_KF, I_TMP, I_SARG, I_CARG, I_SIN, I_COS, I_AR, I_AI, I_DEN, I_RDEN, I_ARM1, I_FRE, I_FIM, I_T1, I_T2 = range(19)
        def PR(k): return SC(19 + k)
        def PI(k): return SC(28 + k)
        Bre = hf[:, 3392:3904]; Bim = hf[:, 3904:4416]
        BcF = hf[:, 4416:6336]
        CcF = hf[:, 6336:7360]
        G1 = hf[:, 7360:7424]; G2 = hf[:, 7424:7488]
        BcTb = hidb[:, 17100:19020]
        Cm0Z = hidb[:, 19020:21068]
        CcS = hTb[:, 0:2048]; KtS = hTb[:, 2048:3072]; BcS = hTb[:, 4096:6144]
        B_ccs = B_hT[0:4]; B_kts = B_hT[4:6]; B_bcs = B_hT[8:12]
        TWO_PI = 2.0 * math.pi

        def dv(fn, reads, writes):
            P.op("dve", fn, reads=reads, writes=writes)

        def tt(out, a, b, op, reads=(B_sc,), writes=(B_sc,)):
            dv(lambda e: e.tensor_tensor(out=out, in0=a, in1=b, op=op), list(reads), list(writes))

        def range_reduce(src_i, dst_i, shift):
            dv(lambda e: e.tensor_scalar(out=SC(dst_i), in0=SC(src_i), scalar1=shift, scalar2=None, op0=ALU.add),
               [B_sc], [B_sc])
            dv(lambda e: e.tensor_copy(out=SC(I_T1), in_=SC(dst_i)), [B_sc], [B_sc])
            for j in range(1, 12):
                thr = (2 * j - 1) * math.pi
                dv(lambda e, thr=thr: e.tensor_scalar(out=SC(I_TMP), in0=SC(I_T1), scalar1=thr, scalar2=-TWO_PI,
                                                      op0=ALU.is_gt, op1=ALU.mult), [B_sc], [B_sc])
                tt(SC(dst_i), SC(dst_i), SC(I_TMP), ALU.add)
            dv(lambda e: e.tensor_scalar(out=SC(dst_i), in0=SC(dst_i), scalar1=3.1415925, scalar2=-3.1415925,
                                         op0=ALU.min, op1=ALU.max), [B_sc], [B_sc])

        for l in range(n_layers):
            so = l * SS_L
            P.op("sp", lambda e, so=so: e.dma_start(out=prm, in_=ssmp_d[:, so:so + SS_L]), writes=[B_prm], dma=True)
            LR = prm[:, SS_LR:SS_LR + 32]; LI = prm[:, SS_LI:SS_LI + 32]; LS = prm[:, SS_LS:SS_LS + 32]
            P.op("act", lambda e, LS=LS: e.activation(out=SC(I_DT), in_=LS, func=AF.Exp), reads=[B_prm], writes=[B_sc])
            tt(SC(I_E1), LR, SC(I_DT), ALU.mult, reads=(B_prm, B_sc))
            P.op("act", lambda e: e.activation(out=SC(I_MAG), in_=SC(I_E1), func=AF.Exp), reads=[B_sc], writes=[B_sc])
            tt(SC(I_ANG), LI, SC(I_DT), ALU.mult, reads=(B_prm, B_sc))
            range_reduce(I_ANG, I_SARG, 0.0)
            range_reduce(I_ANG, I_CARG, 0.5 * math.pi)
            P.op("act", lambda e: e.activation(out=SC(I_SIN), in_=SC(I_SARG), func=AF.Sin), reads=[B_sc], writes=[B_sc])
            P.op("act", lambda e: e.activation(out=SC(I_COS), in_=SC(I_CARG), func=AF.Sin), reads=[B_sc], writes=[B_sc])
            tt(SC(I_AR), SC(I_MAG), SC(I_COS), ALU.mult)
            tt(SC(I_AI), SC(I_MAG), SC(I_SIN), ALU.mult)
            tt(SC(I_DEN), LR, LR, ALU.mult, reads=(B_prm, B_sc))
            tt(SC(I_T1), LI, LI, ALU.mult, reads=(B_prm, B_sc))
            tt(SC(I_DEN), SC(I_DEN), SC(I_T1), ALU.add)
            dv(lambda e: e.reciprocal(out=SC(I_RDEN), in_=SC(I_DEN)), [B_sc], [B_sc])
            dv(lambda e: e.tensor_scalar(out=SC(I_ARM1), in0=SC(I_AR), scalar1=-1.0, scalar2=None, op0=ALU.add),
               [B_sc], [B_sc])
            tt(SC(I_T1), SC(I_ARM1), LR, ALU.mult, reads=(B_prm, B_sc))
            tt(SC(I_T2), SC(I_AI), LI, ALU.mult, reads=(B_prm, B_sc))
            tt(SC(I_T1), SC(I_T1), SC(I_T2), ALU.add)
            tt(SC(I_FRE), SC(I_T1), SC(I_RDEN), ALU.mult)
            tt(SC(I_T1), SC(I_AI), LR, ALU.mult, reads=(B_prm, B_sc))
            tt(SC(I_T2), SC(I_ARM1), LI, ALU.mult, reads=(B_prm, B_sc))
            tt(SC(I_T1), SC(I_T1), SC(I_T2), ALU.subtract)
            tt(SC(I_FIM), SC(I_T1), SC(I_RDEN), ALU.mult)
            dv(lambda e: e.memset(PR(0), 1.0), [], [B_sc])
            dv(lambda e: e.memset(PI(0), 0.0), [], [B_sc])
            for k in range(1, 9):
                tt(SC(I_T1), PR(k - 1), SC(I_AR), ALU.mult)
                tt(SC(I_T2), PI(k - 1), SC(I_AI), ALU.mult)
                tt(PR(k), SC(I_T1), SC(I_T2), ALU.subtract)
                tt(SC(I_T1), PR(k - 1), SC(I_AI), ALU.mult)
                tt(SC(I_T2), PI(k - 1), SC(I_AR), ALU.mult)
                tt(PI(k), SC(I_T1), SC(I_T2), ALU.add)
            dv(lambda e, l=l: e.tensor_copy(out=A8[l][:, 0, :], in_=PR(8)), [B_sc], [B_A8[l]])
            dv(lambda e, l=l: e.tensor_copy(out=A8[l][:, 1, :], in_=PI(8)), [B_sc], [B_A8[l]])
            dv(lambda e, l=l: e.tensor_scalar(out=A8[l][:, 2, :], in0=PI(8), scalar1=-1.0, scalar2=None, op0=ALU.mult),
               [B_sc], [B_A8[l]])
            dv(lambda e, l=l: e.memset(XSc[l][:], 0.0), [], [B_XSc[l]])
            def b3(ap): return ap.rearrange("p (a b) -> p a b", b=16)
            def bc(i, p0=0, n=32): return SC(i)[:, p0:p0 + n].unsqueeze(2).to_broadcast([128, n, 16])
            BR = b3(prm[:, SS_BR:SS_BR + 512]); BI = b3(prm[:, SS_BI:SS_BI + 512])
            CR = b3(prm[:, SS_CR:SS_CR + 512]); CI = b3(prm[:, SS_CI:SS_CI + 512])
            G512a = hf[:, 7488:8000]; G512b = hf[:, 8000:8512]
            tt(b3(G512a), bc(I_FRE), BR, ALU.mult, reads=(B_prm, B_sc), writes=(B_gt,))
            tt(b3(G512b), bc(I_FIM), BI, ALU.mult, reads=(B_prm, B_sc), writes=(B_gt,))
            tt(b3(Bre), b3(G512a), b3(G512b), ALU.subtract, reads=(B_gt,), writes=(B_BB,))
            tt(b3(G512a), bc(I_FRE), BI, ALU.mult, reads=(B_prm, B_sc), writes=(B_gt,))
            tt(b3(G512b), bc(I_FIM), BR, ALU.mult, reads=(B_prm, B_sc), writes=(B_gt,))
            tt(b3(Bim), b3(G512a), b3(G512b), ALU.add, reads=(B_gt,), writes=(B_BB,))
            cm4 = Cm0Z.rearrange("p (a r g h) -> p a r g h", r=2, g=2, h=16)
            for g2 in range(2):
                hm = small[:, SP_HM + g2:SP_HM + g2 + 1]
                dv(lambda e, g2=g2, hm=hm, CR=CR: e.tensor_scalar(out=cm4[:, :, 0, g2, :], in0=CR, scalar1=hm, scalar2=None,
                                                                   op0=ALU.mult), [B_prm, B_small], [B_cm0])
                dv(lambda e, g2=g2, hm=hm, CI=CI: e.tensor_scalar(out=cm4[:, :, 1, g2, :], in0=CI, scalar1=hm, scalar2=-1.0,
                                                                   op0=ALU.mult, op1=ALU.mult), [B_prm, B_small], [B_cm0])
            bcf5 = BcF.rearrange("p (a r j h) -> p a r j h", r=2, j=15, h=16)
            ccf5 = CcF.rearrange("p (a r t h) -> p a r t h", r=2, t=8, h=16)
            g1 = G1.rearrange("p (a h) -> p a h", h=16); g2t = G2.rearrange("p (a h) -> p a h", h=16)
            bctb4 = BcTb.rearrange("p (a r x) -> p a r x", r=2, x=240)
            bcs4 = BcS.rearrange("p (g r x) -> p g r x", r=2, x=128)
            kts3 = KtS.rearrange("p (g x) -> p g x", x=128)
            for ct in range(8):
                p0 = ct * 4
                bre4 = b3(Bre)[:, p0:p0 + 4, :]; bim4 = b3(Bim)[:, p0:p0 + 4, :]
                cre4 = CR[:, p0:p0 + 4, :]; cim4 = CI[:, p0:p0 + 4, :]
                dv(lambda e: e.memset(BcF, 0.0), [], [B_bcf])
                for j in range(8):
                    k = 7 - j
                    prk = PR(k)[:, p0:p0 + 4].unsqueeze(2).to_broadcast([128, 4, 16])
                    pik = PI(k)[:, p0:p0 + 4].unsqueeze(2).to_broadcast([128, 4, 16])
                    tt(g1, prk, bre4, ALU.mult, reads=(B_sc, B_BB), writes=(B_gt,))
                    tt(g2t, pik, bim4, ALU.mult, reads=(B_sc, B_BB), writes=(B_gt,))
                    tt(bcf5[:, :, 0, j, :], g1, g2t, ALU.subtract, reads=(B_gt,), writes=(B_bcf,))
                    tt(g1, prk, bim4, ALU.mult, reads=(B_sc, B_BB), writes=(B_gt,))
                    tt(g2t, pik, bre4, ALU.mult, reads=(B_sc, B_BB), writes=(B_gt,))
                    tt(bcf5[:, :, 1, j, :], g1, g2t, ALU.add, reads=(B_gt,), writes=(B_bcf,))
                dv(lambda e: e.tensor_copy(out=BcTb, in_=BcF), [B_bcf], [B_bctb])
                for t in range(8):
                    prk = PR(t + 1)[:, p0:p0 + 4].unsqueeze(2).to_broadcast([128, 4, 16])
                    pik = PI(t + 1)[:, p0:p0 + 4].unsqueeze(2).to_broadcast([128, 4, 16])
                    tt(g1, prk, cre4, ALU.mult, reads=(B_sc, B_prm), writes=(B_gt,))
                    tt(g2t, pik, cim4, ALU.mult, reads=(B_sc, B_prm), writes=(B_gt,))
                    tt(ccf5[:, :, 0, t, :], g1, g2t, ALU.subtract, reads=(B_gt,), writes=(B_ccf,))
                    tt(g1, pik, cre4, ALU.mult, reads=(B_sc, B_prm), writes=(B_gt,))
                    tt(g2t, prk, cim4, ALU.mult, reads=(B_sc, B_prm), writes=(B_gt,))
                    dv(lambda e, t=t: e.scalar_tensor_tensor(out=ccf5[:, :, 1, t, :], in0=g1, scalar=-1.0, in1=g2t,
                                                             op0=ALU.mult, op1=ALU.subtract), [B_gt], [B_ccf])
                ccs_v = CcS.rearrange("p (a g x) -> p a g x", g=2, x=256)
                for g2 in range(2):
                    hm = small[:, SP_HM + g2:SP_HM + g2 + 1]
                    dv(lambda e, g2=g2, hm=hm: e.tensor_scalar(out=ccs_v[:, :, g2, :], in0=CcF.rearrange("p (a x) -> p a x", x=256),
                                                               scalar1=hm, scalar2=None, op0=ALU.mult),
                       [B_ccf, B_small], B_ccs)
                dv(lambda e: e.memset(BcS, 0.0), [], B_bcs)
                for pl in range(4):
                    pair = p0 + pl
                    bk = nxt("bank", 4)
                    for t in range(8):
                        for ri in range(2):
                            P.op("pe", lambda e, pl=pl, t=t, ri=ri, bk=bk, pair=pair: e.matmul(
                                bank[bk][:, t * 32:(t + 1) * 32],
                                lhsT=bctb4[:, pl, ri, (7 - t) * 16:(7 - t) * 16 + 128],
                                rhs=Cm0Z[:, (pair * 2 + ri) * 32:(pair * 2 + ri + 1) * 32],
                                start=(ri == 0), stop=(ri == 1)),
                                reads=[B_bctb, B_cm0], writes=[B_bank[bk]])
                    kview = bank[bk][:, 0:256].rearrange("p (t g h) -> p g t h", g=2, h=16)
                    idv = ident_f.rearrange("p (t h) -> p t h", h=16)
                    for g2 in range(2):
                        g = pair * 2 + g2
                        dcol = prm[:, SS_D + g:SS_D + g + 1]
                        dv(lambda e, g2=g2, pl=pl, dcol=dcol, kview=kview, idv=idv: e.scalar_tensor_tensor(
                            out=kts3[:, pl * 2 + g2, :].rearrange("p (t h) -> p t h", h=16), in0=idv, scalar=dcol,
                            in1=kview[:, g2, :, :], op0=ALU.mult, op1=ALU.add),
                            [B_bank[bk], B_prm, B_small], B_kts)
                    for ri in range(2):
                        bk2 = nxt("bank", 4)
                        P.op("pe", lambda e, pl=pl, ri=ri, bk2=bk2: e.matmul(
                            bank[bk2][:, 0:128], lhsT=bctb4[:, pl, ri, 0:128], rhs=ident_b, start=True, stop=True),
                            reads=[B_bctb, B_cbf], writes=[B_bank[bk2]])
                        for g2 in range(2):
                            dv(lambda e, pl=pl, ri=ri, g2=g2, bk2=bk2: e.tensor_copy(
                                out=bcs4[:, pl * 2 + g2, ri, g2 * 64:(g2 + 1) * 64], in_=bank[bk2][:, g2 * 64:(g2 + 1) * 64]),
                                [B_bank[bk2]], B_bcs)
                P.op("pool", lambda e, l=l, ct=ct: e.dma_start(out=wscr[("ssm", l)][ct][:, 0:2048], in_=BcS),
                     reads=B_bcs, writes=[B_scr[("ssm", l)]], dma=True)
                P.op("pool", lambda e, l=l, ct=ct: e.dma_start(out=wscr[("ssm", l)][ct][:, 2048:4096], in_=CcS),
                     reads=B_ccs, writes=[B_scr[("ssm", l)]], dma=True)
                P.op("pool", lambda e, l=l, ct=ct: e.dma_start(out=wscr[("ssm", l)][ct][:, 4096:5120], in_=KtS),
                     reads=B_kts, writes=[B_scr[("ssm", l)]], dma=True)

    def wload(k, l, m, c0, cw):
        s = nxt("ws", NSLOT)
        P.op("sp", lambda e: e.dma_start(out=wslot[s][:, :cw], in_=wscr[(k, l)][m][:, c0:c0 + cw]),
             reads=[B_scr[(k, l)]], writes=[B_ws[s]], dma=True)
        return s

    def rms_begin():
        return {"n": 0}

    def rms_add(st, src_ap, src_bufs, total):
        q = nxt("sq", 3)
        P.op("act", lambda e: e.activation(out=sqb[q][:], in_=src_ap, func=AF.Square),
             reads=src_bufs, writes=[B_sq[q]])
        n = st["n"]
        P.op("pe", lambda e: e.matmul(bank[4][:, :], lhsT=ones_b, rhs=sqb[q][:], start=(n == 0), stop=(n == total - 1)),
             reads=[B_sq[q], B_cbf], writes=[B_bank[4]])
        st["n"] = n + 1

    def rms_finish(ri, F):
        t = nxt("tmp", NTMP)
        P.op("act", lambda e: e.activation(out=tmpf[t][:], in_=bank[4][:, :], func=AF.Sqrt, bias=epsb[:, 0:1],
                                           scale=1.0 / F),
             reads=[B_bank[4], B_eps], writes=[B_tmp[t]])
        P.op("dve", lambda e: e.reciprocal(out=Rt[ri][:], in_=tmpf[t][:]), reads=[B_tmp[t]], writes=[B_R[ri]])

    def pre_norm(l, gsf, shf):
        st = rms_begin()
        for c in range(NCH):
            rms_add(st, xT[:, c, :], [B_xT[c]], NCH)
        rms_finish(0, D)
        for c in range(NCH):
            t = nxt("tmp", NTMP)
            P.op("pool", lambda e, c=c, t=t: e.tensor_tensor(out=tmpf[t][:], in0=xT[:, c, :], in1=Rt[0][:], op=ALU.mult),
                 reads=[B_xT[c], B_R[0]], writes=[B_tmp[t]])
            P.op("act", lambda e, c=c, t=t: e.activation(out=hT[:, c, :], in_=tmpf[t][:], func=AF.Identity,
                                                         bias=shf(l, c), scale=gsf(l, c)),
                 reads=[B_tmp[t], B_vec, B_ada], writes=[B_hT[c]])

    def post_update(l, ggf):
        rms_finish(1, D)
        for c in range(NCH):
            t = nxt("tmp", NTMP)
            P.op("pool", lambda e, c=c, t=t: e.tensor_tensor(out=tmpf[t][:], in0=mixb[:, c, :], in1=Rt[1][:], op=ALU.mult),
                 reads=[B_mix[c], B_R[1]], writes=[B_tmp[t]])
            P.op("dve", lambda e, c=c, t=t: e.scalar_tensor_tensor(
                out=xT[:, c, :], in0=tmpf[t][:], scalar=ggf(l, c), in1=xT[:, c, :], op0=ALU.mult, op1=ALU.add),
                reads=[B_tmp[t], B_vec, B_xT[c]], writes=[B_xT[c]])

    def proj_to_mix(k, l, KC, rhs_fn, rhs_bufs):
        st = rms_begin()
        for m in range(NCH):
            bk = nxt("bank", 4)
            kc = 0
            for c0 in range(0, KC * 128, 2048):
                cw = min(2048, KC * 128 - c0)
                s = wload(k, l, m, c0, cw)
                for j in range(cw // 128):
                    P.op("pe", lambda e, s=s, j=j, kc=kc, bk=bk: e.matmul(
                        bank[bk][:, :], lhsT=wslot[s][:, j * 128:(j + 1) * 128], rhs=rhs_fn(kc),
                        start=(kc == 0), stop=(kc == KC - 1)),
                        reads=[B_ws[s]] + rhs_bufs(kc), writes=[B_bank[bk]])
                    kc += 1
            P.op("dve", lambda e, m=m, bk=bk: e.tensor_copy(out=mixb[:, m, :], in_=bank[bk][:, :]),
                 reads=[B_bank[bk]], writes=[B_mix[m]])
            rms_add(st, bank[bk][:, :], [B_bank[bk]], NCH)

    B_hid = [Buf("hid%d" % j) for j in range(NFF)]
    cbuf = tmpf
    B_cbuf = B_tmp

    def ffn(l, ti):
        o = l * SP_L
        pre_norm(l, gs2, sh_f)
        for j in range(NFF):
            cbs = []
            for half in range(2):
                m = j + half * NFF
                s = wload("wup", l, m, 0, 2048)
                bk = nxt("bank", 4)
                for kc in range(NCH):
                    P.op("pe", lambda e, s=s, kc=kc, bk=bk: e.matmul(
                        bank[bk][:, :], lhsT=wslot[s][:, kc * 128:(kc + 1) * 128], rhs=hT[:, kc, :],
                        start=(kc == 0), stop=(kc == NCH - 1)),
                        reads=[B_ws[s], B_hT[kc]], writes=[B_bank[bk]])
                cb = nxt("tmp", NTMP)
                cbs.append(cb)
                w0 = small[:, o + SP_CW + m * 3 + 0:o + SP_CW + m * 3 + 1]
                w1 = small[:, o + SP_CW + m * 3 + 1:o + SP_CW + m * 3 + 2]
                w2 = small[:, o + SP_CW + m * 3 + 2:o + SP_CW + m * 3 + 3]
                bb = small[:, o + SP_CB + m:o + SP_CB + m + 1]
                P.op("act", lambda e, cb=cb, bk=bk, w2=w2, bb=bb: e.activation(
                    out=cbuf[cb][:], in_=bank[bk][:, :], func=AF.Identity, bias=bb, scale=w2),
                    reads=[B_bank[bk], B_small], writes=[B_cbuf[cb]])
                P.op("dve", lambda e, cb=cb, bk=bk, w1=w1: e.scalar_tensor_tensor(
                    out=cbuf[cb][:, 1:T], in0=bank[bk][:, 0:T - 1], scalar=w1, in1=cbuf[cb][:, 1:T],
                    op0=ALU.mult, op1=ALU.add),
                    reads=[B_bank[bk], B_small, B_cbuf[cb]], writes=[B_cbuf[cb]])
                P.op("dve", lambda e, cb=cb, bk=bk, w0=w0: e.scalar_tensor_tensor(
                    out=cbuf[cb][:, 2:T], in0=bank[bk][:, 0:T - 2], scalar=w0, in1=cbuf[cb][:, 2:T],
                    op0=ALU.mult, op1=ALU.add),
                    reads=[B_bank[bk], B_small, B_cbuf[cb]], writes=[B_cbuf[cb]])
                if ti > 0:
                    P.op("dve", lambda e, cb=cb, m=m, w1=w1: e.scalar_tensor_tensor(
                        out=cbuf[cb][:, 0:1], in0=convc[l][:, m, 1:2], scalar=w1, in1=cbuf[cb][:, 0:1],
                        op0=ALU.mult, op1=ALU.add),
                        reads=[B_convc[l], B_small, B_cbuf[cb]], writes=[B_cbuf[cb]])
                    P.op("dve", lambda e, cb=cb, m=m, w0=w0: e.scalar_tensor_tensor(
                        out=cbuf[cb][:, 0:2], in0=convc[l][:, m, 0:2], scalar=w0, in1=cbuf[cb][:, 0:2],
                        op0=ALU.mult, op1=ALU.add),
                        reads=[B_convc[l], B_small, B_cbuf[cb]], writes=[B_cbuf[cb]])
                P.op("dve", lambda e, m=m, bk=bk: e.tensor_copy(out=convc[l][:, m, 0:2], in_=bank[bk][:, T - 2:T]),
                     reads=[B_bank[bk]], writes=[B_convc[l]])
            cv, cg = cbs
            P.op("act", lambda e, cg=cg: e.activation(out=cbuf[cg][:], in_=cbuf[cg][:], func=AF.Gelu_apprx_tanh),
                 reads=[B_cbuf[cg]], writes=[B_cbuf[cg]])
            P.op("dve", lambda e, j=j, cv=cv, cg=cg: e.tensor_tensor(out=hid[:, j, :], in0=cbuf[cg][:], in1=cbuf[cv][:],
                                                                     op=ALU.mult),
                 reads=[B_cbuf[cg], B_cbuf[cv]], writes=[B_hid[j]])
        proj_to_mix("wdn", l, NFF, lambda kc: hid[:, kc, :], lambda kc: [B_hid[kc]])
        post_update(l, gg2)

    sthi = sb("sthi", [128, D], BF16)
    stlo = sb("stlo", [128, D], BF16)
    B_sthi = Buf("sthi"); B_stlo = Buf("stlo")

    def load_x(ti):
        for blk in range(T // 128):
            r0 = ti * T + blk * 128
            P.op("pool", lambda e, r0=r0: e.dma_start(out=stage[:], in_=x_in[r0:r0 + 128, :]),
                 writes=[B_stage], dma=True)
            P.op("act", lambda e: e.activation(out=sthi[:], in_=stage[:], func=AF.Copy),
                 reads=[B_stage], writes=[B_sthi])
            P.op("dve", lambda e: e.tensor_tensor(out=stlo[:], in0=stage[:], in1=sthi[:], op=ALU.subtract),
                 reads=[B_stage, B_sthi], writes=[B_stlo])
            for c4 in range(NCH // 4):
                bk = nxt("bank", 4)
                for q in range(4):
                    c = c4 * 4 + q
                    P.op("pe", lambda e, c=c, q=q, bk=bk: e.matmul(
                        bank[bk][:, q * 128:(q + 1) * 128], lhsT=sthi[:, c * 128:(c + 1) * 128], rhs=ident_b,
                        start=True, stop=False),
                        reads=[B_sthi, B_cbf], writes=[B_bank[bk]])
                    P.op("pe", lambda e, c=c, q=q, bk=bk: e.matmul(
                        bank[bk][:, q * 128:(q + 1) * 128], lhsT=stlo[:, c * 128:(c + 1) * 128], rhs=ident_b,
                        start=False, stop=True),
                        reads=[B_stlo, B_cbf], writes=[B_bank[bk]])
                for q in range(4):
                    c = c4 * 4 + q
                    if q % 2:
                        P.op("act", lambda e, c=c, q=q, bk=bk, blk=blk: e.activation(
                            out=xT[:, c, blk * 128:(blk + 1) * 128], in_=bank[bk][:, q * 128:(q + 1) * 128], func=AF.Copy),
                            reads=[B_bank[bk]], writes=[B_xT[c]])
                    else:
                        P.op("dve", lambda e, c=c, q=q, bk=bk, blk=blk: e.tensor_copy(
                            out=xT[:, c, blk * 128:(blk + 1) * 128], in_=bank[bk][:, q * 128:(q + 1) * 128]),
                            reads=[B_bank[bk]], writes=[B_xT[c]])

    xhi = sb("xhi", [128, 4, 128], BF16)
    xlo = sb("xlo", [128, 4, 128], BF16)
    B_xhi = Buf("xhi"); B_xlo = Buf("xlo")

    def store_x(ti):
        for blk in range(T // 128):
            r0 = ti * T + blk * 128
            for c4 in range(NCH // 4):
                bk = nxt("bank", 4)
                src = xT[:, c4 * 4:(c4 + 1) * 4, blk * 128:(blk + 1) * 128]
                P.op("act", lambda e, src=src: e.activation(out=xhi[:], in_=src, func=AF.Copy),
                     reads=B_xT[c4 * 4:(c4 + 1) * 4], writes=[B_xhi])
                P.op("dve", lambda e, src=src: e.tensor_tensor(out=xlo[:], in0=src, in1=xhi[:], op=ALU.subtract),
                     reads=B_xT[c4 * 4:(c4 + 1) * 4] + [B_xhi], writes=[B_xlo])
                for q in range(4):
                    P.op("pe", lambda e, q=q, bk=bk: e.matmul(
                        bank[bk][:, q * 128:(q + 1) * 128], lhsT=xhi[:, q, :], rhs=ident_b, start=True, stop=False),
                        reads=[B_xhi, B_cbf], writes=[B_bank[bk]])
                    P.op("pe", lambda e, q=q, bk=bk: e.matmul(
                        bank[bk][:, q * 128:(q + 1) * 128], lhsT=xlo[:, q, :], rhs=ident_b, start=False, stop=True),
                        reads=[B_xlo, B_cbf], writes=[B_bank[bk]])
                P.op("dve", lambda e, c4=c4, bk=bk: e.tensor_copy(out=stage[:, c4 * 512:(c4 + 1) * 512], in_=bank[bk][:, :]),
                     reads=[B_bank[bk]], writes=[B_stage])
            P.op("pool", lambda e, r0=r0: e.dma_start(out=y_out[r0:r0 + 128, :], in_=stage[:]),
                 reads=[B_stage], dma=True)

    hflat = hid[:, :, :]
    qT = hid[:, 0:8, :]
    kTv = hid[:, 8:13, :].rearrange("p a b -> p (a b)")
    Vt = hid[:, 13:17, :]
    attnT = hid[:, 17:25, :]
    PT = hid[:, 25:33, :].rearrange("p a b -> p (a b)")
    uT = hid[:, 33:41, :]
    B_q = [Buf("q%d" % i) for i in range(8)]
    B_k = Buf("kT"); B_V = [Buf("V%d" % i) for i in range(4)]
    B_att = [Buf("att%d" % i) for i in range(8)]
    B_PT = [Buf("PT%d" % i) for i in range(8)]
    B_u = [Buf("u%d" % i) for i in range(8)]
    kcar = [sb("kcar%d" % l, [128, 4, 128], BF16) for l in range(DEPTH)]
    vcar = [sb("vcar%d" % l, [128, 512], BF16) for l in range(DEPTH)]
    B_kcar = [Buf("kcar%d" % l) for l in range(DEPTH)]
    B_vcar = [Buf("vcar%d" % l) for l in range(DEPTH)]
    esk = sb("esk", [128, DEPTH, 8])
    B_esk = Buf("esk")
    for l in range(n_layers):
        P.op("act", lambda e, l=l: e.activation(out=esk[:, l, :], in_=small[:, l * SP_L + SP_SINK:l * SP_L + SP_SINK + 8],
                                                func=AF.Exp), reads=[B_small], writes=[B_esk])
    maskP = cbf[:, CB_MP:CB_MP + 128]
    maskC = cbf[:, CB_MC:CB_MC + 128]
    onesP = [cbf[:, CB_OP0:CB_OP0 + 128], cbf[:, CB_OP1:CB_OP1 + 128]]
    ctr["obank"] = 0
    ctr["ev"] = 0

    def evac_bf16(dst_ap, bk, dst_bufs):
        if nxt("ev", 2) == 0:
            P.op("act", lambda e: e.activation(out=dst_ap, in_=bank[bk][:, :], func=AF.Copy),
                 reads=[B_bank[bk]], writes=dst_bufs)
        else:
            P.op("dve", lambda e: e.tensor_copy(out=dst_ap, in_=bank[bk][:, :]), reads=[B_bank[bk]], writes=dst_bufs)

    def in_proj(l):
        dsts = [(qT[:, m, :], [B_q[m]]) for m in range(8)]
        dsts += [(kTv[:, v * 640:v * 640 + T], [B_k]) for v in range(4)]
        dsts += [(uT[:, m, :], [B_u[m]]) for m in range(8)]
        for m in range(IN_TILES):
            if (not do_ssm) and m >= 12:
                break
            s = wload("win", l, m, 0, 2048)
            bk = nxt("bank", 4)
            for kc in range(NCH):
                P.op("pe", lambda e, s=s, kc=kc, bk=bk: e.matmul(
                    bank[bk][:, :], lhsT=wslot[s][:, kc * 128:(kc + 1) * 128], rhs=hT[:, kc, :],
                    start=(kc == 0), stop=(kc == NCH - 1)),
                    reads=[B_ws[s], B_hT[kc]], writes=[B_bank[bk]])
            evac_bf16(dsts[m][0], bk, dsts[m][1])
        for pc4 in range(4):
            s = wload("wv", l, 0, pc4 * 2048, 2048)
            for j in range(4):
                kc = pc4 * 4 + j
                for blk in range(4):
                    P.op("pe", lambda e, s=s, j=j, kc=kc, blk=blk: e.matmul(
                        bank[blk][:, :], lhsT=hT[:, kc, blk * 128:(blk + 1) * 128], rhs=wslot[s][:, j * 512:(j + 1) * 512],
                        start=(kc == 0), stop=(kc == NCH - 1)),
                        reads=[B_ws[s], B_hT[kc]], writes=[B_bank[blk]])
        for blk in range(4):
            evac_bf16(Vt[:, blk, :], blk, [B_V[blk]])

    def attention(l, ti):
        for i in range(4):
            first = (ti == 0 and i == 0)
            for qt in range(8):
                kv = qt // 4
                bS = nxt("bank", 4)
                segs = []
                for hh in range(2):
                    var = kv * 2 + hh
                    for pc in range(2):
                        if pc == 0 and first:
                            continue
                        col = hh * 256 + pc * 128
                        if pc == 0:
                            klhs = kcar[l][:, var, :] if i == 0 else kTv[:, var * 640 + (i - 1) * 128:var * 640 + i * 128]
                            kb = [B_kcar[l]] if i == 0 else [B_k]
                            msk = maskP
                        else:
                            klhs = kTv[:, var * 640 + i * 128:var * 640 + (i + 1) * 128]
                            kb = [B_k]
                            msk = maskC
                        P.op("pe", lambda e, col=col, klhs=klhs, bS=bS, qt=qt, i=i: e.matmul(
                            bank[bS][:, col:col + 128], lhsT=klhs, rhs=qT[:, qt, i * 128:(i + 1) * 128],
                            start=True, stop=False), reads=kb + [B_q[qt]], writes=[B_bank[bS]])
                        P.op("pe", lambda e, col=col, msk=msk, bS=bS: e.matmul(
                            bank[bS][:, col:col + 128], lhsT=ident_b, rhs=msk, start=False, stop=True),
                            reads=[B_cbf], writes=[B_bank[bS]])
                        segs.append((hh, pc, col))
                po = qt * 512
                if first:
                    for hh in range(2):
                        col = hh * 256 + 128
                        P.op("act", lambda e, col=col, bS=bS, po=po: e.activation(
                            out=PT[:, po + col:po + col + 128], in_=bank[bS][:, col:col + 128], func=AF.Exp, scale=0.125),
                            reads=[B_bank[bS]], writes=[B_PT[qt]])
                else:
                    P.op("act", lambda e, bS=bS, po=po: e.activation(
                        out=PT[:, po:po + 512], in_=bank[bS][:, :], func=AF.Exp, scale=0.125),
                        reads=[B_bank[bS]], writes=[B_PT[qt]])
                bO = 5 + nxt("obank", 2)
                for part in range(2):
                    for n_, (hh, pc, col) in enumerate(segs):
                        var = kv * 2 + hh
                        if part == 0:
                            if pc == 0:
                                lh = vcar[l][:, var * 128:(var + 1) * 128] if i == 0 else Vt[:, i - 1, var * 128:(var + 1) * 128]
                                vb = [B_vcar[l]] if i == 0 else [B_V[i - 1]]
                            else:
                                lh = Vt[:, i, var * 128:(var + 1) * 128]
                                vb = [B_V[i]]
                        else:
                            lh = onesP[hh]
                            vb = [B_cbf]
                        P.op("pe", lambda e, part=part, lh=lh, col=col, po=po, bO=bO, n_=n_, ns=len(segs): e.matmul(
                            bank[bO][:, part * 128:(part + 1) * 128], lhsT=lh, rhs=PT[:, po + col:po + col + 128],
                            start=(n_ == 0), stop=(n_ == ns - 1)),
                            reads=vb + [B_PT[qt]], writes=[B_bank[bO]])
                t = nxt("tmp", NTMP)
                P.op("dve", lambda e, t=t, bO=bO, qt=qt: e.tensor_scalar(
                    out=tmpf[t][:, 0:128], in0=bank[bO][:, 128:256], scalar1=esk[:, l, qt:qt + 1], scalar2=None, op0=ALU.add),
                    reads=[B_bank[bO], B_esk], writes=[B_tmp[t]])
                P.op("dve", lambda e, t=t: e.reciprocal(out=tmpf[t][:, 128:256], in_=tmpf[t][:, 0:128]),
                     reads=[B_tmp[t]], writes=[B_tmp[t]])
                P.op("dve", lambda e, t=t, bO=bO, qt=qt, i=i: e.tensor_tensor(
                    out=attnT[:, qt, i * 128:(i + 1) * 128], in0=bank[bO][:, 0:128], in1=tmpf[t][:, 128:256], op=ALU.mult),
                    reads=[B_bank[bO], B_tmp[t]], writes=[B_att[qt]])
        for v in range(4):
            P.op("pool", lambda e, v=v: e.tensor_copy(out=kcar[l][:, v, :], in_=kTv[:, v * 640 + 384:v * 640 + 512]),
                 reads=[B_k], writes=[B_kcar[l]])
        P.op("pool", lambda e: e.tensor_copy(out=vcar[l][:], in_=Vt[:, 3, :]), reads=[B_V[3]], writes=[B_vcar[l]])

    Uall = hid[:, 0:8, :]
    zT = hid[:, 8:16, :]
    XB = hid[:, 25:33, :].rearrange("p a b -> p (a b)")
    ssmT = hid[:, 33:41, :]
    Yct = hid[:, 41, :]
    B_z = [Buf("z%d" % i) for i in range(8)]
    B_XB = Buf("XB"); B_Y = Buf("Yct")
    XSw = sb("XSw", [128, 65, 2, 32])
    B_XSw = Buf("XSw")
    rtmp = sb("rtmp", [128, 2, 2, 32])
    B_rtmp = Buf("rtmp")
    selp = [cbf[:, CB_SEL + g * 352:CB_SEL + (g + 1) * 352] for g in range(8)]

    def ssm_fwd(l, ti):
        for ct in range(8):
            bU = nxt("bank", 4)
            for g_lo in range(8):
                for s_ in range(8):
                    x0 = 112 + 16 * (g_lo - s_)
                    P.op("pe", lambda e, ct=ct, g_lo=g_lo, s_=s_, x0=x0, bU=bU: e.matmul(
                        bank[bU][:, g_lo * 64:(g_lo + 1) * 64], lhsT=selp[g_lo][:, x0:x0 + 128],
                        rhs=uT[:, ct, s_::8], start=(s_ == 0), stop=(s_ == 7)),
                        reads=[B_cbf, B_u[ct]], writes=[B_bank[bU]])
            evac_bf16(Uall[:, ct, :], bU, [B_q[ct]])
            sB = wload("ssm", l, ct, 0, 2048)
            bX = nxt("bank", 4)
            for pl in range(4):
                for ri in range(2):
                    for g2 in range(2):
                        g_lo = pl * 2 + g2
                        P.op("pe", lambda e, ct=ct, pl=pl, ri=ri, g2=g2, g_lo=g_lo, sB=sB, bX=bX: e.matmul(
                            bank[bX][:, (pl * 2 + ri) * 64:(pl * 2 + ri + 1) * 64],
                            lhsT=wslot[sB][:, (g_lo * 2 + ri) * 128:(g_lo * 2 + ri + 1) * 128],
                            rhs=Uall[:, ct, g_lo * 64:(g_lo + 1) * 64], start=(g2 == 0), stop=(g2 == 1)),
                            reads=[B_ws[sB], B_q[ct]], writes=[B_bank[bX]])
            for ri in range(2):
                src = bank[bX][:, :].rearrange("p (a r c) -> p r c a", r=2, c=64)[:, ri, :, :]
                P.op("dve", lambda e, ct=ct, ri=ri, src=src: e.tensor_copy(
                    out=XSw[:, 1:65, ri, ct * 4:(ct + 1) * 4], in_=src),
                    reads=[B_bank[bX]], writes=[B_XSw])
        P.op("pool", lambda e: e.tensor_copy(out=XSw[:, 0, :, :], in_=XSc[l][:]), reads=[B_XSc[l]], writes=[B_XSw])
        a8r = A8[l][:, 0, :].unsqueeze(1).to_broadcast([128, 2, 32])
        for c in range(1, 65):
            P.op("pool", lambda e, c=c: e.tensor_tensor(out=rtmp[:, 0, :, :], in0=XSw[:, c - 1, :, :], in1=a8r, op=ALU.mult),
                 reads=[B_XSw, B_A8[l]], writes=[B_rtmp])
            P.op("pool", lambda e, c=c: e.tensor_tensor(out=rtmp[:, 1, 0, :], in0=XSw[:, c - 1, 1, :], in1=A8[l][:, 2, :],
                                                        op=ALU.mult), reads=[B_XSw, B_A8[l]], writes=[B_rtmp])
            P.op("pool", lambda e, c=c: e.tensor_tensor(out=rtmp[:, 1, 1, :], in0=XSw[:, c - 1, 0, :], in1=A8[l][:, 1, :],
                                                        op=ALU.mult), reads=[B_XSw, B_A8[l]], writes=[B_rtmp])
            P.op("pool", lambda e, c=c: e.tensor_tensor(out=rtmp[:, 0, :, :], in0=rtmp[:, 0, :, :], in1=rtmp[:, 1, :, :],
                                                        op=ALU.add), reads=[B_rtmp], writes=[B_rtmp])
            P.op("pool", lambda e, c=c: e.tensor_tensor(out=XSw[:, c, :, :], in0=XSw[:, c, :, :], in1=rtmp[:, 0, :, :],
                                                        op=ALU.add), reads=[B_XSw, B_rtmp], writes=[B_XSw])
        P.op("pool", lambda e: e.tensor_copy(out=XSc[l][:], in_=XSw[:, 64, :, :]), reads=[B_XSw], writes=[B_XSc[l]])
        xb4 = XB.rearrange("p (q r c) -> p q r c", r=2, c=64)
        for ri in range(2):
            P.op("dve", lambda e, ri=ri: e.tensor_copy(
                out=xb4[:, :, ri, :], in_=XSw[:, 0:64, ri, :].rearrange("p c q -> p q c")),
                reads=[B_XSw], writes=[B_XB] + B_PT)
        for ct in range(8):
            sC = wload("ssm", l, ct, 2048, 2048)
            sK = wload("ssm", l, ct, 4096, 1024)
            bY = nxt("bank", 4)
            for g_lo in range(8):
                pair = ct * 4 + g_lo // 2
                oc = bank[bY][:, g_lo * 64:(g_lo + 1) * 64]
                P.op("pe", lambda e, ct=ct, g_lo=g_lo, sK=sK, oc=oc: e.matmul(
                    oc, lhsT=wslot[sK][:, g_lo * 128:(g_lo + 1) * 128], rhs=Uall[:, ct, g_lo * 64:(g_lo + 1) * 64],
                    start=True, stop=False), reads=[B_ws[sK], B_q[ct]], writes=[B_bank[bY]])
                for ri in range(2):
                    P.op("pe", lambda e, g_lo=g_lo, ri=ri, sC=sC, oc=oc, pair=pair: e.matmul(
                        oc, lhsT=wslot[sC][:, (g_lo * 2 + ri) * 128:(g_lo * 2 + ri + 1) * 128],
                        rhs=xb4[:, pair, ri, :], start=False, stop=(ri == 1)),
                        reads=[B_ws[sC], B_XB], writes=[B_bank[bY]])
            evac_bf16(Yct, bY, [B_Y])
            bZ = nxt("bank", 4)
            for t in range(8):
                for g_lo in range(8):
                    x0 = 112 + 16 * (t - g_lo)
                    P.op("pe", lambda e, t=t, g_lo=g_lo, x0=x0, bZ=bZ: e.matmul(
                        bank[bZ][:, t * 64:(t + 1) * 64], lhsT=selp[t][:, x0:x0 + 128],
                        rhs=Yct[:, g_lo * 64:(g_lo + 1) * 64], start=(g_lo == 0), stop=(g_lo == 7)),
                        reads=[B_cbf, B_Y], writes=[B_bank[bZ]])
            P.op("act", lambda e, ct=ct, bZ=bZ: e.activation(
                out=zT[:, ct, :].rearrange("p (c t) -> p t c", t=8),
                in_=bank[bZ][:, :].rearrange("p (t c) -> p t c", c=64), func=AF.Gelu_apprx_tanh),
                reads=[B_bank[bZ]], writes=[B_z[ct], B_k] + B_V)
        for m in range(8):
            s = wload("wglu", l, m, 0, 1024)
            bk = nxt("bank", 4)
            for kc in range(8):
                P.op("pe", lambda e, s=s, kc=kc, bk=bk: e.matmul(
                    bank[bk][:, :], lhsT=wslot[s][:, kc * 128:(kc + 1) * 128], rhs=zT[:, kc, :],
                    start=(kc == 0), stop=(kc == 7)), reads=[B_ws[s], B_z[kc]], writes=[B_bank[bk]])
            t = nxt("tmp", NTMP)
            P.op("act", lambda e, t=t, bk=bk: e.activation(out=tmpf[t][:], in_=bank[bk][:, :], func=AF.Sigmoid),
                 reads=[B_bank[bk]], writes=[B_tmp[t]])
            P.op("dve", lambda e, t=t, m=m: e.tensor_tensor(out=ssmT[:, m, :], in0=zT[:, m, :], in1=tmpf[t][:], op=ALU.mult),
                 reads=[B_z[m], B_tmp[t]], writes=[B_u[m]])

    def group_norm_to_hT(l, src, src_bufs, c0, gcol, ri):
        st = rms_begin()
        for c in range(8):
            rms_add(st, src[:, c, :], [src_bufs[c]], 8)
        rms_finish(ri, 1024)
        o = l * SP_L
        for c in range(8):
            t = nxt("tmp", NTMP)
            P.op("pool", lambda e, c=c, t=t: e.tensor_tensor(out=tmpf[t][:], in0=src[:, c, :], in1=Rt[ri][:], op=ALU.mult),
                 reads=[src_bufs[c], B_R[ri]], writes=[B_tmp[t]])
            P.op("act", lambda e, c=c, t=t: e.activation(out=hT[:, c0 + c, :], in_=tmpf[t][:], func=AF.Copy,
                                                         scale=small[:, o + gcol + c:o + gcol + c + 1]),
                 reads=[B_tmp[t], B_small], writes=[B_hT[c0 + c]])

    def mixer(l, ti):
        pre_norm(l, gs1, sh_m)
        in_proj(l)
        if do_attn:
            attention(l, ti)
        if do_ssm:
            ssm_fwd(l, ti)
        if do_attn:
            group_norm_to_hT(l, attnT, B_att, 0, SP_GATT, 2)
        else:
            for c in range(8):
                P.op("pool", lambda e, c=c: e.memset(hT[:, c, :], 0.0), writes=[B_hT[c]])
        if do_ssm:
            group_norm_to_hT(l, ssmT, B_u, 8, SP_GSSM, 2)
        else:
            for c in range(8, 16):
                P.op("pool", lambda e, c=c: e.memset(hT[:, c, :], 0.0), writes=[B_hT[c]])
        proj_to_mix("wout", l, NCH, lambda kc: hT[:, kc, :], lambda kc: [B_hT[kc]])
        post_update(l, gg1)

    BIS = int(os.environ.get("BIS", "9"))
    if do_ssm:
        allb = GEN_BUFS + B_hid + B_q + [B_k] + B_V + B_att + B_PT + B_u + B_z + [B_XB, B_Y] + B_hT
        P.op("pool", lambda e: e.memset(rtmp[:, 0, 0, 0:1], 0.0), writes=allb + [B_rtmp])
    for ti in range(n_tiles):
        if BIS >= 1:
            load_x(ti)
        for l in range(n_layers):
            if do_attn or do_ssm:
                mixer(l, ti)
            if do_ffn:
                ffn(l, ti)
        if BIS >= 2:
            store_x(ti)

    P.emit(nc, es)
    es.close()
    return nc


_CACHE = {}


def kernel(**inputs):
    inp = {k: np.asarray(v) for k, v in inputs.items()}
    sh = prep_shared(inp)
    in_maps = []
    for b in range(NB):
        m = dict(sh)
        sm = sh["small"].copy()
        sm[:, SP_C:SP_C + 16] = _fm(inp["c"][b], 16)
        m["small"] = sm
        m["x"] = np.ascontiguousarray(inp["x"][b])
        in_maps.append(m)
    if "nc" not in _CACHE:
        _CACHE["nc"] = build_nc()
    res = run_bass_kernel_spmd(_CACHE["nc"], in_maps, core_ids=list(range(NB)))
    return np.stack([r["y"] for r in res.results], axis=0).astype(np.float32)
```

```python
import os
import numpy as np
import ml_dtypes
from contextlib import ExitStack
import concourse.bass as bass
import concourse.mybir as mybir
from concourse.bass_utils import run_bass_kernel_spmd

F32 = mybir.dt.float32
BF16 = mybir.dt.bfloat16
AF = mybir.ActivationFunctionType
ALU = mybir.AluOpType

D = 2048
SEQ = 4096
NB = 8
DEPTH = 2
DFF = 5632
T = 512
NCH = D // 128
NFF = DFF // 128
EPS = 1e-6
NCHUNK = T // 8

ENGS = ["pe", "act", "dve", "pool", "sp"]


class Buf:
    __slots__ = ("name", "last_w", "readers", "excl")

    def __init__(self, name, excl=False):
        self.name = name
        self.last_w = None
        self.readers = {}
        self.excl = excl


class Op:
    __slots__ = ("eng", "fn", "deps", "idx", "signal", "count", "is_dma", "dsem", "dval", "prev_dma")

    def __init__(self, eng, fn, is_dma):
        self.eng = eng
        self.fn = fn
        self.deps = []
        self.signal = False
        self.count = 0
        self.is_dma = is_dma
        self.dsem = None
        self.dval = 0
        self.prev_dma = None


class Prog:
    NDMA = 12

    def __init__(self):
        self.ops = {e: [] for e in ENGS}
        self.ndma = {e: 0 for e in ENGS}
        self.dma_hist = {e: [] for e in ENGS}

    def op(self, eng, fn, reads=(), writes=(), dma=False):
        o = Op(eng, fn, dma)
        o.idx = len(self.ops[eng])
        deps = {}

        def add(d):
            if d is o:
                return
            if d.is_dma:
                deps[("dma", id(d))] = d
            else:
                k = ("eng", d.eng)
                if k not in deps or deps[k].idx < d.idx:
                    deps[k] = d

        for b in reads:
            if b.last_w is not None:
                add(b.last_w)
            if b.excl:
                for r in b.readers.values():
                    if r.eng != eng:
                        add(r)
        for b in writes:
            if b.last_w is not None:
                add(b.last_w)
            for r in b.readers.values():
                add(r)
        for d in deps.values():
            if (not d.is_dma) and d.eng == eng and eng == "pe":
                continue
            o.deps.append(d)
            if not d.is_dma:
                d.signal = True
        for b in reads:
            if dma:
                b.readers[("dma", id(o))] = o
            else:
                b.readers[("eng", eng)] = o
        for b in writes:
            b.last_w = o
            b.readers = {}
        if dma:
            n = self.ndma[eng]
            self.ndma[eng] += 1
            o.dsem = (eng, n % self.NDMA)
            o.dval = 16 * (n // self.NDMA + 1)
            if n >= self.NDMA:
                o.prev_dma = self.dma_hist[eng][n - self.NDMA]
            self.dma_hist[eng].append(o)
        self.ops[eng].append(o)
        return o

    def emit(self, nc, es):
        engsem = {e: es.enter_context(nc.semaphore("s_" + e)) for e in ENGS}
        dsem = {}
        for e in ENGS:
            for i in range(min(self.NDMA, self.ndma[e])):
                dsem[(e, i)] = es.enter_context(nc.semaphore("d_%s_%d" % (e, i)))
        for e in ENGS:
            c = 0
            for o in self.ops[e]:
                if o.signal and not o.is_dma:
                    c += 1
                    o.count = c
        block = es.enter_context(nc.Block())
        prog = self

        def run(e, eh):
            waited = {}
            for o in prog.ops[e]:
                wl = []
                for d in o.deps:
                    if d.is_dma:
                        wl.append((("d",) + d.dsem, dsem[d.dsem], d.dval))
                    else:
                        wl.append((("e", d.eng), engsem[d.eng], d.count))
                if o.prev_dma is not None:
                    d = o.prev_dma
                    wl.append((("d",) + d.dsem, dsem[d.dsem], d.dval))
                for k, s, v in wl:
                    if waited.get(k, 0) >= v:
                        continue
                    waited[k] = v
                    eh.wait_ge(s, v)
                inst = o.fn(eh)
                if o.is_dma:
                    inst.then_inc(dsem[o.dsem], 16)
                elif o.signal:
                    inst.then_inc(engsem[e], 1)
            for (q, i), s in dsem.items():
                if q == e and prog.ndma[e] > 0:
                    last = [d for d in prog.dma_hist[e] if d.dsem == (q, i)][-1]
                    if waited.get(("d", q, i), 0) < last.dval:
                        eh.wait_ge(s, last.dval)

        @block.tensor
        def _(eh):
            run("pe", eh)

        @block.scalar
        def _(eh):
            run("act", eh)

        @block.vector
        def _(eh):
            run("dve", eh)

        @block.gpsimd
        def _(eh):
            run("pool", eh)

        @block.sync
        def _(eh):
            run("sp", eh)


def _tile_w(w, cols_list):
    K = w.shape[0]
    KC = K // 128
    wz = np.concatenate([w, np.zeros((K, 1), w.dtype)], axis=1)
    out = np.empty((len(cols_list), 128, KC * len(cols_list[0])), np.float32)
    for m, cols in enumerate(cols_list):
        sub = wz[:, cols]
        C = sub.shape[1]
        out[m] = sub.reshape(KC, 128, C).transpose(1, 0, 2).reshape(128, KC * C)
    return out


def _fm(v, nch):
    return np.ascontiguousarray(v.reshape(nch, 128).T)


IN_TILES = 20


def _in_cols():
    cols = []
    for qt in range(8):
        cols.append(list(range(qt * 128, qt * 128 + 128)))
    k0 = list(range(1024, 1088))
    k1 = list(range(1088, 1152))
    z = [-1] * 64
    cols += [k0 + z, z + k0, k1 + z, z + k1]
    for ct in range(8):
        cols.append(list(range(1280 + ct * 128, 1280 + ct * 128 + 128)))
    return cols


def _v_cols():
    v0 = list(range(1152, 1216))
    v1 = list(range(1216, 1280))
    z = [-1] * 64
    return [v0 + z + z + v0 + v1 + z + z + v1]


SP_GPRE, SP_GPOST, SP_GPREF, SP_GPOSTF = 0, 16, 32, 48
SP_BADA = 64
SP_GATT, SP_GSSM = 160, 168
SP_CW = 176
SP_CB = 440
SP_SINK = 528
SP_L = 536
SP_C = 2 * SP_L
SP_ID = SP_C + 16
SP_HM = SP_ID + 128
SP_TOT = SP_HM + 2

SS_LR, SS_LI, SS_LS = 0, 32, 64
SS_BR, SS_BI, SS_CR, SS_CI = 96, 608, 1120, 1632
SS_D = 2144
SS_L = 2208

CB_ID = 0
CB_SEL = 128
CB_MP = CB_SEL + 8 * 352
CB_MC = CB_MP + 128
CB_ONE = CB_MC + 128
CB_OP0 = CB_ONE + 128
CB_OP1 = CB_OP0 + 128
CB_TOT = CB_OP1 + 128


def prep_shared(inp):
    sh = {}
    f32 = np.float32
    small = np.zeros((128, SP_TOT), f32)
    ssmp = np.zeros((128, 2 * SS_L), f32)
    for l in range(DEPTH):
        o = l * SP_L
        small[:, o + SP_GPRE:o + SP_GPRE + 16] = _fm(inp["g_pre_mix"][l], 16)
        small[:, o + SP_GPOST:o + SP_GPOST + 16] = _fm(inp["g_post_mix"][l], 16)
        small[:, o + SP_GPREF:o + SP_GPREF + 16] = _fm(inp["g_pre_ffn"][l], 16)
        small[:, o + SP_GPOSTF:o + SP_GPOSTF + 16] = _fm(inp["g_post_ffn"][l], 16)
        small[:, o + SP_BADA:o + SP_BADA + 96] = _fm(inp["b_ada"][l], 96)
        small[:, o + SP_GATT:o + SP_GATT + 8] = _fm(inp["g_attn_out"][l], 8)
        small[:, o + SP_GSSM:o + SP_GSSM + 8] = _fm(inp["g_ssm_out"][l], 8)
        cw = inp["conv_w"][l]
        small[:, o + SP_CW:o + SP_CW + 264] = cw.reshape(3, 88, 128).transpose(2, 1, 0).reshape(128, 264)
        small[:, o + SP_CB:o + SP_CB + 88] = _fm(inp["conv_b"][l], 88)
        sk = inp["attn_sinks"][l]
        small[:, o + SP_SINK:o + SP_SINK + 8] = sk.reshape(8, 2).T[np.arange(128) // 64]
        so = l * SS_L

        def pairlay(a):
            return a.reshape(32, 2, 64).transpose(1, 2, 0).reshape(128, 32)

        ssmp[:, so + SS_LR:so + SS_LR + 32] = pairlay(inp["lam_re"][l])
        ssmp[:, so + SS_LI:so + SS_LI + 32] = pairlay(inp["lam_im"][l])
        ssmp[:, so + SS_LS:so + SS_LS + 32] = pairlay(np.repeat(inp["log_step"][l][:, None], 64, axis=1))

        def pairlay3(a):
            return a.reshape(32, 2, 64, 16).transpose(1, 2, 0, 3).reshape(128, 512)

        ssmp[:, so + SS_BR:so + SS_BR + 512] = pairlay3(inp["ssm_b_re"][l])
        ssmp[:, so + SS_BI:so + SS_BI + 512] = pairlay3(inp["ssm_b_im"][l])
        ssmp[:, so + SS_CR:so + SS_CR + 512] = pairlay3(inp["ssm_c_re"][l].transpose(0, 2, 1))
        ssmp[:, so + SS_CI:so + SS_CI + 512] = pairlay3(inp["ssm_c_im"][l].transpose(0, 2, 1))
        dd = inp["ssm_d"][l]
        ssmp[:, so + SS_D:so + SS_D + 64] = np.tile(dd.T, (8, 1))
    small[:, SP_ID:SP_ID + 128] = np.eye(128, dtype=f32)
    small[:64, SP_HM] = 1.0
    small[64:, SP_HM + 1] = 1.0
    sh["small"] = small
    sh["ssmp"] = ssmp
    cb = np.zeros((128, CB_TOT), f32)
    cb[:, CB_ID:CB_ID + 128] = np.eye(128)
    k = np.arange(128)
    for g in range(8):
        rows = k[(k // 16) == g]
        cb[rows, CB_SEL + g * 352 + 112 + rows] = 1.0
    kk = k[:, None]
    qq = k[None, :]
    NEG = -30000.0
    cb[:, CB_MP:CB_MP + 128] = np.where(kk > qq, 0.0, NEG)
    cb[:, CB_MC:CB_MC + 128] = np.where(kk <= qq, 0.0, NEG)
    cb[:, CB_ONE:CB_ONE + 128] = 1.0
    cb[:, CB_OP0:CB_OP0 + 64] = 1.0
    cb[:, CB_OP1 + 64:CB_OP1 + 128] = 1.0
    sh["cbf"] = cb.astype(ml_dtypes.bfloat16)
    full = [list(range(m * 128, m * 128 + 128)) for m in range(96)]
    for l in range(DEPTH):
        sh["wada%d" % l] = _tile_w(inp["w_ada"][l], full)
        sh["win%d" % l] = _tile_w(inp["w_in"][l], _in_cols())
        sh["wv%d" % l] = _tile_w(inp["w_in"][l], _v_cols())
        sh["wglu%d" % l] = _tile_w(inp["w_glu"][l], full[:8])
        sh["wout%d" % l] = _tile_w(inp["w_out"][l], full[:16])
        sh["wup%d" % l] = _tile_w(inp["w_up"][l], full[:88])
        sh["wdn%d" % l] = _tile_w(inp["w_down"][l], full[:16])
    return sh


WSHAPES = {"win": (IN_TILES, 2048), "wv": (1, 8192), "wglu": (8, 1024), "wout": (16, 2048),
           "wup": (88, 2048), "wdn": (16, 5632)}


def build_nc(n_tiles=SEQ // T, n_layers=DEPTH, do_attn=True, do_ssm=True, do_ffn=True, dbg=False):
    nc = bass.Bass("TRN2", target_bir_lowering=False)
    P = Prog()
    es = ExitStack()

    def din(name, shape, dt=F32):
        return nc.dram_tensor(name, list(shape), dt, kind="ExternalInput").ap()

    x_in = din("x", [SEQ, D])
    small_d = din("small", [128, SP_TOT])
    ssmp_d = din("ssmp", [128, 2 * SS_L]) if do_ssm else None
    cbf_d = din("cbf", [128, CB_TOT], BF16)
    wada_d = [din("wada%d" % l, [96, 128, 2048]) for l in range(n_layers)]
    wsrc = {}
    wscr = {}
    for l in range(n_layers):
        for k, (M, C) in WSHAPES.items():
            wsrc[(k, l)] = din("%s%d" % (k, l), [M, 128, C])
            wscr[(k, l)] = nc.dram_tensor("s_%s%d" % (k, l), [M, 128, C], BF16, kind="Internal").ap()
        wscr[("ssm", l)] = nc.dram_tensor("s_ssm%d" % l, [8, 128, 5120], BF16, kind="Internal").ap()
    y_out = nc.dram_tensor("y", [SEQ, D], F32, kind="ExternalOutput").ap()

    def sb(name, shape, dt=F32):
        return es.enter_context(nc.sbuf_tensor("sb_" + name, list(shape), dt))

    small = sb("small", [128, SP_TOT])
    cbf = sb("cbf", [128, CB_TOT], BF16)
    ada = sb("ada", [128, DEPTH, 96])
    vec = sb("vec", [128, DEPTH, 4, 16])
    cact = sb("cact", [128, 16])
    epsb = sb("epsb", [128, 1])
    xT = sb("xT", [128, NCH, T])
    hT = sb("hT", [128, NCH, T], BF16)
    mixb = sb("mixb", [128, NCH, T], BF16)
    Rt = [sb("R%d" % i, [128, T]) for i in range(3)]
    NTMP = 6
    tmpf = [sb("tmpf%d" % i, [128, T]) for i in range(NTMP)]
    sqb = [sb("sqb%d" % i, [128, T], BF16) for i in range(3)]
    NSLOT = 6
    wslot = [sb("wslot%d" % i, [128, 2048], BF16) for i in range(NSLOT)]
    stage = sb("stage", [128, D])
    convc = [sb("convc%d" % l, [128, 88, 2]) for l in range(DEPTH)]
    hid = sb("hid", [128, NFF, T], BF16)
    bank = [es.enter_context(nc.psum_tensor("bank%d" % i, [128, 512], F32)) for i in range(8)]

    B_small = Buf("small"); B_cbf = Buf("cbf"); B_ada = Buf("ada"); B_vec = Buf("vec")
    B_cact = Buf("cact"); B_eps = Buf("eps"); B_xT = [Buf("xT%d" % c) for c in range(NCH)]
    B_hT = [Buf("hT%d" % c) for c in range(NCH)]; B_mix = [Buf("mix%d" % c) for c in range(NCH)]
    B_R = [Buf("R%d" % i) for i in range(3)]; B_tmp = [Buf("tmp%d" % i) for i in range(NTMP)]
    B_sq = [Buf("sq%d" % i) for i in range(3)]; B_ws = [Buf("ws%d" % i) for i in range(NSLOT)]
    B_stage = Buf("stage"); B_convc = [Buf("convc%d" % l) for l in range(DEPTH)]
    B_bank = [Buf("bank%d" % i, excl=True) for i in range(8)]
    B_scr = {k: Buf("scr_%s%d" % k) for k in wscr}

    ident_f = small[:, SP_ID:SP_ID + 128]
    ident_b = cbf[:, CB_ID:CB_ID + 128]
    ones_b = cbf[:, CB_ONE:CB_ONE + 128]

    ctr = {"tmp": 0, "sq": 0, "ws": 0, "bank": 0, "tbank": 0}

    def nxt(k, n):
        v = ctr[k]
        ctr[k] = (v + 1) % n
        return v

    P.op("sp", lambda e: e.dma_start(out=small[:], in_=small_d[:, :]), writes=[B_small], dma=True)
    P.op("sp", lambda e: e.dma_start(out=cbf[:], in_=cbf_d[:, :]), writes=[B_cbf], dma=True)
    P.op("dve", lambda e: e.memset(epsb[:], EPS), writes=[B_eps])
    P.op("act", lambda e: e.activation(out=cact[:], in_=small[:, SP_C:SP_C + 16], func=AF.Silu),
         reads=[B_small], writes=[B_cact])

    adast = [xT[:, 4 * i:4 * i + 4, :].rearrange("p a b -> p (a b)") for i in range(3)]
    B_adast = [B_xT[4 * i:4 * i + 4] for i in range(3)]
    cactb = sb("cactb", [128, 16], BF16)
    P.op("dve", lambda e: e.tensor_copy(out=cactb[:], in_=cact[:]), reads=[B_cact], writes=[B_cact])
    adab = [hT[:, 4 * i:4 * i + 4, :].rearrange("p a b -> p (a b)") for i in range(3)]
    B_adab = [B_hT[4 * i:4 * i + 4] for i in range(3)]
    for l in range(n_layers):
        for m in range(96):
            s = m % 3
            P.op("sp", lambda e, s=s, l=l, m=m: e.dma_start(out=adast[s], in_=wada_d[l][m]),
                 writes=B_adast[s], dma=True)
            if m % 2 == 0:
                P.op("act", lambda e, s=s: e.activation(out=adab[s], in_=adast[s], func=AF.Copy),
                     reads=B_adast[s], writes=B_adab[s])
            else:
                P.op("dve", lambda e, s=s: e.tensor_copy(out=adab[s], in_=adast[s]),
                     reads=B_adast[s], writes=B_adab[s])
            bk = 5 + (m % 2)
            for kc in range(16):
                P.op("pe", lambda e, s=s, kc=kc, bk=bk: e.matmul(
                    bank[bk][:, 0:1], lhsT=adab[s][:, kc * 128:(kc + 1) * 128], rhs=cactb[:, kc:kc + 1],
                    start=(kc == 0), stop=(kc == 15)),
                    reads=B_adab[s] + [B_cact], writes=[B_bank[bk]])
            P.op("dve", lambda e, l=l, m=m, bk=bk: e.tensor_tensor(
                out=ada[:, l, m:m + 1], in0=bank[bk][:, 0:1],
                in1=small[:, l * SP_L + SP_BADA + m:l * SP_L + SP_BADA + m + 1], op=ALU.add),
                reads=[B_bank[bk], B_small], writes=[B_ada])
        o = l * SP_L
        for j, (a0, g0) in enumerate([(16, SP_GPRE), (32, SP_GPOST), (64, SP_GPREF), (80, SP_GPOSTF)]):
            P.op("dve", lambda e, l=l, j=j, a0=a0, g0=g0, o=o: e.scalar_tensor_tensor(
                out=vec[:, l, j, :], in0=ada[:, l, a0:a0 + 16], scalar=1.0,
                in1=small[:, o + g0:o + g0 + 16], op0=ALU.add, op1=ALU.mult),
                reads=[B_ada, B_small], writes=[B_vec])

    def gs1(l, c): return vec[:, l, 0, c:c + 1]
    def gg1(l, c): return vec[:, l, 1, c:c + 1]
    def gs2(l, c): return vec[:, l, 2, c:c + 1]
    def gg2(l, c): return vec[:, l, 3, c:c + 1]
    def sh_m(l, c): return ada[:, l, 0 + c:c + 1]
    def sh_f(l, c): return ada[:, l, 48 + c:48 + c + 1]

    pre_f = adast
    pre_b = [hT[:, 4 * i:4 * i + 4, :].rearrange("p a b -> p (a b)") for i in range(3)]
    B_pref = B_adast
    B_preb = [B_hT[4 * i:4 * i + 4] for i in range(3)]
    pc = 0
    wlist = ["win", "wv", "wglu", "wout", "wup", "wdn"]
    if not do_ffn:
        wlist = ["win", "wv", "wglu", "wout"]
    for l in range(n_layers):
        for k in wlist:
            M, C = WSHAPES[k]
            for m in range(M):
                for c0 in range(0, C, 2048):
                    cw = min(2048, C - c0)
                    s = pc % 3
                    P.op("sp", lambda e, s=s, k=k, l=l, m=m, c0=c0, cw=cw: e.dma_start(
                        out=pre_f[s][:, :cw], in_=wsrc[(k, l)][m][:, c0:c0 + cw]),
                        writes=B_pref[s], dma=True)
                    ce = "act" if pc % 2 == 0 else "dve"
                    if ce == "act":
                        P.op("act", lambda e, s=s, cw=cw: e.activation(out=pre_b[s][:, :cw], in_=pre_f[s][:, :cw],
                                                                       func=AF.Copy),
                             reads=B_pref[s], writes=B_preb[s])
                    else:
                        P.op("dve", lambda e, s=s, cw=cw: e.tensor_copy(out=pre_b[s][:, :cw], in_=pre_f[s][:, :cw]),
                             reads=B_pref[s], writes=B_preb[s])
                    P.op("pool", lambda e, s=s, k=k, l=l, m=m, c0=c0, cw=cw: e.dma_start(
                        out=wscr[(k, l)][m][:, c0:c0 + cw], in_=pre_b[s][:, :cw]),
                        reads=B_preb[s], writes=[B_scr[(k, l)]], dma=True)
                    pc += 1


    import math
    hidb = hid[:, :, :].rearrange("p a b -> p (a b)")
    hf = hid[:, :, :].bitcast(F32).rearrange("p a b -> p (a b)")
    hTb = hT[:, :, :].rearrange("p a b -> p (a b)")
    A8 = [sb("A8_%d" % l, [128, 3, 32]) for l in range(DEPTH)]
    XSc = [sb("XSc%d" % l, [128, 2, 32]) for l in range(DEPTH)]
    B_A8 = [Buf("A8_%d" % l) for l in range(DEPTH)]
    B_XSc = [Buf("XSc%d" % l) for l in range(DEPTH)]
    B_prm = Buf("g_prm"); B_sc = Buf("g_sc"); B_BB = Buf("g_B"); B_bcf = Buf("g_bcf"); B_ccf = Buf("g_ccf")
    B_gt = Buf("g_t"); B_bctb = Buf("g_bctb"); B_cm0 = Buf("g_cm0")
    GEN_BUFS = [B_prm, B_sc, B_BB, B_bcf, B_ccf, B_gt, B_bctb, B_cm0]
    if do_ssm:
        prm = hf[:, 0:SS_L]
        def SC(i): return hf[:, 2208 + i * 32:2208 + (i + 1) * 32]
        I_DT, I_E1, I_MAG, I_ANG, I_KF, I_TMP, I_SARG, I_CARG, I_SIN, I_COS, I_AR, I_AI, I_DEN, I_RDEN, I_ARM1, I_FRE, I_FIM, I_T1, I_T2 = range(19)
        def PR(k): return SC(19 + k)
        def PI(k): return SC(28 + k)
        Bre = hf[:, 3392:3904]; Bim = hf[:, 3904:4416]
        BcF = hf[:, 4416:6336]
        CcF = hf[:, 6336:7360]
        G1 = hf[:, 7360:7424]; G2 = hf[:, 7424:7488]
        BcTb = hidb[:, 17100:19020]
        Cm0Z = hidb[:, 19020:21068]
        CcS = hTb[:, 0:2048]; KtS = hTb[:, 2048:3072]; BcS = hTb[:, 4096:6144]
        B_ccs = B_hT[0:4]; B_kts = B_hT[4:6]; B_bcs = B_hT[8:12]
        TWO_PI = 2.0 * math.pi

        def dv(fn, reads, writes):
            P.op("dve", fn, reads=reads, writes=writes)

        def tt(out, a, b, op, reads=(B_sc,), writes=(B_sc,)):
            dv(lambda e: e.tensor_tensor(out=out, in0=a, in1=b, op=op), list(reads), list(writes))

        def range_reduce(src_i, dst_i, shift):
            dv(lambda e: e.tensor_scalar(out=SC(dst_i), in0=SC(src_i), scalar1=shift, scalar2=None, op0=ALU.add),
               [B_sc], [B_sc])
            dv(lambda e: e.tensor_copy(out=SC(I_T1), in_=SC(dst_i)), [B_sc], [B_sc])
            for j in range(1, 12):
                thr = (2 * j - 1) * math.pi
                dv(lambda e, thr=thr: e.tensor_scalar(out=SC(I_TMP), in0=SC(I_T1), scalar1=thr, scalar2=-TWO_PI,
                                                      op0=ALU.is_gt, op1=ALU.mult), [B_sc], [B_sc])
                tt(SC(dst_i), SC(dst_i), SC(I_TMP), ALU.add)
            dv(lambda e: e.tensor_scalar(out=SC(dst_i), in0=SC(dst_i), scalar1=3.1415925, scalar2=-3.1415925,
                                         op0=ALU.min, op1=ALU.max), [B_sc], [B_sc])

        for l in range(n_layers):
            so = l * SS_L
            P.op("sp", lambda e, so=so: e.dma_start(out=prm, in_=ssmp_d[:, so:so + SS_L]), writes=[B_prm], dma=True)
            LR = prm[:, SS_LR:SS_LR + 32]; LI = prm[:, SS_LI:SS_LI + 32]; LS = prm[:, SS_LS:SS_LS + 32]
            P.op("act", lambda e, LS=LS: e.activation(out=SC(I_DT), in_=LS, func=AF.Exp), reads=[B_prm], writes=[B_sc])
            tt(SC(I_E1), LR, SC(I_DT), ALU.mult, reads=(B_prm, B_sc))
            P.op("act", lambda e: e.activation(out=SC(I_MAG), in_=SC(I_E1), func=AF.Exp), reads=[B_sc], writes=[B_sc])
            tt(SC(I_ANG), LI, SC(I_DT), ALU.mult, reads=(B_prm, B_sc))
            range_reduce(I_ANG, I_SARG, 0.0)
            range_reduce(I_ANG, I_CARG, 0.5 * math.pi)
            P.op("act", lambda e: e.activation(out=SC(I_SIN), in_=SC(I_SARG), func=AF.Sin), reads=[B_sc], writes=[B_sc])
            P.op("act", lambda e: e.activation(out=SC(I_COS), in_=SC(I_CARG), func=AF.Sin), reads=[B_sc], writes=[B_sc])
            tt(SC(I_AR), SC(I_MAG), SC(I_COS), ALU.mult)
            tt(SC(I_AI), SC(I_MAG), SC(I_SIN), ALU.mult)
            tt(SC(I_DEN), LR, LR, ALU.mult, reads=(B_prm, B_sc))
            tt(SC(I_T1), LI, LI, ALU.mult, reads=(B_prm, B_sc))
            tt(SC(I_DEN), SC(I_DEN), SC(I_T1), ALU.add)
            dv(lambda e: e.reciprocal(out=SC(I_RDEN), in_=SC(I_DEN)), [B_sc], [B_sc])
            dv(lambda e: e.tensor_scalar(out=SC(I_ARM1), in0=SC(I_AR), scalar1=-1.0, scalar2=None, op0=ALU.add),
               [B_sc], [B_sc])
            tt(SC(I_T1), SC(I_ARM1), LR, ALU.mult, reads=(B_prm, B_sc))
            tt(SC(I_T2), SC(I_AI), LI, ALU.mult, reads=(B_prm, B_sc))
            tt(SC(I_T1), SC(I_T1), SC(I_T2), ALU.add)
            tt(SC(I_FRE), SC(I_T1), SC(I_RDEN), ALU.mult)
            tt(SC(I_T1), SC(I_AI), LR, ALU.mult, reads=(B_prm, B_sc))
            tt(SC(I_T2), SC(I_ARM1), LI, ALU.mult, reads=(B_prm, B_sc))
            tt(SC(I_T1), SC(I_T1), SC(I_T2), ALU.subtract)
            tt(SC(I_FIM), SC(I_T1), SC(I_RDEN), ALU.mult)
            dv(lambda e: e.memset(PR(0), 1.0), [], [B_sc])
            dv(lambda e: e.memset(PI(0), 0.0), [], [B_sc])
            for k in range(1, 9):
                tt(SC(I_T1), PR(k - 1), SC(I_AR), ALU.mult)
                tt(SC(I_T2), PI(k - 1), SC(I_AI), ALU.mult)
                tt(PR(k), SC(I_T1), SC(I_T2), ALU.subtract)
                tt(SC(I_T1), PR(k - 1), SC(I_AI), ALU.mult)
                tt(SC(I_T2), PI(k - 1), SC(I_AR), ALU.mult)
                tt(PI(k), SC(I_T1), SC(I_T2), ALU.add)
            dv(lambda e, l=l: e.tensor_copy(out=A8[l][:, 0, :], in_=PR(8)), [B_sc], [B_A8[l]])
            dv(lambda e, l=l: e.tensor_copy(out=A8[l][:, 1, :], in_=PI(8)), [B_sc], [B_A8[l]])
            dv(lambda e, l=l: e.tensor_scalar(out=A8[l][:, 2, :], in0=PI(8), scalar1=-1.0, scalar2=None, op0=ALU.mult),
               [B_sc], [B_A8[l]])
            dv(lambda e, l=l: e.memset(XSc[l][:], 0.0), [], [B_XSc[l]])
            def b3(ap): return ap.rearrange("p (a b) -> p a b", b=16)
            def bc(i, p0=0, n=32): return SC(i)[:, p0:p0 + n].unsqueeze(2).to_broadcast([128, n, 16])
            BR = b3(prm[:, SS_BR:SS_BR + 512]); BI = b3(prm[:, SS_BI:SS_BI + 512])
            CR = b3(prm[:, SS_CR:SS_CR + 512]); CI = b3(prm[:, SS_CI:SS_CI + 512])
            G512a = hf[:, 7488:8000]; G512b = hf[:, 8000:8512]
            tt(b3(G512a), bc(I_FRE), BR, ALU.mult, reads=(B_prm, B_sc), writes=(B_gt,))
            tt(b3(G512b), bc(I_FIM), BI, ALU.mult, reads=(B_prm, B_sc), writes=(B_gt,))
            tt(b3(Bre), b3(G512a), b3(G512b), ALU.subtract, reads=(B_gt,), writes=(B_BB,))
            tt(b3(G512a), bc(I_FRE), BI, ALU.mult, reads=(B_prm, B_sc), writes=(B_gt,))
            tt(b3(G512b), bc(I_FIM), BR, ALU.mult, reads=(B_prm, B_sc), writes=(B_gt,))
            tt(b3(Bim), b3(G512a), b3(G512b), ALU.add, reads=(B_gt,), writes=(B_BB,))
            cm4 = Cm0Z.rearrange("p (a r g h) -> p a r g h", r=2, g=2, h=16)
            for g2 in range(2):
                hm = small[:, SP_HM + g2:SP_HM + g2 + 1]
                dv(lambda e, g2=g2, hm=hm, CR=CR: e.tensor_scalar(out=cm4[:, :, 0, g2, :], in0=CR, scalar1=hm, scalar2=None,
                                                                   op0=ALU.mult), [B_prm, B_small], [B_cm0])
                dv(lambda e, g2=g2, hm=hm, CI=CI: e.tensor_scalar(out=cm4[:, :, 1, g2, :], in0=CI, scalar1=hm, scalar2=-1.0,
                                                                   op0=ALU.mult, op1=ALU.mult), [B_prm, B_small], [B_cm0])
            bcf5 = BcF.rearrange("p (a r j h) -> p a r j h", r=2, j=15, h=16)
            ccf5 = CcF.rearrange("p (a r t h) -> p a r t h", r=2, t=8, h=16)
            g1 = G1.rearrange("p (a h) -> p a h", h=16); g2t = G2.rearrange("p (a h) -> p a h", h=16)
            bctb4 = BcTb.rearrange("p (a r x) -> p a r x", r=2, x=240)
            bcs4 = BcS.rearrange("p (g r x) -> p g r x", r=2, x=128)
            kts3 = KtS.rearrange("p (g x) -> p g x", x=128)
            for ct in range(8):
                p0 = ct * 4
                bre4 = b3(Bre)[:, p0:p0 + 4, :]; bim4 = b3(Bim)[:, p0:p0 + 4, :]
                cre4 = CR[:, p0:p0 + 4, :]; cim4 = CI[:, p0:p0 + 4, :]
                dv(lambda e: e.memset(BcF, 0.0), [], [B_bcf])
                for j in range(8):
                    k = 7 - j
                    prk = PR(k)[:, p0:p0 + 4].unsqueeze(2).to_broadcast([128, 4, 16])
                    pik = PI(k)[:, p0:p0 + 4].unsqueeze(2).to_broadcast([128, 4, 16])
                    tt(g1, prk, bre4, ALU.mult, reads=(B_sc, B_BB), writes=(B_gt,))
                    tt(g2t, pik, bim4, ALU.mult, reads=(B_sc, B_BB), writes=(B_gt,))
                    tt(bcf5[:, :, 0, j, :], g1, g2t, ALU.subtract, reads=(B_gt,), writes=(B_bcf,))
                    tt(g1, prk, bim4, ALU.mult, reads=(B_sc, B_BB), writes=(B_gt,))
                    tt(g2t, pik, bre4, ALU.mult, reads=(B_sc, B_BB), writes=(B_gt,))
                    tt(bcf5[:, :, 1, j, :], g1, g2t, ALU.add, reads=(B_gt,), writes=(B_bcf,))
                dv(lambda e: e.tensor_copy(out=BcTb, in_=BcF), [B_bcf], [B_bctb])
                for t in range(8):
                    prk = PR(t + 1)[:, p0:p0 + 4].unsqueeze(2).to_broadcast([128, 4, 16])
                    pik = PI(t + 1)[:, p0:p0 + 4].unsqueeze(2).to_broadcast([128, 4, 16])
                    tt(g1, prk, cre4, ALU.mult, reads=(B_sc, B_prm), writes=(B_gt,))
                    tt(g2t, pik, cim4, ALU.mult, reads=(B_sc, B_prm), writes=(B_gt,))
                    tt(ccf5[:, :, 0, t, :], g1, g2t, ALU.subtract, reads=(B_gt,), writes=(B_ccf,))
                    tt(g1, pik, cre4, ALU.mult, reads=(B_sc, B_prm), writes=(B_gt,))
                    tt(g2t, prk, cim4, ALU.mult, reads=(B_sc, B_prm), writes=(B_gt,))
                    dv(lambda e, t=t: e.scalar_tensor_tensor(out=ccf5[:, :, 1, t, :], in0=g1, scalar=-1.0, in1=g2t,
                                                             op0=ALU.mult, op1=ALU.subtract), [B_gt], [B_ccf])
                ccs_v = CcS.rearrange("p (a g x) -> p a g x", g=2, x=256)
                for g2 in range(2):
                    hm = small[:, SP_HM + g2:SP_HM + g2 + 1]
                    dv(lambda e, g2=g2, hm=hm: e.tensor_scalar(out=ccs_v[:, :, g2, :], in0=CcF.rearrange("p (a x) -> p a x", x=256),
                                                               scalar1=hm, scalar2=None, op0=ALU.mult),
                       [B_ccf, B_small], B_ccs)
                dv(lambda e: e.memset(BcS, 0.0), [], B_bcs)
                for pl in range(4):
                    pair = p0 + pl
                    bk = nxt("bank", 4)
                    for t in range(8):
                        for ri in range(2):
                            P.op("pe", lambda e, pl=pl, t=t, ri=ri, bk=bk, pair=pair: e.matmul(
                                bank[bk][:, t * 32:(t + 1) * 32],
                                lhsT=bctb4[:, pl, ri, (7 - t) * 16:(7 - t) * 16 + 128],
                                rhs=Cm0Z[:, (pair * 2 + ri) * 32:(pair * 2 + ri + 1) * 32],
                                start=(ri == 0), stop=(ri == 1)),
                                reads=[B_bctb, B_cm0], writes=[B_bank[bk]])
                    kview = bank[bk][:, 0:256].rearrange("p (t g h) -> p g t h", g=2, h=16)
                    idv = ident_f.rearrange("p (t h) -> p t h", h=16)
                    for g2 in range(2):
                        g = pair * 2 + g2
                        dcol = prm[:, SS_D + g:SS_D + g + 1]
                        dv(lambda e, g2=g2, pl=pl, dcol=dcol, kview=kview, idv=idv: e.scalar_tensor_tensor(
                            out=kts3[:, pl * 2 + g2, :].rearrange("p (t h) -> p t h", h=16), in0=idv, scalar=dcol,
                            in1=kview[:, g2, :, :], op0=ALU.mult, op1=ALU.add),
                            [B_bank[bk], B_prm, B_small], B_kts)
                    for ri in range(2):
                        bk2 = nxt("bank", 4)
                        P.op("pe", lambda e, pl=pl, ri=ri, bk2=bk2: e.matmul(
                            bank[bk2][:, 0:128], lhsT=bctb4[:, pl, ri, 0:128], rhs=ident_b, start=True, stop=True),
                            reads=[B_bctb, B_cbf], writes=[B_bank[bk2]])
                        for g2 in range(2):
                            dv(lambda e, pl=pl, ri=ri, g2=g2, bk2=bk2: e.tensor_copy(
                                out=bcs4[:, pl * 2 + g2, ri, g2 * 64:(g2 + 1) * 64], in_=bank[bk2][:, g2 * 64:(g2 + 1) * 64]),
                                [B_bank[bk2]], B_bcs)
                P.op("pool", lambda e, l=l, ct=ct: e.dma_start(out=wscr[("ssm", l)][ct][:, 0:2048], in_=BcS),
                     reads=B_bcs, writes=[B_scr[("ssm", l)]], dma=True)
                P.op("pool", lambda e, l=l, ct=ct: e.dma_start(out=wscr[("ssm", l)][ct][:, 2048:4096], in_=CcS),
                     reads=B_ccs, writes=[B_scr[("ssm", l)]], dma=True)
                P.op("pool", lambda e, l=l, ct=ct: e.dma_start(out=wscr[("ssm", l)][ct][:, 4096:5120], in_=KtS),
                     reads=B_kts, writes=[B_scr[("ssm", l)]], dma=True)

    def wload(k, l, m, c0, cw):
        s = nxt("ws", NSLOT)
        P.op("sp", lambda e: e.dma_start(out=wslot[s][:, :cw], in_=wscr[(k, l)][m][:, c0:c0 + cw]),
             reads=[B_scr[(k, l)]], writes=[B_ws[s]], dma=True)
        return s

    def rms_begin():
        return {"n": 0}

    def rms_add(st, src_ap, src_bufs, total):
        q = nxt("sq", 3)
        P.op("act", lambda e: e.activation(out=sqb[q][:], in_=src_ap, func=AF.Square),
             reads=src_bufs, writes=[B_sq[q]])
        n = st["n"]
        P.op("pe", lambda e: e.matmul(bank[4][:, :], lhsT=ones_b, rhs=sqb[q][:], start=(n == 0), stop=(n == total - 1)),
             reads=[B_sq[q], B_cbf], writes=[B_bank[4]])
        st["n"] = n + 1

    def rms_finish(ri, F):
        t = nxt("tmp", NTMP)
        P.op("act", lambda e: e.activation(out=tmpf[t][:], in_=bank[4][:, :], func=AF.Sqrt, bias=epsb[:, 0:1],
                                           scale=1.0 / F),
             reads=[B_bank[4], B_eps], writes=[B_tmp[t]])
        P.op("dve", lambda e: e.reciprocal(out=Rt[ri][:], in_=tmpf[t][:]), reads=[B_tmp[t]], writes=[B_R[ri]])

    def pre_norm(l, gsf, shf):
        st = rms_begin()
        for c in range(NCH):
            rms_add(st, xT[:, c, :], [B_xT[c]], NCH)
        rms_finish(0, D)
        for c in range(NCH):
            t = nxt("tmp", NTMP)
            P.op("pool" if c % 2 else "dve", lambda e, c=c, t=t: e.tensor_tensor(out=tmpf[t][:], in0=xT[:, c, :], in1=Rt[0][:], op=ALU.mult),
                 reads=[B_xT[c], B_R[0]], writes=[B_tmp[t]])
            P.op("act", lambda e, c=c, t=t: e.activation(out=hT[:, c, :], in_=tmpf[t][:], func=AF.Identity,
                                                         bias=shf(l, c), scale=gsf(l, c)),
                 reads=[B_tmp[t], B_vec, B_ada], writes=[B_hT[c]])

    def post_update(l, ggf):
        rms_finish(1, D)
        for c in range(NCH):
            t = nxt("tmp", NTMP)
            P.op("pool" if c % 2 else "dve", lambda e, c=c, t=t: e.tensor_tensor(out=tmpf[t][:], in0=mixb[:, c, :], in1=Rt[1][:], op=ALU.mult),
                 reads=[B_mix[c], B_R[1]], writes=[B_tmp[t]])
            P.op("dve", lambda e, c=c, t=t: e.scalar_tensor_tensor(
                out=xT[:, c, :], in0=tmpf[t][:], scalar=ggf(l, c), in1=xT[:, c, :], op0=ALU.mult, op1=ALU.add),
                reads=[B_tmp[t], B_vec, B_xT[c]], writes=[B_xT[c]])

    def proj_to_mix(k, l, KC, rhs_fn, rhs_bufs):
        st = rms_begin()
        for m in range(NCH):
            bk = nxt("bank", 4)
            kc = 0
            for c0 in range(0, KC * 128, 2048):
                cw = min(2048, KC * 128 - c0)
                s = wload(k, l, m, c0, cw)
                for j in range(cw // 128):
                    P.op("pe", lambda e, s=s, j=j, kc=kc, bk=bk: e.matmul(
                        bank[bk][:, :], lhsT=wslot[s][:, j * 128:(j + 1) * 128], rhs=rhs_fn(kc),
                        start=(kc == 0), stop=(kc == KC - 1)),
                        reads=[B_ws[s]] + rhs_bufs(kc), writes=[B_bank[bk]])
                    kc += 1
            P.op("dve", lambda e, m=m, bk=bk: e.tensor_copy(out=mixb[:, m, :], in_=bank[bk][:, :]),
                 reads=[B_bank[bk]], writes=[B_mix[m]])
            rms_add(st, bank[bk][:, :], [B_bank[bk]], NCH)

    B_hid = [Buf("hid%d" % j) for j in range(NFF)]
    cbuf = tmpf
    B_cbuf = B_tmp

    def ffn(l, ti):
        o = l * SP_L
        pre_norm(l, gs2, sh_f)
        for j in range(NFF):
            cbs = []
            for half in range(2):
                m = j + half * NFF
                s = wload("wup", l, m, 0, 2048)
                bk = nxt("bank", 4)
                for kc in range(NCH):
                    P.op("pe", lambda e, s=s, kc=kc, bk=bk: e.matmul(
                        bank[bk][:, :], lhsT=wslot[s][:, kc * 128:(kc + 1) * 128], rhs=hT[:, kc, :],
                        start=(kc == 0), stop=(kc == NCH - 1)),
                        reads=[B_ws[s], B_hT[kc]], writes=[B_bank[bk]])
                cb = nxt("tmp", NTMP)
                cbs.append(cb)
                w0 = small[:, o + SP_CW + m * 3 + 0:o + SP_CW + m * 3 + 1]
                w1 = small[:, o + SP_CW + m * 3 + 1:o + SP_CW + m * 3 + 2]
                w2 = small[:, o + SP_CW + m * 3 + 2:o + SP_CW + m * 3 + 3]
                bb = small[:, o + SP_CB + m:o + SP_CB + m + 1]
                P.op("act", lambda e, cb=cb, bk=bk, w2=w2, bb=bb: e.activation(
                    out=cbuf[cb][:], in_=bank[bk][:, :], func=AF.Identity, bias=bb, scale=w2),
                    reads=[B_bank[bk], B_small], writes=[B_cbuf[cb]])
                P.op("dve", lambda e, cb=cb, bk=bk, w1=w1: e.scalar_tensor_tensor(
                    out=cbuf[cb][:, 1:T], in0=bank[bk][:, 0:T - 1], scalar=w1, in1=cbuf[cb][:, 1:T],
                    op0=ALU.mult, op1=ALU.add),
                    reads=[B_bank[bk], B_small, B_cbuf[cb]], writes=[B_cbuf[cb]])
                P.op("dve", lambda e, cb=cb, bk=bk, w0=w0: e.scalar_tensor_tensor(
                    out=cbuf[cb][:, 2:T], in0=bank[bk][:, 0:T - 2], scalar=w0, in1=cbuf[cb][:, 2:T],
                    op0=ALU.mult, op1=ALU.add),
                    reads=[B_bank[bk], B_small, B_cbuf[cb]], writes=[B_cbuf[cb]])
                if ti > 0:
                    P.op("dve", lambda e, cb=cb, m=m, w1=w1: e.scalar_tensor_tensor(
                        out=cbuf[cb][:, 0:1], in0=convc[l][:, m, 1:2], scalar=w1, in1=cbuf[cb][:, 0:1],
                        op0=ALU.mult, op1=ALU.add),
                        reads=[B_convc[l], B_small, B_cbuf[cb]], writes=[B_cbuf[cb]])
                    P.op("dve", lambda e, cb=cb, m=m, w0=w0: e.scalar_tensor_tensor(
                        out=cbuf[cb][:, 0:2], in0=convc[l][:, m, 0:2], scalar=w0, in1=cbuf[cb][:, 0:2],
                        op0=ALU.mult, op1=ALU.add),
                        reads=[B_convc[l], B_small, B_cbuf[cb]], writes=[B_cbuf[cb]])
                P.op("dve", lambda e, m=m, bk=bk: e.tensor_copy(out=convc[l][:, m, 0:2], in_=bank[bk][:, T - 2:T]),
                     reads=[B_bank[bk]], writes=[B_convc[l]])
            cv, cg = cbs
            P.op("act", lambda e, cg=cg: e.activation(out=cbuf[cg][:], in_=cbuf[cg][:], func=AF.Gelu_apprx_tanh),
                 reads=[B_cbuf[cg]], writes=[B_cbuf[cg]])
            P.op("dve", lambda e, j=j, cv=cv, cg=cg: e.tensor_tensor(out=hid[:, j, :], in0=cbuf[cg][:], in1=cbuf[cv][:],
                                                                     op=ALU.mult),
                 reads=[B_cbuf[cg], B_cbuf[cv]], writes=[B_hid[j]])
        proj_to_mix("wdn", l, NFF, lambda kc: hid[:, kc, :], lambda kc: [B_hid[kc]])
        post_update(l, gg2)

    sthi = sb("sthi", [128, D], BF16)
    stlo = sb("stlo", [128, D], BF16)
    B_sthi = Buf("sthi"); B_stlo = Buf("stlo")

    def load_x(ti):
        for blk in range(T // 128):
            r0 = ti * T + blk * 128
            P.op("pool", lambda e, r0=r0: e.dma_start(out=stage[:], in_=x_in[r0:r0 + 128, :]),
                 writes=[B_stage], dma=True)
            P.op("act", lambda e: e.activation(out=sthi[:], in_=stage[:], func=AF.Copy),
                 reads=[B_stage], writes=[B_sthi])
            P.op("dve", lambda e: e.tensor_tensor(out=stlo[:], in0=stage[:], in1=sthi[:], op=ALU.subtract),
                 reads=[B_stage, B_sthi], writes=[B_stlo])
            for c4 in range(NCH // 4):
                bk = nxt("bank", 4)
                for q in range(4):
                    c = c4 * 4 + q
                    P.op("pe", lambda e, c=c, q=q, bk=bk: e.matmul(
                        bank[bk][:, q * 128:(q + 1) * 128], lhsT=sthi[:, c * 128:(c + 1) * 128], rhs=ident_b,
                        start=True, stop=False),
                        reads=[B_sthi, B_cbf], writes=[B_bank[bk]])
                    P.op("pe", lambda e, c=c, q=q, bk=bk: e.matmul(
                        bank[bk][:, q * 128:(q + 1) * 128], lhsT=stlo[:, c * 128:(c + 1) * 128], rhs=ident_b,
                        start=False, stop=True),
                        reads=[B_stlo, B_cbf], writes=[B_bank[bk]])
                for q in range(4):
                    c = c4 * 4 + q
                    if q % 2:
                        P.op("act", lambda e, c=c, q=q, bk=bk, blk=blk: e.activation(
                            out=xT[:, c, blk * 128:(blk + 1) * 128], in_=bank[bk][:, q * 128:(q + 1) * 128], func=AF.Copy),
                            reads=[B_bank[bk]], writes=[B_xT[c]])
                    else:
                        P.op("dve", lambda e, c=c, q=q, bk=bk, blk=blk: e.tensor_copy(
                            out=xT[:, c, blk * 128:(blk + 1) * 128], in_=bank[bk][:, q * 128:(q + 1) * 128]),
                            reads=[B_bank[bk]], writes=[B_xT[c]])

    xhi = sb("xhi", [128, 4, 128], BF16)
    xlo = sb("xlo", [128, 4, 128], BF16)
    B_xhi = Buf("xhi"); B_xlo = Buf("xlo")

    def store_x(ti):
        for blk in range(T // 128):
            r0 = ti * T + blk * 128
            for c4 in range(NCH // 4):
                bk = nxt("bank", 4)
                src = xT[:, c4 * 4:(c4 + 1) * 4, blk * 128:(blk + 1) * 128]
                P.op("act", lambda e, src=src: e.activation(out=xhi[:], in_=src, func=AF.Copy),
                     reads=B_xT[c4 * 4:(c4 + 1) * 4], writes=[B_xhi])
                P.op("dve", lambda e, src=src: e.tensor_tensor(out=xlo[:], in0=src, in1=xhi[:], op=ALU.subtract),
                     reads=B_xT[c4 * 4:(c4 + 1) * 4] + [B_xhi], writes=[B_xlo])
                for q in range(4):
                    P.op("pe", lambda e, q=q, bk=bk: e.matmul(
                        bank[bk][:, q * 128:(q + 1) * 128], lhsT=xhi[:, q, :], rhs=ident_b, start=True, stop=False),
                        reads=[B_xhi, B_cbf], writes=[B_bank[bk]])
                    P.op("pe", lambda e, q=q, bk=bk: e.matmul(
                        bank[bk][:, q * 128:(q + 1) * 128], lhsT=xlo[:, q, :], rhs=ident_b, start=False, stop=True),
                        reads=[B_xlo, B_cbf], writes=[B_bank[bk]])
                P.op("dve", lambda e, c4=c4, bk=bk: e.tensor_copy(out=stage[:, c4 * 512:(c4 + 1) * 512], in_=bank[bk][:, :]),
                     reads=[B_bank[bk]], writes=[B_stage])
            P.op("pool", lambda e, r0=r0: e.dma_start(out=y_out[r0:r0 + 128, :], in_=stage[:]),
                 reads=[B_stage], dma=True)

    hflat = hid[:, :, :]
    qT = hid[:, 0:8, :]
    kTv = hid[:, 8:13, :].rearrange("p a b -> p (a b)")
    Vt = hid[:, 13:17, :]
    attnT = hid[:, 17:25, :]
    PT = hid[:, 25:33, :].rearrange("p a b -> p (a b)")
    uT = hid[:, 33:41, :]
    B_q = [Buf("q%d" % i) for i in range(8)]
    B_k = Buf("kT"); B_V = [Buf("V%d" % i) for i in range(4)]
    B_att = [Buf("att%d" % i) for i in range(8)]
    B_PT = [Buf("PT%d" % i) for i in range(8)]
    B_u = [Buf("u%d" % i) for i in range(8)]
    kcar = [sb("kcar%d" % l, [128, 4, 128], BF16) for l in range(DEPTH)]
    vcar = [sb("vcar%d" % l, [128, 512], BF16) for l in range(DEPTH)]
    B_kcar = [Buf("kcar%d" % l) for l in range(DEPTH)]
    B_vcar = [Buf("vcar%d" % l) for l in range(DEPTH)]
    esk = sb("esk", [128, DEPTH, 8])
    B_esk = Buf("esk")
    for l in range(n_layers):
        P.op("act", lambda e, l=l: e.activation(out=esk[:, l, :], in_=small[:, l * SP_L + SP_SINK:l * SP_L + SP_SINK + 8],
                                                func=AF.Exp), reads=[B_small], writes=[B_esk])
    maskP = cbf[:, CB_MP:CB_MP + 128]
    maskC = cbf[:, CB_MC:CB_MC + 128]
    onesP = [cbf[:, CB_OP0:CB_OP0 + 128], cbf[:, CB_OP1:CB_OP1 + 128]]
    ctr["obank"] = 0
    ctr["ev"] = 0

    def evac_bf16(dst_ap, bk, dst_bufs):
        if nxt("ev", 2) == 0:
            P.op("act", lambda e: e.activation(out=dst_ap, in_=bank[bk][:, :], func=AF.Copy),
                 reads=[B_bank[bk]], writes=dst_bufs)
        else:
            P.op("dve", lambda e: e.tensor_copy(out=dst_ap, in_=bank[bk][:, :]), reads=[B_bank[bk]], writes=dst_bufs)

    def in_proj(l):
        dsts = [(qT[:, m, :], [B_q[m]]) for m in range(8)]
        dsts += [(kTv[:, v * 640:v * 640 + T], [B_k]) for v in range(4)]
        dsts += [(uT[:, m, :], [B_u[m]]) for m in range(8)]
        for m in range(IN_TILES):
            if (not do_ssm) and m >= 12:
                break
            s = wload("win", l, m, 0, 2048)
            bk = nxt("bank", 4)
            for kc in range(NCH):
                P.op("pe", lambda e, s=s, kc=kc, bk=bk: e.matmul(
                    bank[bk][:, :], lhsT=wslot[s][:, kc * 128:(kc + 1) * 128], rhs=hT[:, kc, :],
                    start=(kc == 0), stop=(kc == NCH - 1)),
                    reads=[B_ws[s], B_hT[kc]], writes=[B_bank[bk]])
            evac_bf16(dsts[m][0], bk, dsts[m][1])
        for pc4 in range(4):
            s = wload("wv", l, 0, pc4 * 2048, 2048)
            for j in range(4):
                kc = pc4 * 4 + j
                for blk in range(4):
                    P.op("pe", lambda e, s=s, j=j, kc=kc, blk=blk: e.matmul(
                        bank[blk][:, :], lhsT=hT[:, kc, blk * 128:(blk + 1) * 128], rhs=wslot[s][:, j * 512:(j + 1) * 512],
                        start=(kc == 0), stop=(kc == NCH - 1)),
                        reads=[B_ws[s], B_hT[kc]], writes=[B_bank[blk]])
        for blk in range(4):
            evac_bf16(Vt[:, blk, :], blk, [B_V[blk]])

    def attention(l, ti):
        for i in range(4):
            first = (ti == 0 and i == 0)
            for qt in range(8):
                kv = qt // 4
                bS = nxt("bank", 4)
                segs = []
                for hh in range(2):
                    var = kv * 2 + hh
                    for pc in range(2):
                        if pc == 0 and first:
                            continue
                        col = hh * 256 + pc * 128
                        if pc == 0:
                            klhs = kcar[l][:, var, :] if i == 0 else kTv[:, var * 640 + (i - 1) * 128:var * 640 + i * 128]
                            kb = [B_kcar[l]] if i == 0 else [B_k]
                            msk = maskP
                        else:
                            klhs = kTv[:, var * 640 + i * 128:var * 640 + (i + 1) * 128]
                            kb = [B_k]
                            msk = maskC
                        P.op("pe", lambda e, col=col, klhs=klhs, bS=bS, qt=qt, i=i: e.matmul(
                            bank[bS][:, col:col + 128], lhsT=klhs, rhs=qT[:, qt, i * 128:(i + 1) * 128],
                            start=True, stop=False), reads=kb + [B_q[qt]], writes=[B_bank[bS]])
                        P.op("pe", lambda e, col=col, msk=msk, bS=bS: e.matmul(
                            bank[bS][:, col:col + 128], lhsT=ident_b, rhs=msk, start=False, stop=True),
                            reads=[B_cbf], writes=[B_bank[bS]])
                        segs.append((hh, pc, col))
                po = qt * 512
                if first:
                    for hh in range(2):
                        col = hh * 256 + 128
                        P.op("act", lambda e, col=col, bS=bS, po=po: e.activation(
                            out=PT[:, po + col:po + col + 128], in_=bank[bS][:, col:col + 128], func=AF.Exp, scale=0.125),
                            reads=[B_bank[bS]], writes=[B_PT[qt]])
                else:
                    P.op("act", lambda e, bS=bS, po=po: e.activation(
                        out=PT[:, po:po + 512], in_=bank[bS][:, :], func=AF.Exp, scale=0.125),
                        reads=[B_bank[bS]], writes=[B_PT[qt]])
                bO = 5 + nxt("obank", 2)
                for part in range(2):
                    for n_, (hh, pc, col) in enumerate(segs):
                        var = kv * 2 + hh
                        if part == 0:
                            if pc == 0:
                                lh = vcar[l][:, var * 128:(var + 1) * 128] if i == 0 else Vt[:, i - 1, var * 128:(var + 1) * 128]
                                vb = [B_vcar[l]] if i == 0 else [B_V[i - 1]]
                            else:
                                lh = Vt[:, i, var * 128:(var + 1) * 128]
                                vb = [B_V[i]]
                        else:
                            lh = onesP[hh]
                            vb = [B_cbf]
                        P.op("pe", lambda e, part=part, lh=lh, col=col, po=po, bO=bO, n_=n_, ns=len(segs): e.matmul(
                            bank[bO][:, part * 128:(part + 1) * 128], lhsT=lh, rhs=PT[:, po + col:po + col + 128],
                            start=(n_ == 0), stop=(n_ == ns - 1)),
                            reads=vb + [B_PT[qt]], writes=[B_bank[bO]])
                t = nxt("tmp", NTMP)
                P.op("dve", lambda e, t=t, bO=bO, qt=qt: e.tensor_scalar(
                    out=tmpf[t][:, 0:128], in0=bank[bO][:, 128:256], scalar1=esk[:, l, qt:qt + 1], scalar2=None, op0=ALU.add),
                    reads=[B_bank[bO], B_esk], writes=[B_tmp[t]])
                P.op("dve", lambda e, t=t: e.reciprocal(out=tmpf[t][:, 128:256], in_=tmpf[t][:, 0:128]),
                     reads=[B_tmp[t]], writes=[B_tmp[t]])
                P.op("dve", lambda e, t=t, bO=bO, qt=qt, i=i: e.tensor_tensor(
                    out=attnT[:, qt, i * 128:(i + 1) * 128], in0=bank[bO][:, 0:128], in1=tmpf[t][:, 128:256], op=ALU.mult),
                    reads=[B_bank[bO], B_tmp[t]], writes=[B_att[qt]])
        for v in range(4):
            P.op("pool", lambda e, v=v: e.tensor_copy(out=kcar[l][:, v, :], in_=kTv[:, v * 640 + 384:v * 640 + 512]),
                 reads=[B_k], writes=[B_kcar[l]])
        P.op("pool", lambda e: e.tensor_copy(out=vcar[l][:], in_=Vt[:, 3, :]), reads=[B_V[3]], writes=[B_vcar[l]])

    Uall = hid[:, 33:41, :]
    zT = hid[:, 8:16, :]
    XB = mixb[:, 0:8, :].rearrange("p a b -> p (a b)")
    ssmT = hid[:, 0:8, :]
    Yct = hid[:, 41, :]
    B_z = [Buf("z%d" % i) for i in range(8)]
    B_XB = Buf("XB"); B_Y = Buf("Yct")
    XSw = sb("XSw", [128, 65, 2, 32])
    B_XSw = Buf("XSw")
    B_XSwh = [Buf("XSw0"), Buf("XSw1")]
    B_rtmph = [Buf("rtmp0"), Buf("rtmp1")]
    B_rtmp2 = Buf("rtmp2")
    rtmp = sb("rtmp", [128, 2, 2, 32])
    B_rtmp = Buf("rtmp")
    selp = [cbf[:, CB_SEL + g * 352:CB_SEL + (g + 1) * 352] for g in range(8)]

    def ssm_A(l, ti):
        for ct in range(8):
            bU = nxt("bank", 4)
            for g_lo in range(8):
                for s_ in range(8):
                    x0 = 112 + 16 * (g_lo - s_)
                    P.op("pe", lambda e, ct=ct, g_lo=g_lo, s_=s_, x0=x0, bU=bU: e.matmul(
                        bank[bU][:, g_lo * 64:(g_lo + 1) * 64], lhsT=selp[g_lo][:, x0:x0 + 128],
                        rhs=uT[:, ct, s_::8], start=(s_ == 0), stop=(s_ == 7)),
                        reads=[B_cbf, B_u[ct]], writes=[B_bank[bU]])
            evac_bf16(Uall[:, ct, :], bU, [B_u[ct]])
            sB = wload("ssm", l, ct, 0, 2048)
            bX = nxt("bank", 4)
            for pl in range(4):
                for ri in range(2):
                    for g2 in range(2):
                        g_lo = pl * 2 + g2
                        P.op("pe", lambda e, ct=ct, pl=pl, ri=ri, g2=g2, g_lo=g_lo, sB=sB, bX=bX: e.matmul(
                            bank[bX][:, (pl * 2 + ri) * 64:(pl * 2 + ri + 1) * 64],
                            lhsT=wslot[sB][:, (g_lo * 2 + ri) * 128:(g_lo * 2 + ri + 1) * 128],
                            rhs=Uall[:, ct, g_lo * 64:(g_lo + 1) * 64], start=(g2 == 0), stop=(g2 == 1)),
                            reads=[B_ws[sB], B_u[ct]], writes=[B_bank[bX]])
            for ri in range(2):
                src = bank[bX][:, :].rearrange("p (a r c) -> p r c a", r=2, c=64)[:, ri, :, :]
                P.op("dve", lambda e, ct=ct, ri=ri, src=src: e.tensor_copy(
                    out=XSw[:, 1:65, ri, ct * 4:(ct + 1) * 4], in_=src),
                    reads=[B_bank[bX]], writes=[B_XSw])
        P.op("pool", lambda e: e.tensor_copy(out=XSw[:, 0, :, :], in_=XSc[l][:]), reads=[B_XSc[l]], writes=[B_XSw])
        a8r = A8[l][:, 0, :].unsqueeze(1).to_broadcast([128, 2, 32])
        a8s = A8[l][:, 1:3, :][:, ::-1, :]
        for c in range(1, 65):
            P.op("pool", lambda e, c=c: e.tensor_tensor(out=rtmp[:, 0, :, :], in0=XSw[:, c - 1, :, :], in1=a8r, op=ALU.mult),
                 reads=[B_XSw, B_A8[l]], writes=[B_rtmp])
            P.op("pool", lambda e, c=c: e.tensor_tensor(out=rtmp[:, 1, :, :], in0=XSw[:, c - 1, ::-1, :], in1=a8s, op=ALU.mult),
                 reads=[B_XSw, B_A8[l]], writes=[B_rtmp2])
            P.op("pool", lambda e, c=c: e.tensor_tensor(out=XSw[:, c, :, :], in0=XSw[:, c, :, :], in1=rtmp[:, 0, :, :], op=ALU.add),
                 reads=[B_XSw, B_rtmp], writes=[B_XSw])
            P.op("pool", lambda e, c=c: e.tensor_tensor(out=XSw[:, c, :, :], in0=XSw[:, c, :, :], in1=rtmp[:, 1, :, :], op=ALU.add),
                 reads=[B_XSw, B_rtmp2], writes=[B_XSw])
        P.op("pool", lambda e: e.tensor_copy(out=XSc[l][:], in_=XSw[:, 64, :, :]), reads=[B_XSw], writes=[B_XSc[l]])

    xb4 = XB.rearrange("p (q r c) -> p q r c", r=2, c=64)

    def ssm_B(l, ti):
        for ri in range(2):
            P.op("dve", lambda e, ri=ri: e.tensor_copy(
                out=xb4[:, :, ri, :], in_=XSw[:, 0:64, ri, :].rearrange("p c q -> p q c")),
                reads=[B_XSw], writes=[B_XB] + B_mix[0:8])
        for ct in range(8):
            sC = wload("ssm", l, ct, 2048, 2048)
            sK = wload("ssm", l, ct, 4096, 1024)
            bY = nxt("bank", 4)
            for g_lo in range(8):
                pair = ct * 4 + g_lo // 2
                oc = bank[bY][:, g_lo * 64:(g_lo + 1) * 64]
                P.op("pe", lambda e, ct=ct, g_lo=g_lo, sK=sK, oc=oc: e.matmul(
                    oc, lhsT=wslot[sK][:, g_lo * 128:(g_lo + 1) * 128], rhs=Uall[:, ct, g_lo * 64:(g_lo + 1) * 64],
                    start=True, stop=False), reads=[B_ws[sK], B_u[ct]], writes=[B_bank[bY]])
                for ri in range(2):
                    P.op("pe", lambda e, g_lo=g_lo, ri=ri, sC=sC, oc=oc, pair=pair: e.matmul(
                        oc, lhsT=wslot[sC][:, (g_lo * 2 + ri) * 128:(g_lo * 2 + ri + 1) * 128],
                        rhs=xb4[:, pair, ri, :], start=False, stop=(ri == 1)),
                        reads=[B_ws[sC], B_XB], writes=[B_bank[bY]])
            evac_bf16(Yct, bY, [B_Y])
            bZ = nxt("bank", 4)
            for t in range(8):
                for g_lo in range(8):
                    x0 = 112 + 16 * (t - g_lo)
                    P.op("pe", lambda e, t=t, g_lo=g_lo, x0=x0, bZ=bZ: e.matmul(
                        bank[bZ][:, t * 64:(t + 1) * 64], lhsT=selp[t][:, x0:x0 + 128],
                        rhs=Yct[:, g_lo * 64:(g_lo + 1) * 64], start=(g_lo == 0), stop=(g_lo == 7)),
                        reads=[B_cbf, B_Y], writes=[B_bank[bZ]])
            P.op("act", lambda e, ct=ct, bZ=bZ: e.activation(
                out=zT[:, ct, :].rearrange("p (c t) -> p t c", t=8),
                in_=bank[bZ][:, :].rearrange("p (t c) -> p t c", c=64), func=AF.Gelu_apprx_tanh),
                reads=[B_bank[bZ]], writes=[B_z[ct], B_k] + B_V)
        for m in range(8):
            s = wload("wglu", l, m, 0, 1024)
            bk = nxt("bank", 4)
            for kc in range(8):
                P.op("pe", lambda e, s=s, kc=kc, bk=bk: e.matmul(
                    bank[bk][:, :], lhsT=wslot[s][:, kc * 128:(kc + 1) * 128], rhs=zT[:, kc, :],
                    start=(kc == 0), stop=(kc == 7)), reads=[B_ws[s], B_z[kc]], writes=[B_bank[bk]])
            t = nxt("tmp", NTMP)
            P.op("act", lambda e, t=t, bk=bk: e.activation(out=tmpf[t][:], in_=bank[bk][:, :], func=AF.Sigmoid),
                 reads=[B_bank[bk]], writes=[B_tmp[t]])
            P.op("dve", lambda e, t=t, m=m: e.tensor_tensor(out=ssmT[:, m, :], in0=zT[:, m, :], in1=tmpf[t][:], op=ALU.mult),
                 reads=[B_z[m], B_tmp[t]], writes=[B_q[m]])

    def group_norm_to_hT(l, src, src_bufs, c0, gcol, ri):
        st = rms_begin()
        for c in range(8):
            rms_add(st, src[:, c, :], [src_bufs[c]], 8)
        rms_finish(ri, 1024)
        o = l * SP_L
        for c in range(8):
            t = nxt("tmp", NTMP)
            P.op("pool", lambda e, c=c, t=t: e.tensor_tensor(out=tmpf[t][:], in0=src[:, c, :], in1=Rt[ri][:], op=ALU.mult),
                 reads=[src_bufs[c], B_R[ri]], writes=[B_tmp[t]])
            P.op("act", lambda e, c=c, t=t: e.activation(out=hT[:, c0 + c, :], in_=tmpf[t][:], func=AF.Copy,
                                                         scale=small[:, o + gcol + c:o + gcol + c + 1]),
                 reads=[B_tmp[t], B_small], writes=[B_hT[c0 + c]])

    def mixer(l, ti):
        pre_norm(l, gs1, sh_m)
        in_proj(l)
        if do_ssm:
            ssm_A(l, ti)
        if do_attn:
            attention(l, ti)
        if do_ssm:
            ssm_B(l, ti)
        if do_attn:
            group_norm_to_hT(l, attnT, B_att, 0, SP_GATT, 2)
        else:
            for c in range(8):
                P.op("pool", lambda e, c=c: e.memset(hT[:, c, :], 0.0), writes=[B_hT[c]])
        if do_ssm:
            group_norm_to_hT(l, ssmT, B_q, 8, SP_GSSM, 2)
        else:
            for c in range(8, 16):
                P.op("pool", lambda e, c=c: e.memset(hT[:, c, :], 0.0), writes=[B_hT[c]])
        proj_to_mix("wout", l, NCH, lambda kc: hT[:, kc, :], lambda kc: [B_hT[kc]])
        post_update(l, gg1)

    BIS = int(os.environ.get("BIS", "9"))
    if do_ssm:
        allb = GEN_BUFS + B_hid + B_q + [B_k] + B_V + B_att + B_PT + B_u + B_z + [B_XB, B_Y] + B_hT
        P.op("pool", lambda e: e.memset(rtmp[:, 0, 0, 0:1], 0.0), writes=allb + [B_rtmp, B_rtmp2])
    for ti in range(n_tiles):
        if BIS >= 1:
            load_x(ti)
        for l in range(n_layers):
            if do_attn or do_ssm:
                mixer(l, ti)
            if do_ffn:
                ffn(l, ti)
        if BIS >= 2:
            store_x(ti)

    P.emit(nc, es)
    es.close()
    return nc


_CACHE = {}


def kernel(**inputs):
    inp = {k: np.asarray(v) for k, v in inputs.items()}
    sh = prep_shared(inp)
    in_maps = []
    for b in range(NB):
        m = dict(sh)
        sm = sh["small"].copy()
        sm[:, SP_C:SP_C + 16] = _fm(inp["c"][b], 16)
        m["small"] = sm
        m["x"] = np.ascontiguousarray(inp["x"][b])
        in_maps.append(m)
    if "nc" not in _CACHE:
        _CACHE["nc"] = build_nc()
    res = run_bass_kernel_spmd(_CACHE["nc"], in_maps, core_ids=list(range(NB)))
    return np.stack([r["y"] for r in res.results], axis=0).astype(np.float32)
```

```python
import os
import numpy as np
import ml_dtypes
from contextlib import ExitStack
import concourse.bass as bass
import concourse.mybir as mybir
from concourse.bass_utils import run_bass_kernel_spmd

F32 = mybir.dt.float32
BF16 = mybir.dt.bfloat16
AF = mybir.ActivationFunctionType
ALU = mybir.AluOpType

D = 2048
SEQ = 4096
NB = 8
DEPTH = 2
DFF = 5632
T = 512
NCH = D // 128
NFF = DFF // 128
EPS = 1e-6
NCHUNK = T // 8

ENGS = ["pe", "act", "dve", "pool", "sp"]


class Buf:
    __slots__ = ("name", "last_w", "readers", "excl")

    def __init__(self, name, excl=False):
        self.name = name
        self.last_w = None
        self.readers = {}
        self.excl = excl


class Op:
    __slots__ = ("eng", "fn", "deps", "idx", "signal", "count", "is_dma", "dsem", "dval", "prev_dma")

    def __init__(self, eng, fn, is_dma):
        self.eng = eng
        self.fn = fn
        self.deps = []
        self.signal = False
        self.count = 0
        self.is_dma = is_dma
        self.dsem = None
        self.dval = 0
        self.prev_dma = None


class Prog:
    NDMA = 12

    def __init__(self):
        self.ops = {e: [] for e in ENGS}
        self.ndma = {e: 0 for e in ENGS}
        self.dma_hist = {e: [] for e in ENGS}

    def op(self, eng, fn, reads=(), writes=(), dma=False):
        o = Op(eng, fn, dma)
        o.idx = len(self.ops[eng])
        deps = {}

        def add(d):
            if d is o:
                return
            if d.is_dma:
                deps[("dma", id(d))] = d
            else:
                k = ("eng", d.eng)
                if k not in deps or deps[k].idx < d.idx:
                    deps[k] = d

        for b in reads:
            if b.last_w is not None:
                add(b.last_w)
            if b.excl:
                for r in b.readers.values():
                    if r.eng != eng:
                        add(r)
        for b in writes:
            if b.last_w is not None:
                add(b.last_w)
            for r in b.readers.values():
                add(r)
        for d in deps.values():
            if (not d.is_dma) and d.eng == eng and eng == "pe":
                continue
            o.deps.append(d)
            if not d.is_dma:
                d.signal = True
        for b in reads:
            if dma:
                b.readers[("dma", id(o))] = o
            else:
                b.readers[("eng", eng)] = o
        for b in writes:
            b.last_w = o
            b.readers = {}
        if dma:
            n = self.ndma[eng]
            self.ndma[eng] += 1
            o.dsem = (eng, n % self.NDMA)
            o.dval = 16 * (n // self.NDMA + 1)
            if n >= self.NDMA:
                o.prev_dma = self.dma_hist[eng][n - self.NDMA]
            self.dma_hist[eng].append(o)
        self.ops[eng].append(o)
        return o

    def emit(self, nc, es):
        engsem = {e: es.enter_context(nc.semaphore("s_" + e)) for e in ENGS}
        dsem = {}
        for e in ENGS:
            for i in range(min(self.NDMA, self.ndma[e])):
                dsem[(e, i)] = es.enter_context(nc.semaphore("d_%s_%d" % (e, i)))
        for e in ENGS:
            c = 0
            for o in self.ops[e]:
                if o.signal and not o.is_dma:
                    c += 1
                    o.count = c
        block = es.enter_context(nc.Block())
        prog = self

        def run(e, eh):
            waited = {}
            for o in prog.ops[e]:
                wl = []
                for d in o.deps:
                    if d.is_dma:
                        wl.append((("d",) + d.dsem, dsem[d.dsem], d.dval))
                    else:
                        wl.append((("e", d.eng), engsem[d.eng], d.count))
                if o.prev_dma is not None:
                    d = o.prev_dma
                    wl.append((("d",) + d.dsem, dsem[d.dsem], d.dval))
                for k, s, v in wl:
                    if waited.get(k, 0) >= v:
                        continue
                    waited[k] = v
                    eh.wait_ge(s, v)
                inst = o.fn(eh)
                if o.is_dma:
                    inst.then_inc(dsem[o.dsem], 16)
                elif o.signal:
                    inst.then_inc(engsem[e], 1)
            for (q, i), s in dsem.items():
                if q == e and prog.ndma[e] > 0:
                    last = [d for d in prog.dma_hist[e] if d.dsem == (q, i)][-1]
                    if waited.get(("d", q, i), 0) < last.dval:
                        eh.wait_ge(s, last.dval)

        @block.tensor
        def _(eh):
            run("pe", eh)

        @block.scalar
        def _(eh):
            run("act", eh)

        @block.vector
        def _(eh):
            run("dve", eh)

        @block.gpsimd
        def _(eh):
            run("pool", eh)

        @block.sync
        def _(eh):
            run("sp", eh)


def _tile_w(w, cols_list):
    K = w.shape[0]
    KC = K // 128
    wz = np.concatenate([w, np.zeros((K, 1), w.dtype)], axis=1)
    out = np.empty((len(cols_list), 128, KC * len(cols_list[0])), np.float32)
    for m, cols in enumerate(cols_list):
        sub = wz[:, cols]
        C = sub.shape[1]
        out[m] = sub.reshape(KC, 128, C).transpose(1, 0, 2).reshape(128, KC * C)
    return out


def _fm(v, nch):
    return np.ascontiguousarray(v.reshape(nch, 128).T)


IN_TILES = 20


def _in_cols():
    cols = []
    for qt in range(8):
        cols.append(list(range(qt * 128, qt * 128 + 128)))
    k0 = list(range(1024, 1088))
    k1 = list(range(1088, 1152))
    z = [-1] * 64
    cols += [k0 + z, z + k0, k1 + z, z + k1]
    for ct in range(8):
        cols.append(list(range(1280 + ct * 128, 1280 + ct * 128 + 128)))
    return cols


def _v_cols():
    v0 = list(range(1152, 1216))
    v1 = list(range(1216, 1280))
    z = [-1] * 64
    return [v0 + z + z + v0 + v1 + z + z + v1]


SP_GPRE, SP_GPOST, SP_GPREF, SP_GPOSTF = 0, 16, 32, 48
SP_BADA = 64
SP_GATT, SP_GSSM = 160, 168
SP_CW = 176
SP_CB = 440
SP_SINK = 528
SP_L = 536
SP_C = 2 * SP_L
SP_ID = SP_C + 16
SP_HM = SP_ID + 128
SP_TOT = SP_HM + 2

SS_LR, SS_LI, SS_LS = 0, 32, 64
SS_BR, SS_BI, SS_CR, SS_CI = 96, 608, 1120, 1632
SS_D = 2144
SS_L = 2208

CB_ID = 0
CB_SEL = 128
CB_MP = CB_SEL + 8 * 352
CB_MC = CB_MP + 128
CB_ONE = CB_MC + 128
CB_OP0 = CB_ONE + 128
CB_OP1 = CB_OP0 + 128
CB_TOT = CB_OP1 + 128


def prep_shared(inp):
    sh = {}
    f32 = np.float32
    small = np.zeros((128, SP_TOT), f32)
    ssmp = np.zeros((128, 2 * SS_L), f32)
    for l in range(DEPTH):
        o = l * SP_L
        small[:, o + SP_GPRE:o + SP_GPRE + 16] = _fm(inp["g_pre_mix"][l], 16)
        small[:, o + SP_GPOST:o + SP_GPOST + 16] = _fm(inp["g_post_mix"][l], 16)
        small[:, o + SP_GPREF:o + SP_GPREF + 16] = _fm(inp["g_pre_ffn"][l], 16)
        small[:, o + SP_GPOSTF:o + SP_GPOSTF + 16] = _fm(inp["g_post_ffn"][l], 16)
        small[:, o + SP_BADA:o + SP_BADA + 96] = _fm(inp["b_ada"][l], 96)
        small[:, o + SP_GATT:o + SP_GATT + 8] = _fm(inp["g_attn_out"][l], 8)
        small[:, o + SP_GSSM:o + SP_GSSM + 8] = _fm(inp["g_ssm_out"][l], 8)
        cw = inp["conv_w"][l]
        small[:, o + SP_CW:o + SP_CW + 264] = cw.reshape(3, 88, 128).transpose(2, 1, 0).reshape(128, 264)
        small[:, o + SP_CB:o + SP_CB + 88] = _fm(inp["conv_b"][l], 88)
        sk = inp["attn_sinks"][l]
        small[:, o + SP_SINK:o + SP_SINK + 8] = sk.reshape(8, 2).T[np.arange(128) // 64]
        so = l * SS_L

        def pairlay(a):
            return a.reshape(32, 2, 64).transpose(1, 2, 0).reshape(128, 32)

        ssmp[:, so + SS_LR:so + SS_LR + 32] = pairlay(inp["lam_re"][l])
        ssmp[:, so + SS_LI:so + SS_LI + 32] = pairlay(inp["lam_im"][l])
        ssmp[:, so + SS_LS:so + SS_LS + 32] = pairlay(np.repeat(inp["log_step"][l][:, None], 64, axis=1))

        def pairlay3(a):
            return a.reshape(32, 2, 64, 16).transpose(1, 2, 0, 3).reshape(128, 512)

        ssmp[:, so + SS_BR:so + SS_BR + 512] = pairlay3(inp["ssm_b_re"][l])
        ssmp[:, so + SS_BI:so + SS_BI + 512] = pairlay3(inp["ssm_b_im"][l])
        ssmp[:, so + SS_CR:so + SS_CR + 512] = pairlay3(inp["ssm_c_re"][l].transpose(0, 2, 1))
        ssmp[:, so + SS_CI:so + SS_CI + 512] = pairlay3(inp["ssm_c_im"][l].transpose(0, 2, 1))
        dd = inp["ssm_d"][l]
        ssmp[:, so + SS_D:so + SS_D + 64] = np.tile(dd.T, (8, 1))
    small[:, SP_ID:SP_ID + 128] = np.eye(128, dtype=f32)
    small[:64, SP_HM] = 1.0
    small[64:, SP_HM + 1] = 1.0
    sh["small"] = small
    sh["ssmp"] = ssmp
    cb = np.zeros((128, CB_TOT), f32)
    cb[:, CB_ID:CB_ID + 128] = np.eye(128)
    k = np.arange(128)
    for g in range(8):
        rows = k[(k // 16) == g]
        cb[rows, CB_SEL + g * 352 + 112 + rows] = 1.0
    kk = k[:, None]
    qq = k[None, :]
    NEG = -30000.0
    cb[:, CB_MP:CB_MP + 128] = np.where(kk > qq, 0.0, NEG)
    cb[:, CB_MC:CB_MC + 128] = np.where(kk <= qq, 0.0, NEG)
    cb[:, CB_ONE:CB_ONE + 128] = 1.0
    cb[:, CB_OP0:CB_OP0 + 64] = 1.0
    cb[:, CB_OP1 + 64:CB_OP1 + 128] = 1.0
    sh["cbf"] = cb.astype(ml_dtypes.bfloat16)
    full = [list(range(m * 128, m * 128 + 128)) for m in range(96)]
    for l in range(DEPTH):
        sh["wada%d" % l] = _tile_w(inp["w_ada"][l], full)
        sh["win%d" % l] = _tile_w(inp["w_in"][l], _in_cols())
        sh["wv%d" % l] = _tile_w(inp["w_in"][l], _v_cols())
        sh["wglu%d" % l] = _tile_w(inp["w_glu"][l], full[:8])
        sh["wout%d" % l] = _tile_w(inp["w_out"][l], full[:16])
        sh["wup%d" % l] = _tile_w(inp["w_up"][l], full[:88])
        sh["wdn%d" % l] = _tile_w(inp["w_down"][l], full[:16])
    return sh


WSHAPES = {"win": (IN_TILES, 2048), "wv": (1, 8192), "wglu": (8, 1024), "wout": (16, 2048),
           "wup": (88, 2048), "wdn": (16, 5632)}


def build_nc(n_tiles=SEQ // T, n_layers=DEPTH, do_attn=True, do_ssm=True, do_ffn=True, dbg=False):
    nc = bass.Bass("TRN2", target_bir_lowering=False)
    P = Prog()
    es = ExitStack()

    def din(name, shape, dt=F32):
        return nc.dram_tensor(name, list(shape), dt, kind="ExternalInput").ap()

    x_in = din("x", [SEQ, D])
    small_d = din("small", [128, SP_TOT])
    ssmp_d = din("ssmp", [128, 2 * SS_L]) if do_ssm else None
    cbf_d = din("cbf", [128, CB_TOT], BF16)
    wada_d = [din("wada%d" % l, [96, 128, 2048]) for l in range(n_layers)]
    wsrc = {}
    wscr = {}
    for l in range(n_layers):
        for k, (M, C) in WSHAPES.items():
            wsrc[(k, l)] = din("%s%d" % (k, l), [M, 128, C])
            wscr[(k, l)] = nc.dram_tensor("s_%s%d" % (k, l), [M, 128, C], BF16, kind="Internal").ap()
        wscr[("ssm", l)] = nc.dram_tensor("s_ssm%d" % l, [8, 128, 5120], BF16, kind="Internal").ap()
    y_out = nc.dram_tensor("y", [SEQ, D], F32, kind="ExternalOutput").ap()

    def sb(name, shape, dt=F32):
        return es.enter_context(nc.sbuf_tensor("sb_" + name, list(shape), dt))

    small = sb("small", [128, SP_TOT])
    cbf = sb("cbf", [128, CB_TOT], BF16)
    ada = sb("ada", [128, DEPTH, 96])
    vec = sb("vec", [128, DEPTH, 4, 16])
    cact = sb("cact", [128, 16])
    epsb = sb("epsb", [128, 1])
    xT = sb("xT", [128, NCH, T])
    hT = sb("hT", [128, NCH, T], BF16)
    mixb = sb("mixb", [128, NCH, T], BF16)
    Rt = [sb("R%d" % i, [128, T]) for i in range(3)]
    NTMP = 5
    tmpf = [sb("tmpf%d" % i, [128, T]) for i in range(NTMP)]
    NSQ = 3
    sqb = [sb("sqb%d" % i, [128, T], BF16) for i in range(NSQ)]
    sxx = sb("sxx", [128, T])
    B_sxx = Buf("sxx")
    NSLOT = 6
    wslot = [sb("wslot%d" % i, [128, 2048], BF16) for i in range(NSLOT)]
    stage = sb("stage", [128, D])
    convc = [sb("convc%d" % l, [128, 88, 2]) for l in range(DEPTH)]
    hid = sb("hid", [128, NFF, T], BF16)
    bank = [es.enter_context(nc.psum_tensor("bank%d" % i, [128, 512], F32)) for i in range(8)]

    B_small = Buf("small"); B_cbf = Buf("cbf"); B_ada = Buf("ada"); B_vec = Buf("vec")
    B_cact = Buf("cact"); B_eps = Buf("eps"); B_xT = [Buf("xT%d" % c) for c in range(NCH)]
    B_hT = [Buf("hT%d" % c) for c in range(NCH)]; B_mix = [Buf("mix%d" % c) for c in range(NCH)]
    B_R = [Buf("R%d" % i) for i in range(3)]; B_tmp = [Buf("tmp%d" % i) for i in range(NTMP)]
    B_sq = [Buf("sq%d" % i) for i in range(3)]; B_ws = [Buf("ws%d" % i) for i in range(NSLOT)]
    B_stage = Buf("stage"); B_convc = [Buf("convc%d" % l) for l in range(DEPTH)]
    B_bank = [Buf("bank%d" % i, excl=True) for i in range(8)]
    B_scr = {k: Buf("scr_%s%d" % k) for k in wscr}

    ident_f = small[:, SP_ID:SP_ID + 128]
    ident_b = cbf[:, CB_ID:CB_ID + 128]
    ones_b = cbf[:, CB_ONE:CB_ONE + 128]

    ctr = {"tmp": 0, "sq": 0, "ws": 0, "bank": 0, "tbank": 0}

    def nxt(k, n):
        v = ctr[k]
        ctr[k] = (v + 1) % n
        return v

    P.op("sp", lambda e: e.dma_start(out=small[:], in_=small_d[:, :]), writes=[B_small], dma=True)
    P.op("sp", lambda e: e.dma_start(out=cbf[:], in_=cbf_d[:, :]), writes=[B_cbf], dma=True)
    P.op("dve", lambda e: e.memset(epsb[:], EPS), writes=[B_eps])
    P.op("act", lambda e: e.activation(out=cact[:], in_=small[:, SP_C:SP_C + 16], func=AF.Silu),
         reads=[B_small], writes=[B_cact])

    adast = [xT[:, 4 * i:4 * i + 4, :].rearrange("p a b -> p (a b)") for i in range(3)]
    B_adast = [B_xT[4 * i:4 * i + 4] for i in range(3)]
    cactb = sb("cactb", [128, 16], BF16)
    P.op("dve", lambda e: e.tensor_copy(out=cactb[:], in_=cact[:]), reads=[B_cact], writes=[B_cact])
    adab = [hT[:, 4 * i:4 * i + 4, :].rearrange("p a b -> p (a b)") for i in range(3)]
    B_adab = [B_hT[4 * i:4 * i + 4] for i in range(3)]
    for l in range(n_layers):
        for m in range(96):
            s = m % 3
            P.op("sp", lambda e, s=s, l=l, m=m: e.dma_start(out=adast[s], in_=wada_d[l][m]),
                 writes=B_adast[s], dma=True)
            if m % 2 == 0:
                P.op("act", lambda e, s=s: e.activation(out=adab[s], in_=adast[s], func=AF.Copy),
                     reads=B_adast[s], writes=B_adab[s])
            else:
                P.op("dve", lambda e, s=s: e.tensor_copy(out=adab[s], in_=adast[s]),
                     reads=B_adast[s], writes=B_adab[s])
            bk = 5 + (m % 2)
            for kc in range(16):
                P.op("pe", lambda e, s=s, kc=kc, bk=bk: e.matmul(
                    bank[bk][:, 0:1], lhsT=adab[s][:, kc * 128:(kc + 1) * 128], rhs=cactb[:, kc:kc + 1],
                    start=(kc == 0), stop=(kc == 15)),
                    reads=B_adab[s] + [B_cact], writes=[B_bank[bk]])
            P.op("dve", lambda e, l=l, m=m, bk=bk: e.tensor_tensor(
                out=ada[:, l, m:m + 1], in0=bank[bk][:, 0:1],
                in1=small[:, l * SP_L + SP_BADA + m:l * SP_L + SP_BADA + m + 1], op=ALU.add),
                reads=[B_bank[bk], B_small], writes=[B_ada])
        o = l * SP_L
        for j, (a0, g0) in enumerate([(16, SP_GPRE), (32, SP_GPOST), (64, SP_GPREF), (80, SP_GPOSTF)]):
            P.op("dve", lambda e, l=l, j=j, a0=a0, g0=g0, o=o: e.scalar_tensor_tensor(
                out=vec[:, l, j, :], in0=ada[:, l, a0:a0 + 16], scalar=1.0,
                in1=small[:, o + g0:o + g0 + 16], op0=ALU.add, op1=ALU.mult),
                reads=[B_ada, B_small], writes=[B_vec])

    def gs1(l, c): return vec[:, l, 0, c:c + 1]
    def gg1(l, c): return vec[:, l, 1, c:c + 1]
    def gs2(l, c): return vec[:, l, 2, c:c + 1]
    def gg2(l, c): return vec[:, l, 3, c:c + 1]
    def sh_m(l, c): return ada[:, l, 0 + c:c + 1]
    def sh_f(l, c): return ada[:, l, 48 + c:48 + c + 1]

    pre_f = adast
    pre_b = [hT[:, 4 * i:4 * i + 4, :].rearrange("p a b -> p (a b)") for i in range(3)]
    B_pref = B_adast
    B_preb = [B_hT[4 * i:4 * i + 4] for i in range(3)]
    pc = 0
    wlist = ["win", "wv", "wglu", "wout", "wup", "wdn"]
    if not do_ffn:
        wlist = ["win", "wv", "wglu", "wout"]
    for l in range(n_layers):
        for k in wlist:
            M, C = WSHAPES[k]
            for m in range(M):
                for c0 in range(0, C, 2048):
                    cw = min(2048, C - c0)
                    s = pc % 3
                    P.op("sp", lambda e, s=s, k=k, l=l, m=m, c0=c0, cw=cw: e.dma_start(
                        out=pre_f[s][:, :cw], in_=wsrc[(k, l)][m][:, c0:c0 + cw]),
                        writes=B_pref[s], dma=True)
                    ce = "act" if pc % 2 == 0 else "dve"
                    if ce == "act":
                        P.op("act", lambda e, s=s, cw=cw: e.activation(out=pre_b[s][:, :cw], in_=pre_f[s][:, :cw],
                                                                       func=AF.Copy),
                             reads=B_pref[s], writes=B_preb[s])
                    else:
                        P.op("dve", lambda e, s=s, cw=cw: e.tensor_copy(out=pre_b[s][:, :cw], in_=pre_f[s][:, :cw]),
                             reads=B_pref[s], writes=B_preb[s])
                    P.op("pool", lambda e, s=s, k=k, l=l, m=m, c0=c0, cw=cw: e.dma_start(
                        out=wscr[(k, l)][m][:, c0:c0 + cw], in_=pre_b[s][:, :cw]),
                        reads=B_preb[s], writes=[B_scr[(k, l)]], dma=True)
                    pc += 1


    import math
    hidb = hid[:, :, :].rearrange("p a b -> p (a b)")
    hf = hid[:, :, :].bitcast(F32).rearrange("p a b -> p (a b)")
    hTb = hT[:, :, :].rearrange("p a b -> p (a b)")
    A8 = [sb("A8_%d" % l, [128, 3, 32]) for l in range(DEPTH)]
    XSc = [sb("XSc%d" % l, [128, 2, 32]) for l in range(DEPTH)]
    B_A8 = [Buf("A8_%d" % l) for l in range(DEPTH)]
    B_XSc = [Buf("XSc%d" % l) for l in range(DEPTH)]
    B_prm = Buf("g_prm"); B_sc = Buf("g_sc"); B_BB = Buf("g_B"); B_bcf = Buf("g_bcf"); B_ccf = Buf("g_ccf")
    B_gt = Buf("g_t"); B_bctb = Buf("g_bctb"); B_cm0 = Buf("g_cm0")
    GEN_BUFS = [B_prm, B_sc, B_BB, B_bcf, B_ccf, B_gt, B_bctb, B_cm0]
    if do_ssm:
        prm = hf[:, 0:SS_L]
        def SC(i): return hf[:, 2208 + i * 32:2208 + (i + 1) * 32]
        I_DT, I_E1, I_MAG, I_ANG, I_KF, I_TMP, I_SARG, I_CARG, I_SIN, I_COS, I_AR, I_AI, I_DEN, I_RDEN, I_ARM1, I_FRE, I_FIM, I_T1, I_T2 = range(19)
        def PR(k): return SC(19 + k)
        def PI(k): return SC(28 + k)
        Bre = hf[:, 3392:3904]; Bim = hf[:, 3904:4416]
        BcF = hf[:, 4416:6336]
        CcF = hf[:, 6336:7360]
        G1 = hf[:, 7360:7424]; G2 = hf[:, 7424:7488]
        BcTb = hidb[:, 17100:19020]
        Cm0Z = hidb[:, 19020:21068]
        CcS = hTb[:, 0:2048]; KtS = hTb[:, 2048:3072]; BcS = hTb[:, 4096:6144]
        B_ccs = B_hT[0:4]; B_kts = B_hT[4:6]; B_bcs = B_hT[8:12]
        TWO_PI = 2.0 * math.pi

        def dv(fn, reads, writes):
            P.op("dve", fn, reads=reads, writes=writes)

        def tt(out, a, b, op, reads=(B_sc,), writes=(B_sc,)):
            dv(lambda e: e.tensor_tensor(out=out, in0=a, in1=b, op=op), list(reads), list(writes))

        def range_reduce(src_i, dst_i, shift):
            dv(lambda e: e.tensor_scalar(out=SC(dst_i), in0=SC(src_i), scalar1=shift, scalar2=None, op0=ALU.add),
               [B_sc], [B_sc])
            dv(lambda e: e.tensor_copy(out=SC(I_T1), in_=SC(dst_i)), [B_sc], [B_sc])
            for j in range(1, 12):
                thr = (2 * j - 1) * math.pi
                dv(lambda e, thr=thr: e.tensor_scalar(out=SC(I_TMP), in0=SC(I_T1), scalar1=thr, scalar2=-TWO_PI,
                                                      op0=ALU.is_gt, op1=ALU.mult), [B_sc], [B_sc])
                tt(SC(dst_i), SC(dst_i), SC(I_TMP), ALU.add)
            dv(lambda e: e.tensor_scalar(out=SC(dst_i), in0=SC(dst_i), scalar1=3.1415925, scalar2=-3.1415925,
                                         op0=ALU.min, op1=ALU.max), [B_sc], [B_sc])

        for l in range(n_layers):
            so = l * SS_L
            P.op("sp", lambda e, so=so: e.dma_start(out=prm, in_=ssmp_d[:, so:so + SS_L]), writes=[B_prm], dma=True)
            LR = prm[:, SS_LR:SS_LR + 32]; LI = prm[:, SS_LI:SS_LI + 32]; LS = prm[:, SS_LS:SS_LS + 32]
            P.op("act", lambda e, LS=LS: e.activation(out=SC(I_DT), in_=LS, func=AF.Exp), reads=[B_prm], writes=[B_sc])
            tt(SC(I_E1), LR, SC(I_DT), ALU.mult, reads=(B_prm, B_sc))
            P.op("act", lambda e: e.activation(out=SC(I_MAG), in_=SC(I_E1), func=AF.Exp), reads=[B_sc], writes=[B_sc])
            tt(SC(I_ANG), LI, SC(I_DT), ALU.mult, reads=(B_prm, B_sc))
            range_reduce(I_ANG, I_SARG, 0.0)
            range_reduce(I_ANG, I_CARG, 0.5 * math.pi)
            P.op("act", lambda e: e.activation(out=SC(I_SIN), in_=SC(I_SARG), func=AF.Sin), reads=[B_sc], writes=[B_sc])
            P.op("act", lambda e: e.activation(out=SC(I_COS), in_=SC(I_CARG), func=AF.Sin), reads=[B_sc], writes=[B_sc])
            tt(SC(I_AR), SC(I_MAG), SC(I_COS), ALU.mult)
            tt(SC(I_AI), SC(I_MAG), SC(I_SIN), ALU.mult)
            tt(SC(I_DEN), LR, LR, ALU.mult, reads=(B_prm, B_sc))
            tt(SC(I_T1), LI, LI, ALU.mult, reads=(B_prm, B_sc))
            tt(SC(I_DEN), SC(I_DEN), SC(I_T1), ALU.add)
            dv(lambda e: e.reciprocal(out=SC(I_RDEN), in_=SC(I_DEN)), [B_sc], [B_sc])
            dv(lambda e: e.tensor_scalar(out=SC(I_ARM1), in0=SC(I_AR), scalar1=-1.0, scalar2=None, op0=ALU.add),
               [B_sc], [B_sc])
            tt(SC(I_T1), SC(I_ARM1), LR, ALU.mult, reads=(B_prm, B_sc))
            tt(SC(I_T2), SC(I_AI), LI, ALU.mult, reads=(B_prm, B_sc))
            tt(SC(I_T1), SC(I_T1), SC(I_T2), ALU.add)
            tt(SC(I_FRE), SC(I_T1), SC(I_RDEN), ALU.mult)
            tt(SC(I_T1), SC(I_AI), LR, ALU.mult, reads=(B_prm, B_sc))
            tt(SC(I_T2), SC(I_ARM1), LI, ALU.mult, reads=(B_prm, B_sc))
            tt(SC(I_T1), SC(I_T1), SC(I_T2), ALU.subtract)
            tt(SC(I_FIM), SC(I_T1), SC(I_RDEN), ALU.mult)
            dv(lambda e: e.memset(PR(0), 1.0), [], [B_sc])
            dv(lambda e: e.memset(PI(0), 0.0), [], [B_sc])
            for k in range(1, 9):
                tt(SC(I_T1), PR(k - 1), SC(I_AR), ALU.mult)
                tt(SC(I_T2), PI(k - 1), SC(I_AI), ALU.mult)
                tt(PR(k), SC(I_T1), SC(I_T2), ALU.subtract)
                tt(SC(I_T1), PR(k - 1), SC(I_AI), ALU.mult)
                tt(SC(I_T2), PI(k - 1), SC(I_AR), ALU.mult)
                tt(PI(k), SC(I_T1), SC(I_T2), ALU.add)
            dv(lambda e, l=l: e.tensor_copy(out=A8[l][:, 0, :], in_=PR(8)), [B_sc], [B_A8[l]])
            dv(lambda e, l=l: e.tensor_copy(out=A8[l][:, 1, :], in_=PI(8)), [B_sc], [B_A8[l]])
            dv(lambda e, l=l: e.tensor_scalar(out=A8[l][:, 2, :], in0=PI(8), scalar1=-1.0, scalar2=None, op0=ALU.mult),
               [B_sc], [B_A8[l]])
            dv(lambda e, l=l: e.memset(XSc[l][:], 0.0), [], [B_XSc[l]])
            def b3(ap): return ap.rearrange("p (a b) -> p a b", b=16)
            def bc(i, p0=0, n=32): return SC(i)[:, p0:p0 + n].unsqueeze(2).to_broadcast([128, n, 16])
            BR = b3(prm[:, SS_BR:SS_BR + 512]); BI = b3(prm[:, SS_BI:SS_BI + 512])
            CR = b3(prm[:, SS_CR:SS_CR + 512]); CI = b3(prm[:, SS_CI:SS_CI + 512])
            G512a = hf[:, 7488:8000]; G512b = hf[:, 8000:8512]
            tt(b3(G512a), bc(I_FRE), BR, ALU.mult, reads=(B_prm, B_sc), writes=(B_gt,))
            tt(b3(G512b), bc(I_FIM), BI, ALU.mult, reads=(B_prm, B_sc), writes=(B_gt,))
            tt(b3(Bre), b3(G512a), b3(G512b), ALU.subtract, reads=(B_gt,), writes=(B_BB,))
            tt(b3(G512a), bc(I_FRE), BI, ALU.mult, reads=(B_prm, B_sc), writes=(B_gt,))
            tt(b3(G512b), bc(I_FIM), BR, ALU.mult, reads=(B_prm, B_sc), writes=(B_gt,))
            tt(b3(Bim), b3(G512a), b3(G512b), ALU.add, reads=(B_gt,), writes=(B_BB,))
            cm4 = Cm0Z.rearrange("p (a r g h) -> p a r g h", r=2, g=2, h=16)
            for g2 in range(2):
                hm = small[:, SP_HM + g2:SP_HM + g2 + 1]
                dv(lambda e, g2=g2, hm=hm, CR=CR: e.tensor_scalar(out=cm4[:, :, 0, g2, :], in0=CR, scalar1=hm, scalar2=None,
                                                                   op0=ALU.mult), [B_prm, B_small], [B_cm0])
                dv(lambda e, g2=g2, hm=hm, CI=CI: e.tensor_scalar(out=cm4[:, :, 1, g2, :], in0=CI, scalar1=hm, scalar2=-1.0,
                                                                   op0=ALU.mult, op1=ALU.mult), [B_prm, B_small], [B_cm0])
            bcf5 = BcF.rearrange("p (a r j h) -> p a r j h", r=2, j=15, h=16)
            ccf5 = CcF.rearrange("p (a r t h) -> p a r t h", r=2, t=8, h=16)
            g1 = G1.rearrange("p (a h) -> p a h", h=16); g2t = G2.rearrange("p (a h) -> p a h", h=16)
            bctb4 = BcTb.rearrange("p (a r x) -> p a r x", r=2, x=240)
            bcs4 = BcS.rearrange("p (g r x) -> p g r x", r=2, x=128)
            kts3 = KtS.rearrange("p (g x) -> p g x", x=128)
            for ct in range(8):
                p0 = ct * 4
                bre4 = b3(Bre)[:, p0:p0 + 4, :]; bim4 = b3(Bim)[:, p0:p0 + 4, :]
                cre4 = CR[:, p0:p0 + 4, :]; cim4 = CI[:, p0:p0 + 4, :]
                dv(lambda e: e.memset(BcF, 0.0), [], [B_bcf])
                for j in range(8):
                    k = 7 - j
                    prk = PR(k)[:, p0:p0 + 4].unsqueeze(2).to_broadcast([128, 4, 16])
                    pik = PI(k)[:, p0:p0 + 4].unsqueeze(2).to_broadcast([128, 4, 16])
                    tt(g1, prk, bre4, ALU.mult, reads=(B_sc, B_BB), writes=(B_gt,))
                    tt(g2t, pik, bim4, ALU.mult, reads=(B_sc, B_BB), writes=(B_gt,))
                    tt(bcf5[:, :, 0, j, :], g1, g2t, ALU.subtract, reads=(B_gt,), writes=(B_bcf,))
                    tt(g1, prk, bim4, ALU.mult, reads=(B_sc, B_BB), writes=(B_gt,))
                    tt(g2t, pik, bre4, ALU.mult, reads=(B_sc, B_BB), writes=(B_gt,))
                    tt(bcf5[:, :, 1, j, :], g1, g2t, ALU.add, reads=(B_gt,), writes=(B_bcf,))
                dv(lambda e: e.tensor_copy(out=BcTb, in_=BcF), [B_bcf], [B_bctb])
                for t in range(8):
                    prk = PR(t + 1)[:, p0:p0 + 4].unsqueeze(2).to_broadcast([128, 4, 16])
                    pik = PI(t + 1)[:, p0:p0 + 4].unsqueeze(2).to_broadcast([128, 4, 16])
                    tt(g1, prk, cre4, ALU.mult, reads=(B_sc, B_prm), writes=(B_gt,))
                    tt(g2t, pik, cim4, ALU.mult, reads=(B_sc, B_prm), writes=(B_gt,))
                    tt(ccf5[:, :, 0, t, :], g1, g2t, ALU.subtract, reads=(B_gt,), writes=(B_ccf,))
                    tt(g1, pik, cre4, ALU.mult, reads=(B_sc, B_prm), writes=(B_gt,))
                    tt(g2t, prk, cim4, ALU.mult, reads=(B_sc, B_prm), writes=(B_gt,))
                    dv(lambda e, t=t: e.scalar_tensor_tensor(out=ccf5[:, :, 1, t, :], in0=g1, scalar=-1.0, in1=g2t,
                                                             op0=ALU.mult, op1=ALU.subtract), [B_gt], [B_ccf])
                ccs_v = CcS.rearrange("p (a g x) -> p a g x", g=2, x=256)
                for g2 in range(2):
                    hm = small[:, SP_HM + g2:SP_HM + g2 + 1]
                    dv(lambda e, g2=g2, hm=hm: e.tensor_scalar(out=ccs_v[:, :, g2, :], in0=CcF.rearrange("p (a x) -> p a x", x=256),
                                                               scalar1=hm, scalar2=None, op0=ALU.mult),
                       [B_ccf, B_small], B_ccs)
                dv(lambda e: e.memset(BcS, 0.0), [], B_bcs)
                for pl in range(4):
                    pair = p0 + pl
                    bk = nxt("bank", 4)
                    for t in range(8):
                        for ri in range(2):
                            P.op("pe", lambda e, pl=pl, t=t, ri=ri, bk=bk, pair=pair: e.matmul(
                                bank[bk][:, t * 32:(t + 1) * 32],
                                lhsT=bctb4[:, pl, ri, (7 - t) * 16:(7 - t) * 16 + 128],
                                rhs=Cm0Z[:, (pair * 2 + ri) * 32:(pair * 2 + ri + 1) * 32],
                                start=(ri == 0), stop=(ri == 1)),
                                reads=[B_bctb, B_cm0], writes=[B_bank[bk]])
                    kview = bank[bk][:, 0:256].rearrange("p (t g h) -> p g t h", g=2, h=16)
                    idv = ident_f.rearrange("p (t h) -> p t h", h=16)
                    for g2 in range(2):
                        g = pair * 2 + g2
                        dcol = prm[:, SS_D + g:SS_D + g + 1]
                        dv(lambda e, g2=g2, pl=pl, dcol=dcol, kview=kview, idv=idv: e.scalar_tensor_tensor(
                            out=kts3[:, pl * 2 + g2, :].rearrange("p (t h) -> p t h", h=16), in0=idv, scalar=dcol,
                            in1=kview[:, g2, :, :], op0=ALU.mult, op1=ALU.add),
                            [B_bank[bk], B_prm, B_small], B_kts)
                    for ri in range(2):
                        bk2 = nxt("bank", 4)
                        P.op("pe", lambda e, pl=pl, ri=ri, bk2=bk2: e.matmul(
                            bank[bk2][:, 0:128], lhsT=bctb4[:, pl, ri, 0:128], rhs=ident_b, start=True, stop=True),
                            reads=[B_bctb, B_cbf], writes=[B_bank[bk2]])
                        for g2 in range(2):
                            dv(lambda e, pl=pl, ri=ri, g2=g2, bk2=bk2: e.tensor_copy(
                                out=bcs4[:, pl * 2 + g2, ri, g2 * 64:(g2 + 1) * 64], in_=bank[bk2][:, g2 * 64:(g2 + 1) * 64]),
                                [B_bank[bk2]], B_bcs)
                P.op("pool", lambda e, l=l, ct=ct: e.dma_start(out=wscr[("ssm", l)][ct][:, 0:2048], in_=BcS),
                     reads=B_bcs, writes=[B_scr[("ssm", l)]], dma=True)
                P.op("pool", lambda e, l=l, ct=ct: e.dma_start(out=wscr[("ssm", l)][ct][:, 2048:4096], in_=CcS),
                     reads=B_ccs, writes=[B_scr[("ssm", l)]], dma=True)
                P.op("pool", lambda e, l=l, ct=ct: e.dma_start(out=wscr[("ssm", l)][ct][:, 4096:5120], in_=KtS),
                     reads=B_kts, writes=[B_scr[("ssm", l)]], dma=True)

    def wload(k, l, m, c0, cw):
        s = nxt("ws", NSLOT)
        P.op("sp", lambda e: e.dma_start(out=wslot[s][:, :cw], in_=wscr[(k, l)][m][:, c0:c0 + cw]),
             reads=[B_scr[(k, l)]], writes=[B_ws[s]], dma=True)
        return s

    def rms_begin():
        return {"n": 0}

    def rms_add(st, src_ap, src_bufs, total):
        q = nxt("sq", 3)
        P.op("act", lambda e: e.activation(out=sqb[q][:], in_=src_ap, func=AF.Square),
             reads=src_bufs, writes=[B_sq[q]])
        n = st["n"]
        P.op("pe", lambda e: e.matmul(bank[4][:, :], lhsT=ones_b, rhs=sqb[q][:], start=(n == 0), stop=(n == total - 1)),
             reads=[B_sq[q], B_cbf], writes=[B_bank[4]])
        st["n"] = n + 1

    def rms_finish(ri, F):
        t = nxt("tmp", NTMP)
        P.op("act", lambda e: e.activation(out=tmpf[t][:], in_=bank[4][:, :], func=AF.Sqrt, bias=epsb[:, 0:1],
                                           scale=1.0 / F),
             reads=[B_bank[4], B_eps], writes=[B_tmp[t]])
        P.op("dve", lambda e: e.reciprocal(out=Rt[ri][:], in_=tmpf[t][:]), reads=[B_tmp[t]], writes=[B_R[ri]])

    def pre_norm(l, gsf, shf):
        st = rms_begin()
        for c in range(NCH):
            rms_add(st, xT[:, c, :], [B_xT[c]], NCH)
        P.op("dve", lambda e: e.tensor_copy(out=sxx[:], in_=bank[4][:, :]), reads=[B_bank[4]], writes=[B_sxx])
        rms_finish(0, D)
        for c in range(NCH):
            t = nxt("tmp", NTMP)
            P.op("pool" if c % 2 else "dve", lambda e, c=c, t=t: e.tensor_tensor(out=tmpf[t][:], in0=xT[:, c, :], in1=Rt[0][:], op=ALU.mult),
                 reads=[B_xT[c], B_R[0]], writes=[B_tmp[t]])
            P.op("act", lambda e, c=c, t=t: e.activation(out=hT[:, c, :], in_=tmpf[t][:], func=AF.Identity,
                                                         bias=shf(l, c), scale=gsf(l, c)),
                 reads=[B_tmp[t], B_vec, B_ada], writes=[B_hT[c]])

    def post_update(l, ggf):
        rms_finish(1, D)
        for c in range(NCH):
            t = nxt("tmp", NTMP)
            P.op("pool" if c % 2 else "dve", lambda e, c=c, t=t: e.tensor_tensor(out=tmpf[t][:], in0=mixb[:, c, :], in1=Rt[1][:], op=ALU.mult),
                 reads=[B_mix[c], B_R[1]], writes=[B_tmp[t]])
            P.op("dve", lambda e, c=c, t=t: e.scalar_tensor_tensor(
                out=xT[:, c, :], in0=tmpf[t][:], scalar=ggf(l, c), in1=xT[:, c, :], op0=ALU.mult, op1=ALU.add),
                reads=[B_tmp[t], B_vec, B_xT[c]], writes=[B_xT[c]])

    def proj_to_mix(k, l, KC, rhs_fn, rhs_bufs, ggf=None):
        pending = []
        for m in range(NCH):
            bk = nxt("bank", 4)
            kc = 0
            for c0 in range(0, KC * 128, 2048):
                cw = min(2048, KC * 128 - c0)
                s = wload(k, l, m, c0, cw)
                for j in range(cw // 128):
                    P.op("pe", lambda e, s=s, j=j, kc=kc, bk=bk: e.matmul(
                        bank[bk][:, :], lhsT=wslot[s][:, j * 128:(j + 1) * 128], rhs=rhs_fn(kc),
                        start=(kc == 0), stop=(kc == KC - 1)),
                        reads=[B_ws[s]] + rhs_bufs(kc), writes=[B_bank[bk]])
                    kc += 1
            for f in pending:
                f()
            pending = []
            P.op("dve", lambda e, m=m, bk=bk: e.tensor_copy(out=mixb[:, m, :], in_=bank[bk][:, :]),
                 reads=[B_bank[bk]], writes=[B_mix[m]])
            q0 = nxt("sq", NSQ)
            P.op("act", lambda e, bk=bk, q0=q0: e.activation(out=sqb[q0][:], in_=bank[bk][:, :], func=AF.Square),
                 reads=[B_bank[bk]], writes=[B_sq[q0]])
            pending.append(lambda m=m, q0=q0: P.op("pe", lambda e: e.matmul(
                bank[4][:, :], lhsT=ones_b, rhs=sqb[q0][:], start=(m == 0), stop=(m == NCH - 1)),
                reads=[B_sq[q0], B_cbf], writes=[B_bank[4]]))
            if ggf is not None:
                q1 = nxt("sq", NSQ)
                P.op("act", lambda e, m=m, bk=bk, q1=q1: e.activation(out=sqb[q1][:], in_=bank[bk][:, :], func=AF.Square,
                                                                     scale=ggf(l, m)),
                     reads=[B_bank[bk], B_vec], writes=[B_sq[q1]])
                pending.append(lambda m=m, q1=q1: P.op("pe", lambda e: e.matmul(
                    bank[5][:, :], lhsT=ones_b, rhs=sqb[q1][:], start=(m == 0), stop=(m == NCH - 1)),
                    reads=[B_sq[q1], B_cbf], writes=[B_bank[5]]))
                q2 = nxt("sq", NSQ)
                P.op("dve", lambda e, m=m, bk=bk, q2=q2: e.scalar_tensor_tensor(
                    out=sqb[q2][:], in0=bank[bk][:, :], scalar=ggf(l, m), in1=xT[:, m, :], op0=ALU.mult, op1=ALU.mult),
                    reads=[B_bank[bk], B_vec, B_xT[m]], writes=[B_sq[q2]])
                pending.append(lambda m=m, q2=q2: P.op("pe", lambda e: e.matmul(
                    bank[7][:, :], lhsT=ones_b, rhs=sqb[q2][:], start=(m == 0), stop=(m == NCH - 1)),
                    reads=[B_sq[q2], B_cbf], writes=[B_bank[7]]))
        for f in pending:
            f()

    def boundary(l, ggf, l2, gsf2, shf2):
        rms_finish(1, D)
        ta = nxt("tmp", NTMP); tb = nxt("tmp", NTMP)
        dvo = lambda fn, r, w: P.op("dve", fn, reads=r, writes=w)
        dvo(lambda e: e.tensor_tensor(out=tmpf[ta][:], in0=bank[5][:, :], in1=Rt[1][:], op=ALU.mult),
            [B_bank[5], B_R[1]], [B_tmp[ta]])
        dvo(lambda e: e.tensor_tensor(out=tmpf[ta][:], in0=tmpf[ta][:], in1=Rt[1][:], op=ALU.mult),
            [B_tmp[ta], B_R[1]], [B_tmp[ta]])
        dvo(lambda e: e.tensor_tensor(out=tmpf[tb][:], in0=bank[7][:, :], in1=Rt[1][:], op=ALU.mult),
            [B_bank[7], B_R[1]], [B_tmp[tb]])
        dvo(lambda e: e.scalar_tensor_tensor(out=tmpf[ta][:], in0=tmpf[tb][:], scalar=2.0, in1=tmpf[ta][:],
                                             op0=ALU.mult, op1=ALU.add), [B_tmp[ta], B_tmp[tb]], [B_tmp[ta]])
        dvo(lambda e: e.tensor_tensor(out=sxx[:], in0=sxx[:], in1=tmpf[ta][:], op=ALU.add), [B_sxx, B_tmp[ta]], [B_sxx])
        if l2 is not None:
            P.op("act", lambda e: e.activation(out=tmpf[tb][:], in_=sxx[:], func=AF.Sqrt, bias=epsb[:, 0:1], scale=1.0 / D),
                 reads=[B_sxx, B_eps], writes=[B_tmp[tb]])
            dvo(lambda e: e.reciprocal(out=Rt[0][:], in_=tmpf[tb][:]), [B_tmp[tb]], [B_R[0]])
        for c in range(NCH):
            t = nxt("tmp", NTMP)
            P.op("pool" if c % 2 else "dve", lambda e, c=c, t=t: e.tensor_tensor(
                out=tmpf[t][:], in0=mixb[:, c, :], in1=Rt[1][:], op=ALU.mult),
                reads=[B_mix[c], B_R[1]], writes=[B_tmp[t]])
            P.op("dve", lambda e, c=c, t=t: e.scalar_tensor_tensor(
                out=xT[:, c, :], in0=tmpf[t][:], scalar=ggf(l, c), in1=xT[:, c, :], op0=ALU.mult, op1=ALU.add),
                reads=[B_tmp[t], B_vec, B_xT[c]], writes=[B_xT[c]])
            if l2 is not None:
                t2 = nxt("tmp", NTMP)
                P.op("dve" if c % 2 else "pool", lambda e, c=c, t2=t2: e.tensor_tensor(
                    out=tmpf[t2][:], in0=xT[:, c, :], in1=Rt[0][:], op=ALU.mult),
                    reads=[B_xT[c], B_R[0]], writes=[B_tmp[t2]])
                P.op("act", lambda e, c=c, t2=t2: e.activation(out=hT[:, c, :], in_=tmpf[t2][:], func=AF.Identity,
                                                               bias=shf2(l2, c), scale=gsf2(l2, c)),
                     reads=[B_tmp[t2], B_vec, B_ada], writes=[B_hT[c]])

    B_hid = [Buf("hid%d" % j) for j in range(NFF)]
    cbuf = tmpf
    B_cbuf = B_tmp

    FUSEB = do_ffn and do_attn and do_ssm

    def ffn(l, ti):
        o = l * SP_L
        if not FUSEB:
            pre_norm(l, gs2, sh_f)
        for j in range(NFF):
            cbs = []
            for half in range(2):
                m = j + half * NFF
                s = wload("wup", l, m, 0, 2048)
                bk = nxt("bank", 4)
                for kc in range(NCH):
                    P.op("pe", lambda e, s=s, kc=kc, bk=bk: e.matmul(
                        bank[bk][:, :], lhsT=wslot[s][:, kc * 128:(kc + 1) * 128], rhs=hT[:, kc, :],
                        start=(kc == 0), stop=(kc == NCH - 1)),
                        reads=[B_ws[s], B_hT[kc]], writes=[B_bank[bk]])
                cb = nxt("tmp", NTMP)
                cbs.append(cb)
                w0 = small[:, o + SP_CW + m * 3 + 0:o + SP_CW + m * 3 + 1]
                w1 = small[:, o + SP_CW + m * 3 + 1:o + SP_CW + m * 3 + 2]
                w2 = small[:, o + SP_CW + m * 3 + 2:o + SP_CW + m * 3 + 3]
                bb = small[:, o + SP_CB + m:o + SP_CB + m + 1]
                P.op("act", lambda e, cb=cb, bk=bk, w2=w2, bb=bb: e.activation(
                    out=cbuf[cb][:], in_=bank[bk][:, :], func=AF.Identity, bias=bb, scale=w2),
                    reads=[B_bank[bk], B_small], writes=[B_cbuf[cb]])
                P.op("dve", lambda e, cb=cb, bk=bk, w1=w1: e.scalar_tensor_tensor(
                    out=cbuf[cb][:, 1:T], in0=bank[bk][:, 0:T - 1], scalar=w1, in1=cbuf[cb][:, 1:T],
                    op0=ALU.mult, op1=ALU.add),
                    reads=[B_bank[bk], B_small, B_cbuf[cb]], writes=[B_cbuf[cb]])
                P.op("dve", lambda e, cb=cb, bk=bk, w0=w0: e.scalar_tensor_tensor(
                    out=cbuf[cb][:, 2:T], in0=bank[bk][:, 0:T - 2], scalar=w0, in1=cbuf[cb][:, 2:T],
                    op0=ALU.mult, op1=ALU.add),
                    reads=[B_bank[bk], B_small, B_cbuf[cb]], writes=[B_cbuf[cb]])
                if ti > 0:
                    P.op("dve", lambda e, cb=cb, m=m, w1=w1: e.scalar_tensor_tensor(
                        out=cbuf[cb][:, 0:1], in0=convc[l][:, m, 1:2], scalar=w1, in1=cbuf[cb][:, 0:1],
                        op0=ALU.mult, op1=ALU.add),
                        reads=[B_convc[l], B_small, B_cbuf[cb]], writes=[B_cbuf[cb]])
                    P.op("dve", lambda e, cb=cb, m=m, w0=w0: e.scalar_tensor_tensor(
                        out=cbuf[cb][:, 0:2], in0=convc[l][:, m, 0:2], scalar=w0, in1=cbuf[cb][:, 0:2],
                        op0=ALU.mult, op1=ALU.add),
                        reads=[B_convc[l], B_small, B_cbuf[cb]], writes=[B_cbuf[cb]])
                P.op("dve", lambda e, m=m, bk=bk: e.tensor_copy(out=convc[l][:, m, 0:2], in_=bank[bk][:, T - 2:T]),
                     reads=[B_bank[bk]], writes=[B_convc[l]])
            cv, cg = cbs
            P.op("act", lambda e, cg=cg: e.activation(out=cbuf[cg][:], in_=cbuf[cg][:], func=AF.Gelu_apprx_tanh),
                 reads=[B_cbuf[cg]], writes=[B_cbuf[cg]])
            P.op("dve", lambda e, j=j, cv=cv, cg=cg: e.tensor_tensor(out=hid[:, j, :], in0=cbuf[cg][:], in1=cbuf[cv][:],
                                                                     op=ALU.mult),
                 reads=[B_cbuf[cg], B_cbuf[cv]], writes=[B_hid[j]])
        proj_to_mix("wdn", l, NFF, lambda kc: hid[:, kc, :], lambda kc: [B_hid[kc]], gg2 if FUSEB else None)
        if not FUSEB:
            post_update(l, gg2)

    sthi = sb("sthi", [128, D], BF16)
    stlo = sb("stlo", [128, D], BF16)
    B_sthi = Buf("sthi"); B_stlo = Buf("stlo")

    def load_x(ti):
        for blk in range(T // 128):
            r0 = ti * T + blk * 128
            P.op("pool", lambda e, r0=r0: e.dma_start(out=stage[:], in_=x_in[r0:r0 + 128, :]),
                 writes=[B_stage], dma=True)
            P.op("act", lambda e: e.activation(out=sthi[:], in_=stage[:], func=AF.Copy),
                 reads=[B_stage], writes=[B_sthi])
            P.op("dve", lambda e: e.tensor_tensor(out=stlo[:], in0=stage[:], in1=sthi[:], op=ALU.subtract),
                 reads=[B_stage, B_sthi], writes=[B_stlo])
            for c4 in range(NCH // 4):
                bk = nxt("bank", 4)
                for q in range(4):
                    c = c4 * 4 + q
                    P.op("pe", lambda e, c=c, q=q, bk=bk: e.matmul(
                        bank[bk][:, q * 128:(q + 1) * 128], lhsT=sthi[:, c * 128:(c + 1) * 128], rhs=ident_b,
                        start=True, stop=False),
                        reads=[B_sthi, B_cbf], writes=[B_bank[bk]])
                    P.op("pe", lambda e, c=c, q=q, bk=bk: e.matmul(
                        bank[bk][:, q * 128:(q + 1) * 128], lhsT=stlo[:, c * 128:(c + 1) * 128], rhs=ident_b,
                        start=False, stop=True),
                        reads=[B_stlo, B_cbf], writes=[B_bank[bk]])
                for q in range(4):
                    c = c4 * 4 + q
                    if q % 2:
                        P.op("act", lambda e, c=c, q=q, bk=bk, blk=blk: e.activation(
                            out=xT[:, c, blk * 128:(blk + 1) * 128], in_=bank[bk][:, q * 128:(q + 1) * 128], func=AF.Copy),
                            reads=[B_bank[bk]], writes=[B_xT[c]])
                    else:
                        P.op("dve", lambda e, c=c, q=q, bk=bk, blk=blk: e.tensor_copy(
                            out=xT[:, c, blk * 128:(blk + 1) * 128], in_=bank[bk][:, q * 128:(q + 1) * 128]),
                            reads=[B_bank[bk]], writes=[B_xT[c]])

    xhi = sb("xhi", [128, 4, 128], BF16)
    xlo = sb("xlo", [128, 4, 128], BF16)
    B_xhi = Buf("xhi"); B_xlo = Buf("xlo")

    def store_x(ti):
        for blk in range(T // 128):
            r0 = ti * T + blk * 128
            for c4 in range(NCH // 4):
                bk = nxt("bank", 4)
                src = xT[:, c4 * 4:(c4 + 1) * 4, blk * 128:(blk + 1) * 128]
                P.op("act", lambda e, src=src: e.activation(out=xhi[:], in_=src, func=AF.Copy),
                     reads=B_xT[c4 * 4:(c4 + 1) * 4], writes=[B_xhi])
                P.op("dve", lambda e, src=src: e.tensor_tensor(out=xlo[:], in0=src, in1=xhi[:], op=ALU.subtract),
                     reads=B_xT[c4 * 4:(c4 + 1) * 4] + [B_xhi], writes=[B_xlo])
                for q in range(4):
                    P.op("pe", lambda e, q=q, bk=bk: e.matmul(
                        bank[bk][:, q * 128:(q + 1) * 128], lhsT=xhi[:, q, :], rhs=ident_b, start=True, stop=False),
                        reads=[B_xhi, B_cbf], writes=[B_bank[bk]])
                    P.op("pe", lambda e, q=q, bk=bk: e.matmul(
                        bank[bk][:, q * 128:(q + 1) * 128], lhsT=xlo[:, q, :], rhs=ident_b, start=False, stop=True),
                        reads=[B_xlo, B_cbf], writes=[B_bank[bk]])
                P.op("dve", lambda e, c4=c4, bk=bk: e.tensor_copy(out=stage[:, c4 * 512:(c4 + 1) * 512], in_=bank[bk][:, :]),
                     reads=[B_bank[bk]], writes=[B_stage])
            P.op("pool", lambda e, r0=r0: e.dma_start(out=y_out[r0:r0 + 128, :], in_=stage[:]),
                 reads=[B_stage], dma=True)

    hflat = hid[:, :, :]
    qT = hid[:, 0:8, :]
    kTv = hid[:, 8:13, :].rearrange("p a b -> p (a b)")
    Vt = hid[:, 13:17, :]
    attnT = hid[:, 17:25, :]
    PT = hid[:, 25:33, :].rearrange("p a b -> p (a b)")
    uT = hid[:, 33:41, :]
    B_q = [Buf("q%d" % i) for i in range(8)]
    B_k = Buf("kT"); B_V = [Buf("V%d" % i) for i in range(4)]
    B_att = [Buf("att%d" % i) for i in range(8)]
    B_PT = [Buf("PT%d" % i) for i in range(8)]
    B_u = [Buf("u%d" % i) for i in range(8)]
    kcar = [sb("kcar%d" % l, [128, 4, 128], BF16) for l in range(DEPTH)]
    vcar = [sb("vcar%d" % l, [128, 512], BF16) for l in range(DEPTH)]
    B_kcar = [Buf("kcar%d" % l) for l in range(DEPTH)]
    B_vcar = [Buf("vcar%d" % l) for l in range(DEPTH)]
    esk = sb("esk", [128, DEPTH, 8])
    B_esk = Buf("esk")
    for l in range(n_layers):
        P.op("act", lambda e, l=l: e.activation(out=esk[:, l, :], in_=small[:, l * SP_L + SP_SINK:l * SP_L + SP_SINK + 8],
                                                func=AF.Exp), reads=[B_small], writes=[B_esk])
    maskP = cbf[:, CB_MP:CB_MP + 128]
    maskC = cbf[:, CB_MC:CB_MC + 128]
    onesP = [cbf[:, CB_OP0:CB_OP0 + 128], cbf[:, CB_OP1:CB_OP1 + 128]]
    ctr["obank"] = 0
    ctr["ev"] = 0

    def evac_bf16(dst_ap, bk, dst_bufs):
        if nxt("ev", 2) == 0:
            P.op("act", lambda e: e.activation(out=dst_ap, in_=bank[bk][:, :], func=AF.Copy),
                 reads=[B_bank[bk]], writes=dst_bufs)
        else:
            P.op("dve", lambda e: e.tensor_copy(out=dst_ap, in_=bank[bk][:, :]), reads=[B_bank[bk]], writes=dst_bufs)

    def in_proj(l):
        dsts = [(qT[:, m, :], [B_q[m]]) for m in range(8)]
        dsts += [(kTv[:, v * 640:v * 640 + T], [B_k]) for v in range(4)]
        dsts += [(uT[:, m, :], [B_u[m]]) for m in range(8)]
        for m in range(IN_TILES):
            if (not do_ssm) and m >= 12:
                break
            s = wload("win", l, m, 0, 2048)
            bk = nxt("bank", 4)
            for kc in range(NCH):
                P.op("pe", lambda e, s=s, kc=kc, bk=bk: e.matmul(
                    bank[bk][:, :], lhsT=wslot[s][:, kc * 128:(kc + 1) * 128], rhs=hT[:, kc, :],
                    start=(kc == 0), stop=(kc == NCH - 1)),
                    reads=[B_ws[s], B_hT[kc]], writes=[B_bank[bk]])
            evac_bf16(dsts[m][0], bk, dsts[m][1])
        for pc4 in range(4):
            s = wload("wv", l, 0, pc4 * 2048, 2048)
            for j in range(4):
                kc = pc4 * 4 + j
                for blk in range(4):
                    P.op("pe", lambda e, s=s, j=j, kc=kc, blk=blk: e.matmul(
                        bank[blk][:, :], lhsT=hT[:, kc, blk * 128:(blk + 1) * 128], rhs=wslot[s][:, j * 512:(j + 1) * 512],
                        start=(kc == 0), stop=(kc == NCH - 1)),
                        reads=[B_ws[s], B_hT[kc]], writes=[B_bank[blk]])
        for blk in range(4):
            evac_bf16(Vt[:, blk, :], blk, [B_V[blk]])

    def attention(l, ti):
        for i in range(4):
            first = (ti == 0 and i == 0)
            for qt in range(8):
                kv = qt // 4
                bS = nxt("bank", 4)
                segs = []
                for hh in range(2):
                    var = kv * 2 + hh
                    for pc in range(2):
                        if pc == 0 and first:
                            continue
                        col = hh * 256 + pc * 128
                        if pc == 0:
                            klhs = kcar[l][:, var, :] if i == 0 else kTv[:, var * 640 + (i - 1) * 128:var * 640 + i * 128]
                            kb = [B_kcar[l]] if i == 0 else [B_k]
                            msk = maskP
                        else:
                            klhs = kTv[:, var * 640 + i * 128:var * 640 + (i + 1) * 128]
                            kb = [B_k]
                            msk = maskC
                        P.op("pe", lambda e, col=col, klhs=klhs, bS=bS, qt=qt, i=i: e.matmul(
                            bank[bS][:, col:col + 128], lhsT=klhs, rhs=qT[:, qt, i * 128:(i + 1) * 128],
                            start=True, stop=False), reads=kb + [B_q[qt]], writes=[B_bank[bS]])
                        P.op("pe", lambda e, col=col, msk=msk, bS=bS: e.matmul(
                            bank[bS][:, col:col + 128], lhsT=ident_b, rhs=msk, start=False, stop=True),
                            reads=[B_cbf], writes=[B_bank[bS]])
                        segs.append((hh, pc, col))
                po = qt * 512
                if first:
                    for hh in range(2):
                        col = hh * 256 + 128
                        P.op("act", lambda e, col=col, bS=bS, po=po: e.activation(
                            out=PT[:, po + col:po + col + 128], in_=bank[bS][:, col:col + 128], func=AF.Exp, scale=0.125),
                            reads=[B_bank[bS]], writes=[B_PT[qt]])
                else:
                    P.op("act", lambda e, bS=bS, po=po: e.activation(
                        out=PT[:, po:po + 512], in_=bank[bS][:, :], func=AF.Exp, scale=0.125),
                        reads=[B_bank[bS]], writes=[B_PT[qt]])
                bO = 5 + nxt("obank", 2)
                for part in range(2):
                    for n_, (hh, pc, col) in enumerate(segs):
                        var = kv * 2 + hh
                        if part == 0:
                            if pc == 0:
                                lh = vcar[l][:, var * 128:(var + 1) * 128] if i == 0 else Vt[:, i - 1, var * 128:(var + 1) * 128]
                                vb = [B_vcar[l]] if i == 0 else [B_V[i - 1]]
                            else:
                                lh = Vt[:, i, var * 128:(var + 1) * 128]
                                vb = [B_V[i]]
                        else:
                            lh = onesP[hh]
                            vb = [B_cbf]
                        P.op("pe", lambda e, part=part, lh=lh, col=col, po=po, bO=bO, n_=n_, ns=len(segs): e.matmul(
                            bank[bO][:, part * 128:(part + 1) * 128], lhsT=lh, rhs=PT[:, po + col:po + col + 128],
                            start=(n_ == 0), stop=(n_ == ns - 1)),
                            reads=vb + [B_PT[qt]], writes=[B_bank[bO]])
                t = nxt("tmp", NTMP)
                P.op("dve", lambda e, t=t, bO=bO, qt=qt: e.tensor_scalar(
                    out=tmpf[t][:, 0:128], in0=bank[bO][:, 128:256], scalar1=esk[:, l, qt:qt + 1], scalar2=None, op0=ALU.add),
                    reads=[B_bank[bO], B_esk], writes=[B_tmp[t]])
                P.op("dve", lambda e, t=t: e.reciprocal(out=tmpf[t][:, 128:256], in_=tmpf[t][:, 0:128]),
                     reads=[B_tmp[t]], writes=[B_tmp[t]])
                P.op("dve", lambda e, t=t, bO=bO, qt=qt, i=i: e.tensor_tensor(
                    out=attnT[:, qt, i * 128:(i + 1) * 128], in0=bank[bO][:, 0:128], in1=tmpf[t][:, 128:256], op=ALU.mult),
                    reads=[B_bank[bO], B_tmp[t]], writes=[B_att[qt]])
        for v in range(4):
            P.op("pool", lambda e, v=v: e.tensor_copy(out=kcar[l][:, v, :], in_=kTv[:, v * 640 + 384:v * 640 + 512]),
                 reads=[B_k], writes=[B_kcar[l]])
        P.op("pool", lambda e: e.tensor_copy(out=vcar[l][:], in_=Vt[:, 3, :]), reads=[B_V[3]], writes=[B_vcar[l]])

    Uall = hid[:, 33:41, :]
    zT = hid[:, 8:16, :]
    XB = mixb[:, 0:8, :].rearrange("p a b -> p (a b)")
    ssmT = hid[:, 0:8, :]
    Yct = hid[:, 41, :]
    B_z = [Buf("z%d" % i) for i in range(8)]
    B_XB = Buf("XB"); B_Y = Buf("Yct")
    XSw = sb("XSw", [128, 65, 2, 32])
    B_XSw = Buf("XSw")
    B_XSwh = [Buf("XSw0"), Buf("XSw1")]
    B_rtmph = [Buf("rtmp0"), Buf("rtmp1")]
    B_rtmp2 = Buf("rtmp2")
    rtmp = sb("rtmp", [128, 2, 2, 32])
    B_rtmp = Buf("rtmp")
    selp = [cbf[:, CB_SEL + g * 352:CB_SEL + (g + 1) * 352] for g in range(8)]

    def ssm_A(l, ti):
        for ct in range(8):
            bU = nxt("bank", 4)
            for g_lo in range(8):
                for s_ in range(8):
                    x0 = 112 + 16 * (g_lo - s_)
                    P.op("pe", lambda e, ct=ct, g_lo=g_lo, s_=s_, x0=x0, bU=bU: e.matmul(
                        bank[bU][:, g_lo * 64:(g_lo + 1) * 64], lhsT=selp[g_lo][:, x0:x0 + 128],
                        rhs=uT[:, ct, s_::8], start=(s_ == 0), stop=(s_ == 7)),
                        reads=[B_cbf, B_u[ct]], writes=[B_bank[bU]])
            evac_bf16(Uall[:, ct, :], bU, [B_u[ct]])
            sB = wload("ssm", l, ct, 0, 2048)
            bX = nxt("bank", 4)
            for pl in range(4):
                for ri in range(2):
                    for g2 in range(2):
                        g_lo = pl * 2 + g2
                        P.op("pe", lambda e, ct=ct, pl=pl, ri=ri, g2=g2, g_lo=g_lo, sB=sB, bX=bX: e.matmul(
                            bank[bX][:, (pl * 2 + ri) * 64:(pl * 2 + ri + 1) * 64],
                            lhsT=wslot[sB][:, (g_lo * 2 + ri) * 128:(g_lo * 2 + ri + 1) * 128],
                            rhs=Uall[:, ct, g_lo * 64:(g_lo + 1) * 64], start=(g2 == 0), stop=(g2 == 1)),
                            reads=[B_ws[sB], B_u[ct]], writes=[B_bank[bX]])
            for ri in range(2):
                src = bank[bX][:, :].rearrange("p (a r c) -> p r c a", r=2, c=64)[:, ri, :, :]
                P.op("dve", lambda e, ct=ct, ri=ri, src=src: e.tensor_copy(
                    out=XSw[:, 1:65, ri, ct * 4:(ct + 1) * 4], in_=src),
                    reads=[B_bank[bX]], writes=[B_XSw])
        P.op("pool", lambda e: e.tensor_copy(out=XSw[:, 0, :, :], in_=XSc[l][:]), reads=[B_XSc[l]], writes=[B_XSw])
        a8r = A8[l][:, 0, :].unsqueeze(1).to_broadcast([128, 2, 32])
        a8s = A8[l][:, 1:3, :][:, ::-1, :]
        for c in range(1, 65):
            P.op("pool", lambda e, c=c: e.tensor_tensor(out=rtmp[:, 0, :, :], in0=XSw[:, c - 1, :, :], in1=a8r, op=ALU.mult),
                 reads=[B_XSw, B_A8[l]], writes=[B_rtmp])
            P.op("pool", lambda e, c=c: e.tensor_tensor(out=rtmp[:, 1, :, :], in0=XSw[:, c - 1, ::-1, :], in1=a8s, op=ALU.mult),
                 reads=[B_XSw, B_A8[l]], writes=[B_rtmp2])
            P.op("pool", lambda e, c=c: e.tensor_tensor(out=XSw[:, c, :, :], in0=XSw[:, c, :, :], in1=rtmp[:, 0, :, :], op=ALU.add),
                 reads=[B_XSw, B_rtmp], writes=[B_XSw])
            P.op("pool", lambda e, c=c: e.tensor_tensor(out=XSw[:, c, :, :], in0=XSw[:, c, :, :], in1=rtmp[:, 1, :, :], op=ALU.add),
                 reads=[B_XSw, B_rtmp2], writes=[B_XSw])
        P.op("pool", lambda e: e.tensor_copy(out=XSc[l][:], in_=XSw[:, 64, :, :]), reads=[B_XSw], writes=[B_XSc[l]])

    xb4 = XB.rearrange("p (q r c) -> p q r c", r=2, c=64)

    def ssm_B(l, ti):
        for ri in range(2):
            P.op("dve", lambda e, ri=ri: e.tensor_copy(
                out=xb4[:, :, ri, :], in_=XSw[:, 0:64, ri, :].rearrange("p c q -> p q c")),
                reads=[B_XSw], writes=[B_XB] + B_mix[0:8])
        for ct in range(8):
            sC = wload("ssm", l, ct, 2048, 2048)
            sK = wload("ssm", l, ct, 4096, 1024)
            bY = nxt("bank", 4)
            for g_lo in range(8):
                pair = ct * 4 + g_lo // 2
                oc = bank[bY][:, g_lo * 64:(g_lo + 1) * 64]
                P.op("pe", lambda e, ct=ct, g_lo=g_lo, sK=sK, oc=oc: e.matmul(
                    oc, lhsT=wslot[sK][:, g_lo * 128:(g_lo + 1) * 128], rhs=Uall[:, ct, g_lo * 64:(g_lo + 1) * 64],
                    start=True, stop=False), reads=[B_ws[sK], B_u[ct]], writes=[B_bank[bY]])
                for ri in range(2):
                    P.op("pe", lambda e, g_lo=g_lo, ri=ri, sC=sC, oc=oc, pair=pair: e.matmul(
                        oc, lhsT=wslot[sC][:, (g_lo * 2 + ri) * 128:(g_lo * 2 + ri + 1) * 128],
                        rhs=xb4[:, pair, ri, :], start=False, stop=(ri == 1)),
                        reads=[B_ws[sC], B_XB], writes=[B_bank[bY]])
            evac_bf16(Yct, bY, [B_Y])
            bZ = nxt("bank", 4)
            for t in range(8):
                for g_lo in range(8):
                    x0 = 112 + 16 * (t - g_lo)
                    P.op("pe", lambda e, t=t, g_lo=g_lo, x0=x0, bZ=bZ: e.matmul(
                        bank[bZ][:, t * 64:(t + 1) * 64], lhsT=selp[t][:, x0:x0 + 128],
                        rhs=Yct[:, g_lo * 64:(g_lo + 1) * 64], start=(g_lo == 0), stop=(g_lo == 7)),
                        reads=[B_cbf, B_Y], writes=[B_bank[bZ]])
            P.op("act", lambda e, ct=ct, bZ=bZ: e.activation(
                out=zT[:, ct, :].rearrange("p (c t) -> p t c", t=8),
                in_=bank[bZ][:, :].rearrange("p (t c) -> p t c", c=64), func=AF.Gelu_apprx_tanh),
                reads=[B_bank[bZ]], writes=[B_z[ct], B_k] + B_V)
        for m in range(8):
            s = wload("wglu", l, m, 0, 1024)
            bk = nxt("bank", 4)
            for kc in range(8):
                P.op("pe", lambda e, s=s, kc=kc, bk=bk: e.matmul(
                    bank[bk][:, :], lhsT=wslot[s][:, kc * 128:(kc + 1) * 128], rhs=zT[:, kc, :],
                    start=(kc == 0), stop=(kc == 7)), reads=[B_ws[s], B_z[kc]], writes=[B_bank[bk]])
            t = nxt("tmp", NTMP)
            P.op("act", lambda e, t=t, bk=bk: e.activation(out=tmpf[t][:], in_=bank[bk][:, :], func=AF.Sigmoid),
                 reads=[B_bank[bk]], writes=[B_tmp[t]])
            P.op("dve", lambda e, t=t, m=m: e.tensor_tensor(out=ssmT[:, m, :], in0=zT[:, m, :], in1=tmpf[t][:], op=ALU.mult),
                 reads=[B_z[m], B_tmp[t]], writes=[B_q[m]])

    def group_norm_to_hT(l, src, src_bufs, c0, gcol, ri):
        st = rms_begin()
        for c in range(8):
            rms_add(st, src[:, c, :], [src_bufs[c]], 8)
        rms_finish(ri, 1024)
        o = l * SP_L
        for c in range(8):
            t = nxt("tmp", NTMP)
            P.op("pool", lambda e, c=c, t=t: e.tensor_tensor(out=tmpf[t][:], in0=src[:, c, :], in1=Rt[ri][:], op=ALU.mult),
                 reads=[src_bufs[c], B_R[ri]], writes=[B_tmp[t]])
            P.op("act", lambda e, c=c, t=t: e.activation(out=hT[:, c0 + c, :], in_=tmpf[t][:], func=AF.Copy,
                                                         scale=small[:, o + gcol + c:o + gcol + c + 1]),
                 reads=[B_tmp[t], B_small], writes=[B_hT[c0 + c]])

    def mixer(l, ti):
        if not FUSEB or l == 0:
            pre_norm(l, gs1, sh_m)
        in_proj(l)
        if do_ssm:
            ssm_A(l, ti)
        if do_attn:
            attention(l, ti)
        if do_ssm:
            ssm_B(l, ti)
        if do_attn:
            group_norm_to_hT(l, attnT, B_att, 0, SP_GATT, 2)
        else:
            for c in range(8):
                P.op("pool", lambda e, c=c: e.memset(hT[:, c, :], 0.0), writes=[B_hT[c]])
        if do_ssm:
            group_norm_to_hT(l, ssmT, B_q, 8, SP_GSSM, 2)
        else:
            for c in range(8, 16):
                P.op("pool", lambda e, c=c: e.memset(hT[:, c, :], 0.0), writes=[B_hT[c]])
        proj_to_mix("wout", l, NCH, lambda kc: hT[:, kc, :], lambda kc: [B_hT[kc]], gg1 if FUSEB else None)
        if FUSEB:
            boundary(l, gg1, l, gs2, sh_f)
        else:
            post_update(l, gg1)

    BIS = int(os.environ.get("BIS", "9"))
    if do_ssm:
        allb = GEN_BUFS + B_hid + B_q + [B_k] + B_V + B_att + B_PT + B_u + B_z + [B_XB, B_Y] + B_hT
        P.op("pool", lambda e: e.memset(rtmp[:, 0, 0, 0:1], 0.0), writes=allb + [B_rtmp, B_rtmp2])
    for ti in range(n_tiles):
        if BIS >= 1:
            load_x(ti)
        for l in range(n_layers):
            if do_attn or do_ssm:
                mixer(l, ti)
            if do_ffn:
                ffn(l, ti)
                if FUSEB:
                    if l + 1 < n_layers:
                        boundary(l, gg2, l + 1, gs1, sh_m)
                    else:
                        boundary(l, gg2, None, None, None)
        if BIS >= 2:
            store_x(ti)

    P.emit(nc, es)
    es.close()
    return nc


_CACHE = {}


def kernel(**inputs):
    inp = {k: np.asarray(v) for k, v in inputs.items()}
    sh = prep_shared(inp)
    in_maps = []
    for b in range(NB):
        m = dict(sh)
        sm = sh["small"].copy()
        sm[:, SP_C:SP_C + 16] = _fm(inp["c"][b], 16)
        m["small"] = sm
        m["x"] = np.ascontiguousarray(inp["x"][b])
        in_maps.append(m)
    if "nc" not in _CACHE:
        _CACHE["nc"] = build_nc()
    res = run_bass_kernel_spmd(_CACHE["nc"], in_maps, core_ids=list(range(NB)))
    return np.stack([r["y"] for r in res.results], axis=0).astype(np.float32)
```
